# Optimizing a Trainium2 kernel written in Bass

```python
import math
import jax
import jax.numpy as jnp
from jax import lax
import numpy as np

D_MODEL = 1024
BATCH = 8
SEQ = 4096
DEPTH = 4

N_MIXERS = 3
PLE_DIM = 256
NORM_EPS = 1e-6
D_FF = ((8 * D_MODEL + 3 * 256 - 1) // (3 * 256)) * 256

SSD_D_INNER = 2 * D_MODEL
SSD_HEADDIM = 64
SSD_HEADS = SSD_D_INNER // SSD_HEADDIM
SSD_GROUPS = 8
SSD_STATE = 128
SSD_CONV = 5
SSD_CHUNK = 128
SSD_CONV_DIM = SSD_D_INNER + 2 * SSD_GROUPS * SSD_STATE
SSD_IN_DIM = SSD_D_INNER + SSD_CONV_DIM + 2 * SSD_HEADS

HY_WIDTH = D_MODEL
HY_SHORT = 3
HY_EMB_DIM = 33
HY_BANDS = (HY_EMB_DIM - 1) // 2
HY_FILTER_HIDDEN = 64
HY_DECAY_TARGET = 1e-2
HY_FAST_DECAY_PCT = 0.3
HY_SLOW_DECAY_PCT = 1.5

GLA_HEADS = 4
GLA_KEY_DIM = D_MODEL // 2
GLA_VALUE_DIM = D_MODEL
GLA_HEAD_K = GLA_KEY_DIM // GLA_HEADS
GLA_HEAD_V = GLA_VALUE_DIM // GLA_HEADS
GLA_GATE_RANK = 16
GLA_GATE_NORMALIZER = 16.0
GLA_CHUNK = 64
GLA_IN_DIM = 2 * GLA_KEY_DIM + 2 * GLA_VALUE_DIM + 2 * GLA_GATE_RANK

kernel_name = 'bidir_hybrid_ssd_hyena_gla_block'


def rms_norm(x, w):
    xf = x.astype(jnp.float32)
    y = xf * lax.rsqrt(jnp.mean(xf * xf, axis=-1, keepdims=True) + NORM_EPS)
    return (y * w.astype(jnp.float32)).astype(x.dtype)


def centred_depthwise_conv(x, w, b):
    k = w.shape[0]
    y = lax.conv_general_dilated(x, w[:, None, :].astype(x.dtype), window_strides=(1,),
                                 padding=[(k // 2, k // 2)],
                                 dimension_numbers=('NWC', 'WIO', 'NWC'),
                                 feature_group_count=x.shape[-1])
    return y + b


def flip_seq(t):
    return jnp.flip(t, axis=1)


def to_chunks(t, size):
    b, l = t.shape[0], t.shape[1]
    return jnp.moveaxis(t.reshape((b, l // size, size) + t.shape[2:]), 1, 0)


def from_chunks(t):
    t = jnp.moveaxis(t, 0, 1)
    return t.reshape((t.shape[0], t.shape[1] * t.shape[2]) + t.shape[3:])


def ssd_chunk_scan(x, dt, a, bm, cm):
    bsz, l, h, p = x.shape
    g, n = bm.shape[2], bm.shape[3]
    k = h // g
    a_gk = a.reshape(g, k)
    mask = jnp.tril(jnp.ones((SSD_CHUNK, SSD_CHUNK), dtype=bool))[None, :, :, None, None]

    def step(state, inp):
        xc, dtc, bc, cc = inp
        acum = jnp.cumsum(dtc * a_gk, axis=1)
        decay = jnp.exp(jnp.where(mask, acum[:, :, None] - acum[:, None], -jnp.inf))
        cb = jnp.einsum('blgn,bsgn->blsg', cc, bc)
        scores = cb[..., None] * decay * dtc[:, None]
        y = jnp.einsum('blsgk,bsgkp->blgkp', scores, xc)
        y = y + jnp.einsum('blgn,bgkpn->blgkp', cc, state) * jnp.exp(acum)[..., None]
        last = acum[:, -1]
        w_s = jnp.exp(last[:, None] - acum) * dtc
        state = state * jnp.exp(last)[..., None, None] + jnp.einsum('bsgn,bsgk,bsgkp->bgkpn', bc, w_s, xc)
        return state, y

    xs = to_chunks(x.reshape(bsz, l, g, k, p), SSD_CHUNK)
    dts = to_chunks(dt.reshape(bsz, l, g, k), SSD_CHUNK)
    state0 = jnp.zeros((bsz, g, k, p, n), jnp.float32)
    _, ys = lax.scan(step, state0, (xs, dts, to_chunks(bm, SSD_CHUNK), to_chunks(cm, SSD_CHUNK)))
    return from_chunks(ys).reshape(bsz, l, h, p)


def mamba2_mixer(h, in_w, conv_w, conv_b, dt_bias, a_log, d_skip, norm_w, out_w):
    bsz, l, _ = h.shape
    zxbcdt = h @ in_w
    z, xbc, dt = jnp.split(zxbcdt, [SSD_D_INNER, SSD_D_INNER + SSD_CONV_DIM], axis=-1)
    xbc = jax.nn.silu(centred_depthwise_conv(xbc, conv_w, conv_b)).astype(jnp.float32)
    xs, bm, cm = jnp.split(xbc, [SSD_D_INNER, SSD_D_INNER + SSD_GROUPS * SSD_STATE], axis=-1)
    xs = xs.reshape(bsz, l, SSD_HEADS, SSD_HEADDIM)
    bm = bm.reshape(bsz, l, SSD_GROUPS, SSD_STATE)
    cm = cm.reshape(bsz, l, SSD_GROUPS, SSD_STATE)
    dt = jax.nn.softplus(dt.astype(jnp.float32).reshape(bsz, l, 2, SSD_HEADS) + dt_bias.astype(jnp.float32))
    a = -jnp.exp(a_log.astype(jnp.float32))
    y_f = ssd_chunk_scan(xs, dt[:, :, 0], a[0], bm, cm)
    y_b = flip_seq(ssd_chunk_scan(flip_seq(xs), flip_seq(dt[:, :, 1]), a[1], flip_seq(bm), flip_seq(cm)))
    y = y_f + y_b + xs * d_skip.astype(jnp.float32)[:, None]
    y = y.reshape(bsz, l, SSD_D_INNER) * jax.nn.silu(z.astype(jnp.float32))
    yg = y.reshape(bsz, l, SSD_GROUPS, SSD_D_INNER // SSD_GROUPS)
    yg = yg * lax.rsqrt(jnp.mean(yg * yg, axis=-1, keepdims=True) + NORM_EPS)
    y = yg.reshape(bsz, l, SSD_D_INNER) * norm_w.astype(jnp.float32)
    return y.astype(h.dtype) @ out_w


def hyena_filters(l, f_w1, f_b1, f_w2, f_b2, f_w3, sin_freq):
    f32 = jnp.float32
    t = jnp.linspace(0.0, 1.0, l, dtype=f32)[:, None]
    w = 2.0 * math.pi * jnp.arange(l, dtype=f32)[:, None] / l
    bands = jnp.linspace(1e-4, HY_BANDS - 1, HY_BANDS, dtype=f32)[None]
    z = jnp.concatenate([t, jnp.cos(bands * w), -jnp.sin(bands * w)], axis=-1)
    freq = sin_freq.astype(f32)
    hid = jnp.sin(freq * (z @ f_w1.astype(f32) + f_b1.astype(f32)))
    hid = jnp.sin(freq * (hid @ f_w2.astype(f32) + f_b2.astype(f32)))
    filt = (hid @ f_w3.astype(f32)).reshape(l, 2, HY_WIDTH)
    min_decay = math.log(HY_DECAY_TARGET) / HY_SLOW_DECAY_PCT
    max_decay = math.log(HY_DECAY_TARGET) / HY_FAST_DECAY_PCT
    deltas = jnp.abs(jnp.linspace(min_decay, max_decay, HY_WIDTH, dtype=f32))
    filt = filt * jnp.exp(-t * deltas)[:, None, :]
    return filt[:, 0], filt[:, 1]


def two_sided_fftconv(u, h_fwd, h_bwd):
    l, c = h_fwd.shape
    filt2 = jnp.concatenate([h_fwd, jnp.zeros((1, c), h_fwd.dtype), jnp.flip(h_bwd[1:], axis=0)], axis=0)
    filt_f = jnp.fft.rfft(filt2, n=2 * l, axis=0)
    u_f = jnp.fft.rfft(u, n=2 * l, axis=1)
    return jnp.fft.irfft(u_f * filt_f[None], n=2 * l, axis=1)[:, :l]


def hyena_mixer(h, in_w, in_b, conv_w, conv_b, f_w1, f_b1, f_w2, f_b2, f_w3, sin_freq, skip, out_w, out_b):
    l = h.shape[1]
    u = centred_depthwise_conv(h @ in_w + in_b, conv_w, conv_b).astype(jnp.float32)
    x0, x1, v = jnp.split(u, 3, axis=-1)
    h_fwd, h_bwd = hyena_filters(l, f_w1, f_b1, f_w2, f_b2, f_w3, sin_freq)
    v = v * x1
    v = two_sided_fftconv(v, h_fwd, h_bwd) + v * skip.astype(jnp.float32)
    y = v * x0
    return y.astype(h.dtype) @ out_w + out_b


def gla_chunk_scan(q, k, v, g):
    bsz, _, h, dk = q.shape
    dv = v.shape[-1]
    mask = jnp.tril(jnp.ones((GLA_CHUNK, GLA_CHUNK), dtype=bool))[None, :, :, None, None]

    def step(state, inp):
        qc, kc, vc, gc = inp
        bc = jnp.cumsum(gc, axis=1)
        decay = jnp.exp(jnp.where(mask, bc[:, :, None] - bc[:, None], -jnp.inf))
        att = jnp.einsum('blhd,bshd,blshd->blsh', qc, kc, decay)
        o = jnp.einsum('blsh,bshv->blhv', att, vc) + jnp.einsum('blhd,bhdv->blhv', qc * jnp.exp(bc), state)
        last = bc[:, -1]
        state = state * jnp.exp(last)[..., None] + jnp.einsum('bshd,bshv->bhdv', kc * jnp.exp(last[:, None] - bc), vc)
        return state, o

    state0 = jnp.zeros((bsz, h, dk, dv), jnp.float32)
    _, o = lax.scan(step, state0, (to_chunks(q, GLA_CHUNK), to_chunks(k, GLA_CHUNK),
                                   to_chunks(v, GLA_CHUNK), to_chunks(g, GLA_CHUNK)))
    return from_chunks(o)


def gla_mixer(h, in_w, gk_w, gk_b, norm_w, out_w):
    bsz, l, _ = h.shape
    f32 = jnp.float32
    proj = h @ in_w
    q, k, v, g, gl_f, gl_b = jnp.split(proj, [GLA_KEY_DIM, 2 * GLA_KEY_DIM,
                                              2 * GLA_KEY_DIM + GLA_VALUE_DIM,
                                              2 * GLA_KEY_DIM + 2 * GLA_VALUE_DIM,
                                              2 * GLA_KEY_DIM + 2 * GLA_VALUE_DIM + GLA_GATE_RANK], axis=-1)
    q = q.astype(f32).reshape(bsz, l, GLA_HEADS, GLA_HEAD_K) * (GLA_HEAD_K ** -0.5)
    k = k.astype(f32).reshape(bsz, l, GLA_HEADS, GLA_HEAD_K)
    v = v.astype(f32).reshape(bsz, l, GLA_HEADS, GLA_HEAD_V)

    def log_gate(gl, w, b):
        return (jax.nn.log_sigmoid((gl @ w + b).astype(f32)) / GLA_GATE_NORMALIZER).reshape(bsz, l, GLA_HEADS, GLA_HEAD_K)

    g_f = log_gate(gl_f, gk_w[0], gk_b[0])
    g_b = log_gate(gl_b, gk_w[1], gk_b[1])
    o = gla_chunk_scan(q, k, v, g_f) + flip_seq(gla_chunk_scan(flip_seq(q), flip_seq(k), flip_seq(v), flip_seq(g_b)))
    o = rms_norm(o, norm_w).reshape(bsz, l, GLA_VALUE_DIM) * jax.nn.silu(g.astype(f32))
    return o.astype(h.dtype) @ out_w


def swiglu(h, w1, w3, w2):
    return (jax.nn.silu(h @ w1) * (h @ w3)) @ w2


def setup_inputs(seed: int = 0) -> dict:
    key = jax.random.key(seed)
    ks = iter(jax.random.split(key, 64))
    f32 = jnp.float32
    D = D_MODEL
    n_a = len(range(0, DEPTH, N_MIXERS))
    n_b = len(range(1, DEPTH, N_MIXERS))
    n_c = len(range(2, DEPTH, N_MIXERS))

    def nrm(shape, scale):
        return jax.random.normal(next(ks), shape, f32) * scale

    def gain(shape):
        return 1.0 + 0.02 * jax.random.normal(next(ks), shape, f32)

    def dt_bias_init(shape):
        dt = jnp.exp(jax.random.uniform(next(ks), shape, f32, math.log(1e-3), math.log(1e-1)))
        return dt + jnp.log(-jnp.expm1(-dt))

    return {
        'x': nrm((BATCH, SEQ, D), 1.0),
        'p': nrm((DEPTH, BATCH, SEQ, PLE_DIM), 1.0),
        'norm_mix': gain((DEPTH, D)),
        'norm_ffn': gain((DEPTH, D)),
        'norm_ple': gain((DEPTH, D)),
        'ple_gate': nrm((DEPTH, D, D), D ** -0.5),
        'ple_proj': nrm((DEPTH, PLE_DIM, D), PLE_DIM ** -0.5),
        'ffn_w1': nrm((DEPTH, D, D_FF), D ** -0.5),
        'ffn_w3': nrm((DEPTH, D, D_FF), D ** -0.5),
        'ffn_w2': nrm((DEPTH, D_FF, D), D_FF ** -0.5),
        'final_norm': gain((D,)),
        'ssd_in_w': nrm((n_a, D, SSD_IN_DIM), D ** -0.5),
        'ssd_conv_w': nrm((n_a, SSD_CONV, SSD_CONV_DIM), SSD_CONV ** -0.5),
        'ssd_conv_b': nrm((n_a, SSD_CONV_DIM), 0.02),
        'ssd_dt_bias': dt_bias_init((n_a, 2, SSD_HEADS)),
        'ssd_a_log': jnp.log(jax.random.uniform(next(ks), (n_a, 2, SSD_HEADS), f32, 1.0, 16.0)),
        'ssd_d': 1.0 + nrm((n_a, SSD_HEADS), 0.1),
        'ssd_norm': gain((n_a, SSD_D_INNER)),
        'ssd_out_w': nrm((n_a, SSD_D_INNER, D), SSD_D_INNER ** -0.5),
        'hy_in_w': nrm((n_b, D, 3 * HY_WIDTH), D ** -0.5),
        'hy_in_b': nrm((n_b, 3 * HY_WIDTH), 0.02),
        'hy_conv_w': nrm((n_b, HY_SHORT, 3 * HY_WIDTH), HY_SHORT ** -0.5),
        'hy_conv_b': nrm((n_b, 3 * HY_WIDTH), 0.02),
        'hy_f_w1': nrm((n_b, HY_EMB_DIM, HY_FILTER_HIDDEN), HY_EMB_DIM ** -0.5),
        'hy_f_b1': nrm((n_b, HY_FILTER_HIDDEN), 0.02),
        'hy_f_w2': nrm((n_b, HY_FILTER_HIDDEN, HY_FILTER_HIDDEN), HY_FILTER_HIDDEN ** -0.5),
        'hy_f_b2': nrm((n_b, HY_FILTER_HIDDEN), 0.02),
        'hy_f_w3': nrm((n_b, HY_FILTER_HIDDEN, 2 * HY_WIDTH), 0.005),
        'hy_sin_freq': 1.0 + nrm((n_b, HY_FILTER_HIDDEN), 0.1),
        'hy_skip': nrm((n_b, HY_WIDTH), 0.5),
        'hy_out_w': nrm((n_b, HY_WIDTH, D), HY_WIDTH ** -0.5),
        'hy_out_b': nrm((n_b, D), 0.02),
        'gla_in_w': nrm((n_c, D, GLA_IN_DIM), D ** -0.5),
        'gla_gk_w': nrm((n_c, 2, GLA_GATE_RANK, GLA_KEY_DIM), GLA_GATE_RANK ** -0.5),
        'gla_gk_b': nrm((n_c, 2, GLA_KEY_DIM), 0.1),
        'gla_norm': gain((n_c, GLA_HEAD_V)),
        'gla_out_w': nrm((n_c, GLA_VALUE_DIM, D), GLA_VALUE_DIM ** -0.5),
    }


def reference(x, p, norm_mix, norm_ffn, norm_ple, ple_gate, ple_proj, ffn_w1, ffn_w3, ffn_w2, final_norm,
              ssd_in_w, ssd_conv_w, ssd_conv_b, ssd_dt_bias, ssd_a_log, ssd_d, ssd_norm, ssd_out_w,
              hy_in_w, hy_in_b, hy_conv_w, hy_conv_b, hy_f_w1, hy_f_b1, hy_f_w2, hy_f_b2, hy_f_w3,
              hy_sin_freq, hy_skip, hy_out_w, hy_out_b,
              gla_in_w, gla_gk_w, gla_gk_b, gla_norm, gla_out_w):
    h = x
    for i in range(DEPTH):
        kind = i % N_MIXERS
        j = i // N_MIXERS
        hn = rms_norm(h, norm_mix[i])
        if kind == 0:
            mix = mamba2_mixer(hn, ssd_in_w[j], ssd_conv_w[j], ssd_conv_b[j], ssd_dt_bias[j],
                               ssd_a_log[j], ssd_d[j], ssd_norm[j], ssd_out_w[j])
        elif kind == 1:
            mix = hyena_mixer(hn, hy_in_w[j], hy_in_b[j], hy_conv_w[j], hy_conv_b[j], hy_f_w1[j], hy_f_b1[j],
                              hy_f_w2[j], hy_f_b2[j], hy_f_w3[j], hy_sin_freq[j], hy_skip[j],
                              hy_out_w[j], hy_out_b[j])
        else:
            mix = gla_mixer(hn, gla_in_w[j], gla_gk_w[j], gla_gk_b[j], gla_norm[j], gla_out_w[j])
        h = h + mix
        h = h + swiglu(rms_norm(h, norm_ffn[i]), ffn_w1[i], ffn_w3[i], ffn_w2[i])
        gate = jax.nn.sigmoid(rms_norm(h, norm_ple[i]) @ ple_gate[i])
        h = h + gate * (p[i] @ ple_proj[i])
    return rms_norm(h, final_norm)
```

```python
import contextlib
import math
import numpy as np
import ml_dtypes
import concourse.bass as bass
import concourse.mybir as mybir
from concourse.bass_utils import run_bass_kernel_spmd

F32 = mybir.dt.float32
BF16 = mybir.dt.bfloat16
AF = mybir.ActivationFunctionType
ALU = mybir.AluOpType
AX = mybir.AxisListType

D = 1024
L = 4096
DEPTH = 4
DFF = 2816
NFC = DFF // 128
PLE = 256
EPS = 1e-6
TT = 1024
NTT = L // TT


class Buf:
    __slots__ = ("w", "r", "name")

    def __init__(self, name=""):
        self.w = []
        self.r = []
        self.name = name


class Prog:
    ENG = ("pe", "act", "dve", "pool", "sp")
    ROT = 16000
    NDS = 10

    def __init__(self, nc, es):
        self.nc = nc
        self.es = es
        self.q = {e: [] for e in self.ENG}
        self.cnt = {e: 0 for e in self.ENG}
        self.nsem = 0
        self.cur = {e: self._newsem() for e in self.ENG}
        self.seen = {e: {} for e in self.ENG}
        self.dsem = {e: [] for e in self.ENG}
        self.dnext = {e: 0 for e in self.ENG}
        self.ninst = 0

    def _newsem(self):
        self.nsem += 1
        return self.es.enter_context(self.nc.semaphore("s%d" % self.nsem))

    def _wait(self, e, sem, val):
        k = id(sem)
        if self.seen[e].get(k, 0) < val:
            self.seen[e][k] = val
            self.q[e].append(("w", sem, val))

    def _waits(self, e, reads, writes, is_dma=False):
        need = {}

        def add(t, kind):
            sem, val, eng = t
            if eng == e and (e == "pe" or kind != "raw"):
                return
            if is_dma and kind == "waw" and eng == "dma":
                return
            k = id(sem)
            if k not in need or need[k][1] < val:
                need[k] = (sem, val)

        for b in reads:
            for t in b.w:
                add(t, "raw")
        for b in writes:
            for t in b.w:
                add(t, "waw")
            for t in b.r:
                add(t, "war")
        for sem, val in need.values():
            self._wait(e, sem, val)

    def _mark(self, t, reads, writes):
        for b in reads:
            if t[2] != "dma":
                b.r = [x for x in b.r if x[2] != t[2]]
            b.r.append(t)
        for b in writes:
            if t[2] == "dma" and not b.r and b.w and all(x[2] == "dma" for x in b.w):
                b.w = b.w + [t]
            else:
                b.w = [t]
            b.r = []

    def op(self, e, fn, reads=(), writes=()):
        fns = fn if isinstance(fn, (list, tuple)) else [fn]
        self._waits(e, reads, writes)
        self.cnt[e] += 1
        if self.cnt[e] > self.ROT:
            self.cur[e] = self._newsem()
            self.cnt[e] = 1
        t = (self.cur[e], self.cnt[e], e)
        for f in fns[:-1]:
            self.q[e].append(("i", f, None))
        self.q[e].append(("i", fns[-1], t))
        self.ninst += len(fns)
        self._mark(t, reads, writes)
        return t

    def dma(self, qe, out, in_, reads=(), writes=()):
        lst = self.dsem[qe]
        i = self.dnext[qe]
        self.dnext[qe] = (i + 1) % self.NDS
        if len(lst) <= i:
            lst.append([self._newsem(), 0])
        s = lst[i]
        if s[1] >= self.ROT:
            self._wait(qe, s[0], s[1])
            s[0] = self._newsem()
            s[1] = 0
        self._waits(qe, reads, writes, is_dma=True)
        if s[1] > 0:
            self._wait(qe, s[0], s[1])
        s[1] += 16
        t = (s[0], s[1], "dma")
        self.q[qe].append(("d", out, in_, t))
        self.ninst += 1
        self._mark(t, reads, writes)
        return t

    def barrier(self):
        ticks = [(self.cur[e], self.cnt[e]) for e in self.ENG if self.cnt[e] > 0]
        for e in self.ENG:
            for s in self.dsem[e]:
                if s[1] > 0:
                    ticks.append((s[0], s[1]))
        for e in self.ENG:
            for sem, val in ticks:
                self._wait(e, sem, val)

    def emit(self):
        nc = self.nc

        def run(eng, items):
            for it in items:
                if it[0] == "w":
                    eng.wait_ge(it[1], it[2])
                elif it[0] == "i":
                    ins = it[1](eng)
                    if it[2] is not None:
                        ins.then_inc(it[2][0], 1)
                else:
                    eng.dma_start(out=it[1], in_=it[2]).then_inc(it[3][0], 16)

        with nc.Block() as block:
            @block.tensor
            def _(e):
                run(e, self.q["pe"])

            @block.scalar
            def _(e):
                run(e, self.q["act"])

            @block.vector
            def _(e):
                run(e, self.q["dve"])

            @block.gpsimd
            def _(e):
                run(e, self.q["pool"])

            @block.sync
            def _(e):
                run(e, self.q["sp"])


class Arena:
    def __init__(self, nc, es, nbytes):
        self.t = es.enter_context(nc.sbuf_tensor("arena", [128, nbytes // 2], BF16))
        self.size = nbytes
        self.off = 0

    def reset(self, off=0):
        self.off = off

    def alloc(self, free_shape, dt, parts=128):
        n = int(np.prod(free_shape))
        esz = 4 if dt == F32 else 2
        nb = (n * esz + 63) // 64 * 64
        assert self.off + nb <= self.size, ("arena overflow", self.off, nb, self.size)
        a = self.t[0:parts, self.off // 2:(self.off + n * esz) // 2]
        self.off += nb
        if dt == F32:
            a = a.bitcast(F32)
        if len(free_shape) == 2:
            a = a.rearrange("p (a b) -> p a b", b=free_shape[1])
        elif len(free_shape) == 3:
            a = a.rearrange("p (a b c) -> p a b c", b=free_shape[1], c=free_shape[2])
        return a


class Ctx:
    pass


def psum_next(c):
    i = c.ps_i
    c.ps_i = (i + 1) % len(c.ps)
    return c.ps[i], c.ps_b[i]


def ring_load(c, src_ap, kc, ncols, srcbufs=()):
    P = c.P
    i = c.ring_i
    c.ring_i = (i + 1) % len(c.ring)
    dst = c.ring[i][:, 0:kc * ncols].rearrange("p (a b) -> p a b", b=ncols)
    P.dma("pool", dst, src_ap, reads=srcbufs, writes=[c.ring_b[i]])
    return dst, c.ring_b[i]


def wview(w_ap, r0, nrows, c0, ncols):
    return w_ap[r0:r0 + nrows, c0:c0 + ncols].rearrange("(kc p) n -> p kc n", p=128)


def rmsnorm_tile(c, h_sb, h_bufs, gcol, out_sb, out_bufs, out_f32=False):
    P = c.P
    sq, rstd, ones_mean, gam = c.sq, c.rstd, c.ones_mean, c.gam
    sq_hb = c.sq_hb
    for hf in range(TT // 512):
        P.op("act", lambda e, hf=hf: e.activation(out=sq[:, :, hf * 512:(hf + 1) * 512], in_=h_sb[:, :, hf * 512:(hf + 1) * 512], func=AF.Square),
             reads=h_bufs, writes=[sq_hb[hf]])
    for hf in range(TT // 512):
        ps, pb = psum_next(c)
        fns = []
        for kc in range(8):
            fns.append(lambda e, kc=kc, ps=ps, hf=hf: e.matmul(ps[:], ones_mean[:], sq[:, kc, hf * 512:(hf + 1) * 512],
                                                              start=(kc == 0), stop=(kc == 7)))
        P.op("pe", fns, reads=[sq_hb[hf], c.const_b], writes=[pb])
        rs = rstd[:, hf * 512:(hf + 1) * 512]
        P.op("dve", lambda e, ps=ps, rs=rs: e.tensor_scalar_add(out=rs, in0=ps[:], scalar1=EPS), reads=[pb], writes=[c.rstd_b[hf]])
        P.op("act", lambda e, rs=rs: e.activation(out=rs, in_=rs, func=AF.Sqrt), reads=[c.rstd_b[hf]], writes=[c.rstd_b[hf]])
        P.op("dve", lambda e, rs=rs: e.reciprocal(out=rs, in_=rs), reads=[c.rstd_b[hf]], writes=[c.rstd_b[hf]])
    for kc in range(8):
        P.op("dve", lambda e, kc=kc: e.scalar_tensor_tensor(out=out_sb[:, kc, :], in0=h_sb[:, kc, :],
                                                            scalar=gam[:, gcol * 8 + kc:gcol * 8 + kc + 1], in1=rstd[:],
                                                            op0=ALU.mult, op1=ALU.mult),
             reads=[h_bufs[kc], c.const_b] + c.rstd_b,
             writes=(out_bufs[kc] if isinstance(out_bufs[kc], list) else [out_bufs[kc]]))


def linear_fm(c, w_ap, K, N, x_sb, x_bufs, evac, col_piece=None):
    P = c.P
    KC = K // 128
    if col_piece is None:
        col_piece = min(N, 4096 // KC)
    for c0 in range(0, N, col_piece):
        ncol = min(col_piece, N - c0)
        wsb, wb = ring_load(c, wview(w_ap, 0, K, c0, ncol), KC, ncol)
        for j in range((ncol + 127) // 128):
            dc = (c0 // 128) + j
            wd = min(128, ncol - j * 128)
            for hf in range(TT // 512):
                ps, pb = psum_next(c)
                fns = []
                for kc in range(KC):
                    fns.append(lambda e, kc=kc, ps=ps, hf=hf, j=j, wsb=wsb, wd=wd: e.matmul(
                        ps[0:wd, :], wsb[:, kc, j * 128:j * 128 + wd], x_sb[:, kc, hf * 512:(hf + 1) * 512],
                        start=(kc == 0), stop=(kc == KC - 1)))
                P.op("pe", fns, reads=[wb] + list(x_bufs[:KC]), writes=[pb])
                evac(dc, hf, ps, pb)


def token_stage(c, li, yT, CK, wo_ap, bo_col, first=False):
    P = c.P
    A = c.arena
    A.reset(c.arena_base)
    h_sb = A.alloc([8, TT], F32)
    hn_sb = A.alloc([8, TT], BF16)
    big_flat = A.alloc([16 * TT], BF16)
    big = big_flat.rearrange("p (a b) -> p a b", b=TT)
    big_f32 = big_flat.bitcast(F32).rearrange("p (a b) -> p a b", b=TT)
    c.sq = A.alloc([8, TT], BF16)
    c.rstd = A.alloc([TT], F32)
    tmp = [A.alloc([512], F32) for _ in range(3)]
    ptm = A.alloc([8, PLE], F32)
    pT = A.alloc([2, TT], BF16)
    c.ring = [A.alloc([4096], BF16) for _ in range(5)]
    h_b = [Buf() for _ in range(8)]
    hn_b = [Buf() for _ in range(8)]
    big_b = [Buf() for _ in range(16)]
    c.sq_b = Buf()
    c.sq_hb = [Buf() for _ in range(TT // 512)]
    c.rstd_b = [Buf() for _ in range(TT // 512)]
    tmp_b = [Buf() for _ in range(3)]
    ptm_b = Buf()
    pT_b = Buf()
    c.ring_b = [Buf() for _ in c.ring]
    c.ring_i = 0
    tmp_i = [0]
    xs_b = Buf()

    def next_tmp():
        i = tmp_i[0]
        tmp_i[0] = (i + 1) % 3
        return tmp[i], tmp_b[i]

    last = (li == c.last_layer) and not first
    skip = c.skip

    def load_Y(tt_):
        P.dma("sp", big[:, 0:CK, :], yT[0:CK * 128, tt_ * TT:(tt_ + 1) * TT].rearrange("(c p) t -> p c t", p=128),
              reads=[c.yT_b[tt_]], writes=big_b[:CK])

    def load_p(tt_):
        P.dma("sp", ptm[:], c.p_in[li, tt_ * TT:(tt_ + 1) * TT, :].rearrange("(s p) d -> p s d", p=128), writes=[ptm_b])

    for tt in range(NTT):
        t0 = tt * TT
        if first:
            x_sb = big_f32
            P.dma("sp", x_sb[:, :, :], c.x_in[t0:t0 + TT, :].rearrange("(s p) d -> p s d", p=128), writes=[xs_b])
            for hh in range(2):
                for dc in range(8):
                    ps, pb = psum_next(c)
                    fns = [lambda e, s=s, dc=dc, ps=ps, hh=hh: e.transpose(ps[:, s * 128:(s + 1) * 128], x_sb[:, hh * 4 + s, dc * 128:(dc + 1) * 128], c.ident_f[:])
                           for s in range(4)]
                    P.op("pe", fns, reads=[xs_b, c.const_b], writes=[pb])
                    P.op("act", lambda e, ps=ps, dc=dc, hh=hh: e.copy(out=h_sb[:, dc, hh * 512:(hh + 1) * 512], in_=ps[:]),
                         reads=[pb], writes=[h_b[dc]])
        else:
            P.dma("sp", h_sb[:], c.hT[:, t0:t0 + TT].rearrange("(c p) t -> p c t", p=128), reads=[c.hT_b[tt]], writes=h_b)
            if tt == 0 or last:
                load_Y(tt)
                load_p(tt)

            def ev_out(dc, hf, ps, pb):
                sl = slice(hf * 512, (hf + 1) * 512)
                if bo_col is None:
                    P.op("dve", lambda e: e.tensor_tensor(out=h_sb[:, dc, sl], in0=h_sb[:, dc, sl], in1=ps[:], op=ALU.add),
                         reads=[pb, h_b[dc]], writes=[h_b[dc]])
                else:
                    P.op("dve", lambda e: e.scalar_tensor_tensor(out=h_sb[:, dc, sl], in0=ps[:], scalar=c.gam[:, bo_col * 8 + dc:bo_col * 8 + dc + 1],
                                                                 in1=h_sb[:, dc, sl], op0=ALU.add, op1=ALU.add),
                         reads=[pb, h_b[dc], c.const_b], writes=[h_b[dc]])
            if "outproj" not in skip:
                linear_fm(c, wo_ap, CK * 128, D, big, big_b, ev_out)

            rmsnorm_tile(c, h_sb, h_b, c.gcol_ffn + li, hn_sb, hn_b)
            w1 = c.ffn_w1[li]
            w3 = c.ffn_w3[li]
            w2 = c.ffn_w2[li]
            for fh in range(0 if "ffn" not in skip else 2, 2):
                f_base = fh * 11
                groups = [(0, 4), (4, 8), (8, 11)]
                for (ga, gb) in groups:
                    n = gb - ga
                    w1s, w1b = ring_load(c, wview(w1, 0, D, (f_base + ga) * 128, n * 128), 8, n * 128)
                    w3s, w3b = ring_load(c, wview(w3, 0, D, (f_base + ga) * 128, n * 128), 8, n * 128)
                    for j in range(n):
                        fl = ga + j
                        for hf in range(TT // 512):
                            sl = slice(hf * 512, (hf + 1) * 512)
                            psa, pba = psum_next(c)
                            psb, pbb = psum_next(c)
                            P.op("pe", [lambda e, kc=kc, psa=psa, j=j, sl=sl, w1s=w1s: e.matmul(psa[:], w1s[:, kc, j * 128:(j + 1) * 128], hn_sb[:, kc, sl],
                                                                                               start=(kc == 0), stop=(kc == 7)) for kc in range(8)],
                                 reads=[w1b] + hn_b, writes=[pba])
                            P.op("pe", [lambda e, kc=kc, psb=psb, j=j, sl=sl, w3s=w3s: e.matmul(psb[:], w3s[:, kc, j * 128:(j + 1) * 128], hn_sb[:, kc, sl],
                                                                                               start=(kc == 0), stop=(kc == 7)) for kc in range(8)],
                                 reads=[w3b] + hn_b, writes=[pbb])
                            tm, tmb = next_tmp()
                            P.op("act", lambda e, tm=tm, psa=psa: e.activation(out=tm[:], in_=psa[:], func=AF.Silu), reads=[pba], writes=[tmb])
                            P.op("dve", lambda e, tm=tm, psb=psb, fl=fl, sl=sl: e.tensor_tensor(out=big[:, fl, sl], in0=tm[:], in1=psb[:], op=ALU.mult),
                                 reads=[tmb, pbb], writes=[big_b[fl]])
                w2p = []
                for (ga, gb) in groups:
                    n = gb - ga
                    w2p.append(ring_load(c, wview(w2, (f_base + ga) * 128, n * 128, 0, D), n, D) + (ga, gb))
                for dc in range(8):
                    for hf in range(TT // 512):
                        sl = slice(hf * 512, (hf + 1) * 512)
                        ps, pb = psum_next(c)
                        fns = []
                        for (ws, wb, ga, gb) in w2p:
                            for j in range(gb - ga):
                                fl = ga + j
                                fns.append(lambda e, ws=ws, j=j, fl=fl, ps=ps, sl=sl, dc=dc: e.matmul(
                                    ps[:], ws[:, j, dc * 128:(dc + 1) * 128], big[:, fl, sl], start=(fl == 0), stop=(fl == 10)))
                        P.op("pe", fns, reads=[x[1] for x in w2p] + big_b[:11], writes=[pb])
                        P.op("dve", lambda e, ps=ps, dc=dc, sl=sl: e.tensor_tensor(out=h_sb[:, dc, sl], in0=h_sb[:, dc, sl], in1=ps[:], op=ALU.add),
                             reads=[pb, h_b[dc]], writes=[h_b[dc]])

            if not last and tt + 1 < NTT:
                load_Y(tt + 1)
            rmsnorm_tile(c, h_sb, h_b, c.gcol_ple + li, hn_sb, hn_b)
            for kc in range(2):
                for hf in range(TT // 512):
                    ps, pb = psum_next(c)
                    fns = [lambda e, s=s, kc=kc, hf=hf, ps=ps: e.transpose(ps[:, s * 128:(s + 1) * 128], ptm[:, hf * 4 + s, kc * 128:(kc + 1) * 128], c.ident_f[:])
                           for s in range(4)]
                    P.op("pe", fns, reads=[ptm_b, c.const_b], writes=[pb])
                    P.op("act", lambda e, ps=ps, kc=kc, hf=hf: e.copy(out=pT[:, kc, hf * 512:(hf + 1) * 512], in_=ps[:]),
                         reads=[pb], writes=[pT_b])
            if not last and tt + 1 < NTT:
                load_p(tt + 1)
            gate_t = {}
            wp_sb, wp_b = ring_load(c, wview(c.ple_proj[li], 0, PLE, 0, D), 2, D)

            def ev_gate(dc, hf, ps, pb, wp_sb=wp_sb, wp_b=wp_b):
                tm, tmb = next_tmp()
                P.op("act", lambda e: e.activation(out=tm[:], in_=ps[:], func=AF.Sigmoid), reads=[pb], writes=[tmb])
                gate_t[(dc, hf)] = (tm, tmb)
                ps2, pb2 = psum_next(c)
                sl = slice(hf * 512, (hf + 1) * 512)
                P.op("pe", [lambda e, kc=kc: e.matmul(ps2[:], wp_sb[:, kc, dc * 128:(dc + 1) * 128], pT[:, kc, sl], start=(kc == 0), stop=(kc == 1))
                            for kc in range(2)], reads=[wp_b, pT_b], writes=[pb2])
                P.op("dve", lambda e: e.tensor_tensor(out=tm[:], in0=tm[:], in1=ps2[:], op=ALU.mult), reads=[tmb, pb2], writes=[tmb])
                P.op("dve", lambda e: e.tensor_tensor(out=h_sb[:, dc, sl], in0=h_sb[:, dc, sl], in1=tm[:], op=ALU.add),
                     reads=[tmb, h_b[dc]], writes=[h_b[dc]])
            if "ple" not in skip:
                linear_fm(c, c.ple_gate[li], D, D, hn_sb, hn_b, ev_gate)

        if not last:
            P.dma("act", c.hT[:, t0:t0 + TT].rearrange("(c p) t -> p c t", p=128), h_sb[:], reads=h_b, writes=[c.hT_b[tt]])
            nl = 0 if first else li + 1
            rmsnorm_tile(c, h_sb, h_b, c.gcol_mix + nl, hn_sb, hn_b)
            P.dma("act", c.hnT[:, t0:t0 + TT].rearrange("(c p) t -> p c t", p=128), hn_sb[:], reads=hn_b, writes=[c.hnT_b[tt]])
        else:
            onf = big_f32
            fin_b = [[big_b[2 * k], big_b[2 * k + 1]] for k in range(8)]
            rmsnorm_tile(c, h_sb, h_b, c.gcol_fin, onf, fin_b, out_f32=True)
            for s in range(TT // 128):
                for dg in range(2):
                    ps, pb = psum_next(c)
                    fns = [lambda e, s=s, dg=dg, j=j, ps=ps: e.transpose(ps[:, j * 128:(j + 1) * 128], onf[:, dg * 4 + j, s * 128:(s + 1) * 128], c.ident_f[:])
                           for j in range(4)]
                    P.op("pe", fns, reads=big_b + [c.const_b], writes=[pb])
                    ob, obb = c.ostage[c.ost_i], c.ostage_b[c.ost_i]
                    c.ost_i = (c.ost_i + 1) % len(c.ostage)
                    P.op("act", lambda e, ps=ps, ob=ob: e.copy(out=ob[:], in_=ps[:]), reads=[pb], writes=[obb])
                    P.dma("act", c.out[t0 + s * 128:t0 + (s + 1) * 128, dg * 512:(dg + 1) * 512], ob[:], reads=[obb], writes=[c.out_b])
    P.barrier()


def proj_stage(c, w_ap, N, out_dram, bias_col=None, odt=BF16):
    P = c.P
    A = c.arena
    A.reset(c.arena_base)
    hn_sb = A.alloc([8, TT], BF16)
    hn_b = [Buf() for _ in range(8)]
    stg = [A.alloc([TT], odt) for _ in range(3)]
    stg_b = [Buf() for _ in range(3)]
    c.ring = [A.alloc([4096], BF16) for _ in range(5)]
    c.ring_b = [Buf() for _ in c.ring]
    c.ring_i = 0
    gam = c.gam
    for tt in range(NTT):
        t0 = tt * TT
        P.dma("sp", hn_sb[:], c.hnT[:, t0:t0 + TT].rearrange("(c p) t -> p c t", p=128), writes=hn_b)

        def evac(dc, hf, ps, pb, t0=t0):
            wd = min(128, N - dc * 128)
            sg, sgb = stg[dc % 3], stg_b[dc % 3]
            sl = slice(hf * 512, (hf + 1) * 512)
            if bias_col is None:
                P.op("act", lambda e: e.copy(out=sg[0:wd, sl], in_=ps[0:wd, :]), reads=[pb], writes=[sgb])
            else:
                col = bias_col * 8 + dc
                P.op("act", lambda e: e.activation(out=sg[0:wd, sl], in_=ps[0:wd, :], func=AF.Identity, bias=gam[0:wd, col:col + 1]),
                     reads=[pb, c.const_b], writes=[sgb])
            if hf == TT // 512 - 1:
                P.dma("act", out_dram[dc * 128:dc * 128 + wd, t0:t0 + TT], sg[0:wd, :], reads=[sgb])
        linear_fm(c, w_ap, D, N, hn_sb, hn_b, evac)
    P.barrier()


def proj_stage_tm(c, w_ap, N, out_dram):
    P = c.P
    A = c.arena
    A.reset(c.arena_base)
    hn_sb = A.alloc([8, TT], BF16)
    hn_b = [Buf() for _ in range(8)]
    stg = [A.alloc([512], BF16) for _ in range(4)]
    stg_b = [Buf() for _ in range(4)]
    c.ring = [A.alloc([4096], BF16) for _ in range(5)]
    c.ring_b = [Buf() for _ in c.ring]
    c.ring_i = 0
    si = 0
    for tt in range(NTT):
        t0 = tt * TT
        P.dma("sp", hn_sb[:], c.hnT[:, t0:t0 + TT].rearrange("(c p) t -> p c t", p=128), writes=hn_b)
        for n0 in range(0, N, 512):
            wsb, wb = ring_load(c, wview(w_ap, 0, D, n0, 512), 8, 512)
            for sub in range(TT // 128):
                ps, pb = psum_next(c)
                P.op("pe", [lambda e, kc=kc, ps=ps, sub=sub, wsb=wsb: e.matmul(ps[:], hn_sb[:, kc, sub * 128:(sub + 1) * 128], wsb[:, kc, :],
                                                                              start=(kc == 0), stop=(kc == 7)) for kc in range(8)],
                     reads=[wb] + hn_b, writes=[pb])
                sg, sgb = stg[si], stg_b[si]
                si = (si + 1) % 4
                P.op("act", lambda e, sg=sg, ps=ps: e.copy(out=sg[:], in_=ps[:]), reads=[pb], writes=[sgb])
                P.dma("act", out_dram[t0 + sub * 128:t0 + (sub + 1) * 128, n0:n0 + 512], sg[:], reads=[sgb])
    P.barrier()


def hyena_core(c, j):
    P = c.P
    A = c.arena
    A.reset(c.arena_base)
    gam = c.gam
    G = c.gcol_hy
    PI = math.pi
    hid2 = A.alloc([L], F32)
    mark = A.off
    zT = A.alloc([L], F32)
    hid1 = A.alloc([L], F32)
    w1 = A.alloc([64], F32)
    w2 = A.alloc([64], F32)
    sc = A.alloc([8], F32)
    mtmp = A.alloc([512], F32)
    mt_b = Buf()
    MAGIC = 12582912.0
    mb = Buf()
    P.dma("sp", zT[0:33, :], c.hy_z[:, :], writes=[mb])
    P.dma("sp", w1[0:33, :], c.hy_f_w1[j], writes=[mb])
    P.dma("sp", w2[0:64, :], c.hy_f_w2[j], writes=[mb])
    P.dma("sp", sc[0:64, 0:1], c.hy_sin_freq[j].rearrange("(p o) -> p o", o=1), writes=[mb])
    P.dma("sp", sc[0:64, 1:2], c.hy_f_b1[j].rearrange("(p o) -> p o", o=1), writes=[mb])
    P.dma("sp", sc[0:64, 2:3], c.hy_f_b2[j].rearrange("(p o) -> p o", o=1), writes=[mb])
    P.barrier()
    P.op("dve", lambda e: e.tensor_tensor(out=sc[0:64, 3:4], in0=sc[0:64, 1:2], in1=sc[0:64, 0:1], op=ALU.mult), reads=[mb], writes=[mb])
    P.op("dve", lambda e: e.tensor_tensor(out=sc[0:64, 4:5], in0=sc[0:64, 2:3], in1=sc[0:64, 0:1], op=ALU.mult), reads=[mb], writes=[mb])
    P.op("dve", lambda e: e.memset(sc[0:64, 5:6], -PI), reads=[mb], writes=[mb])
    hb1 = Buf()
    hb2 = Buf()
    for (wsb, kk, src, dst, fbcol, sb_, db_) in ((w1, 33, zT, hid1, 3, mb, hb1), (w2, 64, hid1, hid2, 4, hb1, hb2)):
        for tb in range(8):
            sl = slice(tb * 512, (tb + 1) * 512)
            ps, pb = psum_next(c)
            P.op("pe", lambda e, ps=ps, wsb=wsb, kk=kk, src=src, sl=sl: e.matmul(ps[0:64, :], wsb[0:kk, 0:64], src[0:kk, sl], start=True, stop=True),
                 reads=[mb, sb_], writes=[pb])
            P.op("dve", lambda e, ps=ps, dst=dst, sl=sl, fbcol=fbcol: e.tensor_scalar(out=dst[0:64, sl], in0=ps[0:64, :], scalar1=sc[0:64, 0:1],
                                                                                    scalar2=sc[0:64, fbcol:fbcol + 1], op0=ALU.mult, op1=ALU.add),
                 reads=[pb, mb], writes=[db_])
            P.op("dve", lambda e, dst=dst, sl=sl: e.tensor_scalar(out=mtmp[0:64, :], in0=dst[0:64, sl], scalar1=1.0 / (2.0 * PI), scalar2=MAGIC,
                                                                 op0=ALU.mult, op1=ALU.add), reads=[db_], writes=[mt_b])
            P.op("dve", lambda e: e.tensor_scalar(out=mtmp[0:64, :], in0=mtmp[0:64, :], scalar1=-MAGIC, scalar2=-2.0 * PI,
                                                  op0=ALU.add, op1=ALU.mult), reads=[mt_b], writes=[mt_b])
            P.op("dve", lambda e, dst=dst, sl=sl: e.tensor_tensor(out=dst[0:64, sl], in0=dst[0:64, sl], in1=mtmp[0:64, :], op=ALU.add),
                 reads=[mt_b, db_], writes=[db_])
            P.op("act", lambda e, dst=dst, sl=sl: e.activation(out=dst[0:64, sl], in_=dst[0:64, sl], func=AF.Sin),
                 reads=[db_], writes=[db_])
    P.barrier()
    A.reset(mark)
    w3 = A.alloc([2048], F32)
    hf = A.alloc([L], F32)
    hb = A.alloc([L], F32)
    hs = A.alloc([L], F32)
    dstg = [A.alloc([512], F32) for _ in range(2)]
    ub = [A.alloc([L + 2], BF16) for _ in range(3)]
    x0c = A.alloc([L], F32)
    x1c = A.alloc([L], F32)
    vv = A.alloc([L], F32)
    tstg = [A.alloc([4, 128], BF16) for _ in range(3)]
    w3_b, hf_b, hb_b, hs_b, x0_b, x1_b, vv_b = [Buf() for _ in range(7)]
    dstg_b = [Buf() for _ in range(2)]
    ub_b = [Buf() for _ in range(3)]
    tstg_b = [Buf() for _ in range(3)]
    tsi = [0]
    P.dma("sp", w3[0:64, :], c.hy_f_w3[j], writes=[w3_b])
    for k in range(3):
        P.op("dve", lambda e, k=k: e.memset(ub[k][:, 0:1], 0.0), writes=[ub_b[k]])
        P.op("dve", lambda e, k=k: e.memset(ub[k][:, L + 1:L + 2], 0.0), writes=[ub_b[k]])

    def tm_store(src, src_b, kind, cc):
        sv = src[:].rearrange("p (t two) -> p t two", two=2)
        for par in range(2):
            for t4 in range(4):
                ps, pb = psum_next(c)
                P.op("pe", [lambda e, k=k, ps=ps, t4=t4, par=par: e.transpose(ps[:, k * 128:(k + 1) * 128], sv[:, (t4 * 4 + k) * 128:(t4 * 4 + k + 1) * 128, par], c.ident_f[:])
                            for k in range(4)], reads=[src_b, c.const_b], writes=[pb])
                sg, sgb = tstg[tsi[0]], tstg_b[tsi[0]]
                tsi[0] = (tsi[0] + 1) % 3
                P.op("act", lambda e, sg=sg, ps=ps: e.copy(out=sg[:].rearrange("p k t -> p (k t)"), in_=ps[:]), reads=[pb], writes=[sgb])
                P.dma("act", c.hy_tm[kind, par, t4 * 512:(t4 + 1) * 512, cc * 128:(cc + 1) * 128].rearrange("(k p) c -> p k c", p=128), sg[:], reads=[sgb])

    def prep(cc):
        P.dma("sp", x1c[:], c.hy_decay[cc * 128:(cc + 1) * 128, :], writes=[x1_b])
        for (dirn, hsb, hbuf) in ((0, hf, hf_b), (1, hb, hb_b)):
            for tb in range(8):
                sl = slice(tb * 512, (tb + 1) * 512)
                ps, pb = psum_next(c)
                c0 = dirn * 1024 + cc * 128
                P.op("pe", lambda e, ps=ps, c0=c0, sl=sl: e.matmul(ps[:], w3[0:64, c0:c0 + 128], hid2[0:64, sl], start=True, stop=True),
                     reads=[w3_b, hb2], writes=[pb])
                P.op("dve", lambda e, ps=ps, hsb=hsb, sl=sl: e.tensor_tensor(out=hsb[:, sl], in0=ps[:], in1=x1c[:, sl], op=ALU.mult),
                     reads=[pb, x1_b], writes=[hbuf])
        P.op("dve", lambda e: e.memset(hb[:, 0:1], 0.0), reads=[hb_b], writes=[hb_b])
        P.op("dve", lambda e: e.tensor_tensor(out=hs[:], in0=hf[:], in1=hb[:], op=ALU.add), reads=[hf_b, hb_b], writes=[hs_b])
        P.op("dve", lambda e: e.tensor_tensor(out=hb[:], in0=hf[:], in1=hb[:], op=ALU.subtract), reads=[hf_b, hb_b], writes=[hb_b])
        tm_store(hs, hs_b, 0, cc)
        tm_store(hb, hb_b, 1, cc)
        for k in range(3):
            r0 = k * 1024 + cc * 128
            P.dma("sp", ub[k][:, 1:L + 1], c.uT[r0:r0 + 128, :], writes=[ub_b[k]])
        outs = (x0c, x1c, vv)
        outs_b = (x0_b, x1_b, vv_b)
        for k in range(3):
            o, ob = outs[k], outs_b[k]
            wc = [(G + 3 + tap * 3 + k) * 8 + cc for tap in range(3)]
            bc = (G + 12 + k) * 8 + cc
            P.op("dve", lambda e, o=o, k=k, wc=wc, bc=bc: e.tensor_scalar(out=o[:], in0=ub[k][:, 0:L], scalar1=gam[:, wc[0]:wc[0] + 1],
                                                                          scalar2=gam[:, bc:bc + 1], op0=ALU.mult, op1=ALU.add),
                 reads=[ub_b[k], c.const_b], writes=[ob])
            for tap in (1, 2):
                P.op("dve", lambda e, o=o, k=k, wc=wc, tap=tap: e.scalar_tensor_tensor(out=o[:], in0=ub[k][:, tap:tap + L], scalar=gam[:, wc[tap]:wc[tap] + 1],
                                                                                      in1=o[:], op0=ALU.mult, op1=ALU.add),
                     reads=[ub_b[k], c.const_b, ob], writes=[ob])
        P.op("dve", lambda e: e.tensor_tensor(out=vv[:], in0=vv[:], in1=x1c[:], op=ALU.mult), reads=[vv_b, x1_b], writes=[vv_b])
        P.dma("pool", c.hy_x0T[cc * 128:(cc + 1) * 128, :], x0c[:], reads=[x0_b])
        P.dma("pool", c.hy_vvT[cc * 128:(cc + 1) * 128, :], vv[:], reads=[vv_b])
        tm_store(vv, vv_b, 2, cc)

    for cc in range(8):
        prep(cc)
    P.barrier()

    A.reset(c.arena_base)
    tme = [A.alloc([16, 512], BF16) for _ in range(4)]
    tab = [[A.alloc([16, 128], BF16) for _ in range(4)] for _ in range(2)]
    tsb = [A.alloc([512], F32) for _ in range(4)]
    hst = [A.alloc([4, 512], F32) for _ in range(2)]
    tme_b = [Buf() for _ in range(4)]
    tab_b = [Buf() for _ in range(2)]
    tsb_b = [Buf() for _ in range(4)]
    hst_b = [Buf() for _ in range(2)]

    def load_tabs(kc):
        b = kc % 2
        for ti in range(4):
            P.dma("sp", tab[b][ti][:], c.hy_ftab[ti, kc].rearrange("p (a k) -> p a k", k=128), writes=[tab_b[b]])
        return tab[b], tab_b[b]

    def fwd_groups(kc, srcs, src_bufs):
        tb_, tbb = load_tabs(kc)
        outs = []
        for gi in range(4):
            ps, pb = psum_next(c)
            P.op("pe", [lambda e, a=a, ps=ps, gi=gi: e.matmul(ps[:], tb_[gi][:, a, :], srcs[gi][:, a, :], start=(a == 0), stop=(a == 15)) for a in range(16)],
                 reads=[tbb, src_bufs[gi]], writes=[pb])
            outs.append((ps, pb))
        return outs

    def combine(outs, dst4, dst_b):
        (pce, bce), (pco, bco), (pse, bse), (pso, bso) = outs
        i1, i2 = (0, 1) if combine.flip == 0 else (2, 3)
        combine.flip ^= 1
        t1, t1b, t2, t2b = tsb[i1], tsb_b[i1], tsb[i2], tsb_b[i2]
        P.op("act", lambda e: e.copy(out=t1[:], in_=pco[:]), reads=[bco], writes=[t1b])
        P.op("act", lambda e: e.copy(out=t2[:], in_=pso[:]), reads=[bso], writes=[t2b])
        P.op("dve", lambda e: e.tensor_tensor(out=dst4[0], in0=pce[:], in1=t1[:], op=ALU.add), reads=[bce, t1b], writes=[dst_b])
        P.op("dve", lambda e: e.tensor_tensor(out=dst4[2], in0=pce[:], in1=t1[:], op=ALU.subtract), reads=[bce, t1b], writes=[dst_b])
        P.op("dve", lambda e: e.scalar_tensor_tensor(out=dst4[1], in0=pse[:], scalar=-1.0, in1=t2[:], op0=ALU.mult, op1=ALU.subtract),
             reads=[bse, t2b], writes=[dst_b])
        P.op("dve", lambda e: e.tensor_tensor(out=dst4[3], in0=pse[:], in1=t2[:], op=ALU.subtract), reads=[bse, t2b], writes=[dst_b])
    combine.flip = 0

    def passF(half):
        cs = slice(half * 512, (half + 1) * 512)
        for i, (kind, par) in enumerate(((0, 0), (0, 1), (1, 0), (1, 1))):
            P.dma("sp", tme[i][:], c.hy_tm[kind, par, :, cs].rearrange("(a p) c -> p a c", p=128), writes=[tme_b[i]])
        for kc in range(16):
            outs = fwd_groups(kc, tme, tme_b)
            hb_i = kc % 2
            st, stb = hst[hb_i], hst_b[hb_i]
            combine(outs, [st[:, q, :] for q in range(4)], stb)
            P.dma("pool", c.hy_hF[kc, :, :, cs].rearrange("q p c -> p q c"), st[:], reads=[stb])

    for half in range(2):
        passF(half)
    P.barrier()

    A.reset(c.arena_base)
    Z = A.alloc([16, 4, 512], BF16)
    Z_b = [Buf() for _ in range(16)]
    mark2 = A.off
    ueo = [A.alloc([16, 512], BF16) for _ in range(2)]
    tab = [[A.alloc([16, 128], BF16) for _ in range(4)] for _ in range(2)]
    tsb = [A.alloc([512], F32) for _ in range(4)]
    hsl = [A.alloc([4, 512], F32) for _ in range(2)]
    U4 = A.alloc([4, 512], F32)
    Y4 = A.alloc([4, 512], F32)
    mm = [A.alloc([512], F32) for _ in range(2)]
    ueo_b = [Buf() for _ in range(2)]
    tab_b = [Buf() for _ in range(2)]
    tsb_b = [Buf() for _ in range(4)]
    hsl_b = [Buf() for _ in range(2)]
    U_b, Y_b = Buf(), Buf()
    mm_b = [Buf() for _ in range(2)]
    fwd_end = A.off
    A.reset(mark2)
    itab = [[A.alloc([16, 512], BF16) for _ in range(2)] for _ in range(2)]
    ycomb = [A.alloc([512, 2], F32) for _ in range(4)]
    vsl = [A.alloc([1024], BF16) for _ in range(2)]
    xsl = [A.alloc([1024], BF16) for _ in range(2)]
    ost = [A.alloc([1024], BF16) for _ in range(2)]
    ytmp = [A.alloc([1024], F32) for _ in range(2)]
    itab_b = [Buf() for _ in range(2)]
    ycomb_b = [Buf() for _ in range(4)]
    vsl_b = [Buf() for _ in range(2)]
    xsl_b = [Buf() for _ in range(2)]
    ost_b = [Buf() for _ in range(2)]
    ytmp_b = [Buf() for _ in range(2)]
    skg = (G + 15) * 8

    def cmul(kc, hq):
        U = [U4[:, q, :] for q in range(4)]
        Y = [Y4[:, q, :] for q in range(4)]
        H = [hsl[hq][:, q, :] for q in range(4)]
        hb_ = hsl_b[hq]
        for o in (0, 2):
            re, im = o, o + 1
            P.op("dve", lambda e, re=re: e.tensor_tensor(out=mm[0][:], in0=U[re], in1=H[re], op=ALU.mult), reads=[U_b, hb_], writes=[mm_b[0]])
            P.op("dve", lambda e, im=im: e.tensor_tensor(out=mm[1][:], in0=U[im], in1=H[im], op=ALU.mult), reads=[U_b, hb_], writes=[mm_b[1]])
            P.op("dve", lambda e, re=re: e.tensor_tensor(out=Y[re], in0=mm[0][:], in1=mm[1][:], op=ALU.subtract), reads=mm_b, writes=[Y_b])
            P.op("dve", lambda e, re=re, im=im: e.tensor_tensor(out=mm[0][:], in0=U[re], in1=H[im], op=ALU.mult), reads=[U_b, hb_, Y_b], writes=[mm_b[0]])
            P.op("dve", lambda e, re=re, im=im: e.tensor_tensor(out=mm[1][:], in0=U[im], in1=H[re], op=ALU.mult), reads=[U_b, hb_, Y_b], writes=[mm_b[1]])
            P.op("dve", lambda e, im=im: e.tensor_tensor(out=Y[im], in0=mm[0][:], in1=mm[1][:], op=ALU.add), reads=mm_b, writes=[Y_b])
        P.op("dve", lambda e: e.tensor_tensor(out=Z[:, kc, 0, :], in0=Y[0], in1=Y[2], op=ALU.add), reads=[Y_b], writes=[Z_b[kc]])
        P.op("dve", lambda e: e.tensor_tensor(out=Z[:, kc, 1, :], in0=Y[1], in1=Y[3], op=ALU.subtract), reads=[Y_b], writes=[Z_b[kc]])
        P.op("dve", lambda e: e.tensor_tensor(out=Z[:, kc, 2, :], in0=Y[0], in1=Y[2], op=ALU.subtract), reads=[Y_b], writes=[Z_b[kc]])
        P.op("dve", lambda e: e.tensor_tensor(out=Z[:, kc, 3, :], in0=Y[1], in1=Y[3], op=ALU.add), reads=[Y_b], writes=[Z_b[kc]])

    def passU(half):
        cs = slice(half * 512, (half + 1) * 512)
        for par in range(2):
            P.dma("sp", ueo[par][:], c.hy_tm[2, par, :, cs].rearrange("(a p) c -> p a c", p=128), writes=[ueo_b[par]])
        srcs = [ueo[0], ueo[1], ueo[0], ueo[1]]
        sbufs = [ueo_b[0], ueo_b[1], ueo_b[0], ueo_b[1]]
        for kc in range(16):
            hq = kc % 2
            P.dma("sp", hsl[hq][:], c.hy_hF[kc, :, :, cs].rearrange("q p c -> p q c"), writes=[hsl_b[hq]])
            outs = fwd_groups(kc, srcs, sbufs)
            combine(outs, [U4[:, q, :] for q in range(4)], U_b)
            cmul(kc, hq)
        P.barrier()
        ii = [0]
        for tb in range(4):
            for par in range(2):
                ib_ = ii[0] % 2
                ii[0] += 1
                for q in range(2):
                    P.dma("sp", itab[ib_][q][:], c.hy_itab[2 * par + q, tb].rearrange("p (a t) -> p a t", t=512), writes=[itab_b[ib_]])
                for cq in range(4):
                    ps, pb = psum_next(c)
                    fns = []
                    for kc in range(16):
                        for q in range(2):
                            fns.append(lambda e, kc=kc, q=q, ps=ps, cq=cq, par=par, ib_=ib_: e.matmul(
                                ps[:], Z[:, kc, 2 * par + q, cq * 128:(cq + 1) * 128], itab[ib_][q][:, kc, :],
                                start=(kc == 0 and q == 0), stop=(kc == 15 and q == 1)))
                    P.op("pe", fns, reads=Z_b + [itab_b[ib_]], writes=[pb])
                    P.op("act", lambda e, ps=ps, cq=cq, par=par: e.copy(out=ycomb[cq][:, :, par], in_=ps[:]), reads=[pb], writes=[ycomb_b[cq]])
            for cq in range(4):
                r0 = half * 512 + cq * 128
                sb_ = cq % 2
                P.dma("sp", vsl[sb_][:], c.hy_vvT[r0:r0 + 128, tb * 1024:(tb + 1) * 1024], writes=[vsl_b[sb_]])
                P.dma("sp", xsl[sb_][:], c.hy_x0T[r0:r0 + 128, tb * 1024:(tb + 1) * 1024], writes=[xsl_b[sb_]])
                skc = skg + half * 4 + cq
                P.op("dve", lambda e, sb_=sb_, cq=cq, skc=skc: e.scalar_tensor_tensor(out=ytmp[sb_][:], in0=vsl[sb_][:], scalar=gam[:, skc:skc + 1],
                                                                                    in1=ycomb[cq][:].rearrange("p t two -> p (t two)"), op0=ALU.mult, op1=ALU.add),
                     reads=[vsl_b[sb_], ycomb_b[cq], c.const_b], writes=[ytmp_b[sb_]])
                P.op("dve", lambda e, sb_=sb_: e.tensor_tensor(out=ost[sb_][:], in0=ytmp[sb_][:], in1=xsl[sb_][:], op=ALU.mult),
                     reads=[ytmp_b[sb_], xsl_b[sb_]], writes=[ost_b[sb_]])
                P.dma("pool", c.yT[r0:r0 + 128, tb * 1024:(tb + 1) * 1024], ost[sb_][:], reads=[ost_b[sb_]])
        P.barrier()

    for half in range(2):
        passU(half)
    P.barrier()


def gla_core(c, j):
    P = c.P
    A = c.arena
    A.reset(c.arena_base)
    cst = c.cst
    Mtri = (cst[:, 384:512], cst[:, 512:640])
    Mrev = (cst[:, 640:768], cst[:, 768:896])
    mask = (cst[:, 896:1024], cst[:, 1024:1152])
    NT = L // 128
    SCALE = 128.0 ** -0.5
    gkw = [A.alloc([512], BF16) for _ in range(2)]
    gkb = [A.alloc([512], BF16) for _ in range(2)]
    one_row = A.alloc([128], BF16)
    normw = A.alloc([256], F32)
    S32 = A.alloc([4, 256], F32)
    Sbf = [A.alloc([4, 256], BF16) for _ in range(2)]
    NB = 2
    qT = [A.alloc([4, 128], BF16) for _ in range(NB)]
    kT = [A.alloc([4, 128], BF16) for _ in range(NB)]
    ktm = [A.alloc([512], BF16) for _ in range(NB)]
    vtm = [A.alloc([1024], BF16) for _ in range(NB)]
    gtm_sb = [A.alloc([1024], BF16) for _ in range(NB)]
    gl_sb = [A.alloc([128], BF16) for _ in range(NB)]
    ofin = [A.alloc([1024], F32) for _ in range(NB)]
    sp2 = [A.alloc([512], F32) for _ in range(2)]
    eb2 = [A.alloc([512], F32) for _ in range(2)]
    enb2 = [A.alloc([512], F32) for _ in range(2)]
    erev2 = [A.alloc([512], F32) for _ in range(2)]
    qt2 = [A.alloc([4, 128], BF16) for _ in range(2)]
    kt2 = [A.alloc([4, 128], BF16) for _ in range(2)]
    khat2 = [A.alloc([512], BF16) for _ in range(2)]
    att2 = [A.alloc([4, 128], BF16) for _ in range(2)]
    qtz2 = [A.alloc([4, 2, 128], BF16) for _ in range(2)]
    tb2 = [[Buf() for _ in range(9)] for _ in range(2)]
    o_sb = A.alloc([1024], F32)
    sq = A.alloc([1024], F32)
    sg = A.alloc([1024], F32)
    ssq = A.alloc([4], F32)
    ytT = A.alloc([8, 512], BF16)
    cb_ = Buf()
    in_b = [Buf() for _ in range(NB)]
    o_b, sq_b, sg_b, ssq_b, yt_b = [Buf() for _ in range(5)]
    S32_b = [Buf() for _ in range(4)]
    Sbf_b = [[Buf() for _ in range(4)] for _ in range(2)]
    for dr in range(2):
        P.dma("pool", gkw[dr][0:16, :], c.gla_gk_w[j, dr], writes=[cb_])
        P.dma("pool", gkb[dr][0:1, :], c.gla_gk_b[j, dr].rearrange("(o n) -> o n", o=1), writes=[cb_])
    P.dma("sp", normw[:], c.gla_norm[j].partition_broadcast(128), writes=[cb_])
    P.op("dve", lambda e: e.memset(one_row[0:1, :], 1.0), writes=[cb_])
    for q_ in range(2):
        P.op("dve", lambda e, q_=q_: e.memset(qtz2[q_][:].rearrange("p h c t -> p (h c t)"), 0.0), writes=[tb2[q_][8]])
    ps_o = [(c.ps[0], c.ps_b[0]), (c.ps[1], c.ps_b[1])]
    rot = [2]

    def psn():
        i = rot[0]
        rot[0] = 2 + (i - 1) % 6
        return c.ps[i], c.ps_b[i]

    def run_tile(dr, it, ti):
        t0 = ti * 128
        p2 = it % 2
        sp, eb, enb, erev, qt, kt, khat, att, qtz = sp2[p2], eb2[p2], enb2[p2], erev2[p2], qt2[p2], kt2[p2], khat2[p2], att2[p2], qtz2[p2]
        sp_b, eb_b, enb_b, erev_b, qt_b, kt_b, khat_b, att_b, qtz_b = tb2[p2]
        b = it % NB
        ib = in_b[b]
        P.dma("sp", qT[b][:], c.qkT[0:512, t0:t0 + 128].rearrange("(h p) t -> p h t", p=128), writes=[ib])
        P.dma("sp", kT[b][:], c.qkT[512:1024, t0:t0 + 128].rearrange("(h p) t -> p h t", p=128), writes=[ib])
        P.dma("sp", ktm[b][:], c.gtm[t0:t0 + 128, 0:512], writes=[ib])
        P.dma("sp", vtm[b][:], c.gtm[t0:t0 + 128, 512:1536], writes=[ib])
        P.dma("sp", gl_sb[b][0:16, :], c.glT[dr * 16:(dr + 1) * 16, t0:t0 + 128], writes=[ib])
        if dr == 1:
            P.dma("sp", gtm_sb[b][:], c.gtm[t0:t0 + 128, 1536:2560], writes=[ib])
            P.dma("sp", ofin[b][:], c.ofwd[t0:t0 + 128, :], writes=[ib])
        ps, pb = psn()
        P.op("pe", [lambda e, ps=ps, b=b: e.matmul(ps[:], gl_sb[b][0:16, :], gkw[dr][0:16, :], start=True, stop=False),
                    lambda e, ps=ps: e.matmul(ps[:], one_row[0:1, :], gkb[dr][0:1, :], start=False, stop=True)],
             reads=[ib, cb_], writes=[pb])
        P.op("act", lambda e, ps=ps: e.activation(out=sp[:], in_=ps[:], func=AF.Exp, scale=-1.0), reads=[pb], writes=[sp_b])
        P.op("dve", lambda e: e.tensor_scalar_add(out=sp[:], in0=sp[:], scalar1=1.0), reads=[sp_b], writes=[sp_b])
        P.op("act", lambda e: e.activation(out=sp[:], in_=sp[:], func=AF.Ln), reads=[sp_b], writes=[sp_b])
        psb_, pbb_ = psn()
        P.op("pe", [lambda e, h=h, psb_=psb_: e.matmul(psb_[:, h * 128:(h + 1) * 128], sp[:, h * 128:(h + 1) * 128], Mtri[dr], start=True, stop=True)
                    for h in range(4)], reads=[sp_b, c.const_b], writes=[pbb_])
        psr, pbr = psn()
        P.op("pe", lambda e, psr=psr: e.matmul(psr[:], Mrev[dr], sp[:], start=True, stop=True), reads=[sp_b, c.const_b], writes=[pbr])
        P.op("act", lambda e, psb_=psb_: e.activation(out=eb[:], in_=psb_[:], func=AF.Exp), reads=[pbb_], writes=[eb_b])
        P.op("act", lambda e, psb_=psb_: e.activation(out=enb[:], in_=psb_[:], func=AF.Exp, scale=-1.0), reads=[pbb_], writes=[enb_b])
        P.op("act", lambda e, psr=psr: e.activation(out=erev[:], in_=psr[:], func=AF.Exp), reads=[pbr], writes=[erev_b])
        P.op("dve", lambda e, b=b: e.scalar_tensor_tensor(out=qt[:].rearrange("p h t -> p (h t)"), in0=qT[b][:].rearrange("p h t -> p (h t)"),
                                                          scalar=SCALE, in1=eb[:], op0=ALU.mult, op1=ALU.mult),
             reads=[ib, eb_b], writes=[qt_b])
        for cc_ in range(2):
            P.op("act", lambda e, cc_=cc_: e.copy(out=qtz[:, :, cc_, cc_ * 64:(cc_ + 1) * 64], in_=qt[:, :, cc_ * 64:(cc_ + 1) * 64]),
                 reads=[qt_b], writes=[qtz_b])
        P.op("dve", lambda e, b=b: e.tensor_tensor(out=kt[:].rearrange("p h t -> p (h t)"), in0=kT[b][:].rearrange("p h t -> p (h t)"),
                                                   in1=enb[:], op=ALU.mult), reads=[ib, enb_b], writes=[kt_b])
        P.op("dve", lambda e, b=b: e.tensor_tensor(out=khat[:], in0=ktm[b][:], in1=erev[:], op=ALU.mult), reads=[ib, erev_b], writes=[khat_b])
        psa, pba = psn()
        P.op("pe", [lambda e, h=h, psa=psa: e.matmul(psa[:, h * 128:(h + 1) * 128], kt[:, h, :], qt[:, h, :], start=True, stop=True) for h in range(4)],
             reads=[kt_b, qt_b], writes=[pba])
        P.op("dve", lambda e, psa=psa: e.tensor_tensor(out=att[:], in0=psa[:].rearrange("p (h t) -> p h t", h=4),
                                                       in1=mask[dr].unsqueeze(1).broadcast_to([128, 4, 128]), op=ALU.mult),
             reads=[pba, c.const_b], writes=[att_b])
        if c.debug and dr == 0 and ti == 0:
            P.dma("pool", c.dbg[:, 0:512], sp[:], reads=[sp_b])
            P.dma("pool", c.dbg[:, 512:1024], eb[:], reads=[eb_b])
            P.dma("pool", c.dbg[:, 1024:1536], erev[:], reads=[erev_b])
            P.dma("pool", c.dbg[:, 1536:2048], att[:].rearrange("p h t -> p (h t)"), reads=[att_b])
            P.dma("pool", c.dbg[:, 2048:2560], qt[:].rearrange("p h t -> p (h t)"), reads=[qt_b])
            P.dma("pool", c.dbg[:, 2560:3072], kt[:].rearrange("p h t -> p (h t)"), reads=[kt_b])
            P.dma("pool", c.dbg[:, 3072:3584], khat[:], reads=[khat_b])
            P.dma("pool", c.dbg[:, 3584:4608], qtz[:].rearrange("p h c t -> p (h c t)"), reads=[qtz_b])
        def chain():
            chunks = (0, 1) if dr == 0 else (1, 0)
            fns = []
            for h in range(4):
                po = ps_o[h // 2][0]
                oc = slice((h % 2) * 256, (h % 2 + 1) * 256)
                r0 = chunks[0] * 64
                fns.append(lambda e, po=po, oc=oc, h=h, b=b: e.matmul(po[:, oc], att[:, h, :], vtm[b][:, h * 256:(h + 1) * 256], start=(h % 2 == 0), stop=False,
                                                                      skip_group_check=True))
                fns.append(lambda e, po=po, oc=oc, h=h, c0=chunks[0]: e.matmul(po[:, oc], qtz[:, h, c0, :], Sbf[0][:, h, :], start=False, stop=False,
                                                                               skip_group_check=True))
            P.op("pe", fns, reads=[att_b, ib, qtz_b] + Sbf_b[0], writes=[ps_o[0][1], ps_o[1][1]])
            for ci, ch in enumerate(chunks):
                r0 = ch * 64
                for h in range(4):
                    if dr == 0:
                        lc = h * 128 + r0 + 63
                    else:
                        lc = h * 128 + r0
                    pss, pbs = psn()
                    P.op("pe", lambda e, pss=pss, h=h, r0=r0, b=b: e.matmul(pss[:, 0:256], khat[r0:r0 + 64, h * 128:(h + 1) * 128],
                                                                          vtm[b][r0:r0 + 64, h * 256:(h + 1) * 256], start=True, stop=True),
                         reads=[khat_b, ib], writes=[pbs])
                    P.op("dve", lambda e, pss=pss, h=h, lc=lc: e.scalar_tensor_tensor(out=S32[:, h, :], in0=S32[:, h, :], scalar=eb[:, lc:lc + 1],
                                                                                      in1=pss[:, 0:256], op0=ALU.mult, op1=ALU.add),
                         reads=[pbs, eb_b, S32_b[h]], writes=[S32_b[h]])
                    P.op("act", lambda e, h=h, ci=ci: e.copy(out=Sbf[1 - ci][:, h, :], in_=S32[:, h, :]), reads=[S32_b[h]], writes=[Sbf_b[1 - ci][h]])
                if ci == 0:
                    r1 = chunks[1] * 64
                    fns = []
                    for h in range(4):
                        po = ps_o[h // 2][0]
                        oc = slice((h % 2) * 256, (h % 2 + 1) * 256)
                        fns.append(lambda e, po=po, oc=oc, h=h, c1=chunks[1]: e.matmul(po[:, oc], qtz[:, h, c1, :], Sbf[1][:, h, :], start=False, stop=True, skip_group_check=True))
                    P.op("pe", fns, reads=[qtz_b] + Sbf_b[1], writes=[ps_o[0][1], ps_o[1][1]])
            if dr == 0:
                for k in range(2):
                    P.op("act", lambda e, k=k: e.copy(out=o_sb[:, k * 512:(k + 1) * 512], in_=ps_o[k][0][:]), reads=[ps_o[k][1]], writes=[o_b])
                P.dma("act", c.ofwd[t0:t0 + 128, :], o_sb[:], reads=[o_b])
            else:
                for k in range(2):
                    P.op("dve", lambda e, k=k, b=b: e.tensor_tensor(out=o_sb[:, k * 512:(k + 1) * 512], in0=ps_o[k][0][:], in1=ofin[b][:, k * 512:(k + 1) * 512], op=ALU.add),
                         reads=[ps_o[k][1], ib], writes=[o_b])
                P.op("dve", lambda e: e.tensor_tensor(out=sq[:], in0=o_sb[:], in1=o_sb[:], op=ALU.mult), reads=[o_b], writes=[sq_b])
                P.op("dve", lambda e: e.reduce_sum(out=ssq[:], in_=sq[:].rearrange("p (h v) -> p h v", h=4), axis=AX.X), reads=[sq_b], writes=[ssq_b])
                P.op("dve", lambda e: e.tensor_scalar(out=ssq[:], in0=ssq[:], scalar1=1.0 / 256.0, scalar2=EPS, op0=ALU.mult, op1=ALU.add), reads=[ssq_b], writes=[ssq_b])
                P.op("act", lambda e: e.activation(out=ssq[:], in_=ssq[:], func=AF.Sqrt), reads=[ssq_b], writes=[ssq_b])
                P.op("dve", lambda e: e.reciprocal(out=ssq[:], in_=ssq[:]), reads=[ssq_b], writes=[ssq_b])
                P.op("act", lambda e, b=b: e.activation(out=sg[:], in_=gtm_sb[b][:], func=AF.Silu), reads=[ib], writes=[sg_b])
                P.op("dve", lambda e: e.tensor_tensor(out=sq[:].rearrange("p (h v) -> p h v", h=4), in0=o_sb[:].rearrange("p (h v) -> p h v", h=4),
                                                      in1=ssq[:].unsqueeze(2).broadcast_to([128, 4, 256]), op=ALU.mult), reads=[o_b, ssq_b, sq_b], writes=[sq_b])
                P.op("dve", lambda e: e.tensor_tensor(out=sq[:].rearrange("p (h v) -> p h v", h=4), in0=sq[:].rearrange("p (h v) -> p h v", h=4),
                                                      in1=normw[:].unsqueeze(1).broadcast_to([128, 4, 256]), op=ALU.mult), reads=[sq_b, cb_], writes=[sq_b])
                P.op("dve", lambda e: e.tensor_tensor(out=sq[:], in0=sq[:], in1=sg[:], op=ALU.mult), reads=[sq_b, sg_b], writes=[sq_b])
                slot = ti % 4
                for g in range(2):
                    pst, pbt = psn()
                    P.op("pe", [lambda e, pst=pst, g=g, k=k: e.transpose(pst[:, k * 128:(k + 1) * 128], sq[:, (g * 4 + k) * 128:(g * 4 + k + 1) * 128], c.ident_f[:])
                                for k in range(4)], reads=[sq_b, c.const_b], writes=[pbt])
                    P.op("act", lambda e, pst=pst, g=g, slot=slot: e.copy(out=ytT[:, g * 4:(g + 1) * 4, slot * 128:(slot + 1) * 128],
                                                                         in_=pst[:].rearrange("p (k t) -> p k t", k=4)), reads=[pbt], writes=[yt_b])
                if slot == 0:
                    tb0 = (ti // 4) * 512
                    P.dma("act", c.yT[0:1024, tb0:tb0 + 512].rearrange("(k p) t -> p k t", p=128), ytT[:], reads=[yt_b])
        return chain

    for dr in range(2):
        for h in range(4):
            P.op("dve", lambda e, h=h: e.memset(S32[:, h, :], 0.0), writes=[S32_b[h]])
            P.op("dve", lambda e, h=h: e.memset(Sbf[0][:, h, :], 0.0), writes=[Sbf_b[0][h]])
        order = list(range(NT)) if dr == 0 else list(range(NT - 1, -1, -1))
        pending = None
        for it, ti in enumerate(order):
            ch = run_tile(dr, it, ti)
            if pending is not None:
                pending()
            pending = ch
        pending()
        P.barrier()
    P.barrier()


def ssd_conv_stage(c, j):
    P = c.P
    A = c.arena
    A.reset(c.arena_base)
    gam = c.gam
    G = c.gcol_ssd + j * 24
    ub = [A.alloc([L + 4], BF16) for _ in range(2)]
    acc = [A.alloc([L], BF16) for _ in range(2)]
    stg = [A.alloc([4, 128], BF16) for _ in range(3)]
    diag = [A.alloc([5, 128], BF16) for _ in range(2)]
    diag_b = [Buf() for _ in range(2)]
    ub_b = [Buf() for _ in range(2)]
    acc_b = [Buf() for _ in range(2)]
    stg_b = [Buf() for _ in range(3)]
    si = [0]

    def chunk(cc):
        b = cc % 2
        u, ubb, a, ab = ub[b], ub_b[b], acc[b], acc_b[b]
        P.op("pool", lambda e: e.memset(u[:, 0:2], 0.0), writes=[ubb])
        P.op("pool", lambda e: e.memset(u[:, L + 2:L + 4], 0.0), writes=[ubb])
        P.dma("sp", u[:, 2:L + 2], c.xbcT[cc * 128:(cc + 1) * 128, :], writes=[ubb])
        grp, kc = cc // 8, cc % 8
        wc = [(G + tap * 4 + grp) * 8 + kc for tap in range(5)]
        bc = (G + 20 + grp) * 8 + kc
        dg, dgb = diag[b], diag_b[b]
        for tap in range(5):
            P.op("dve", lambda e, tap=tap: e.tensor_scalar(out=dg[:, tap, :], in0=c.ident_f[:], scalar1=gam[:, wc[tap]:wc[tap] + 1], scalar2=None, op0=ALU.mult),
                 reads=[c.const_b], writes=[dgb])
        for tb in range(L // 512):
            ps, pb = psum_next(c)
            P.op("pe", [lambda e, tap=tap, ps=ps, tb=tb: e.matmul(ps[:], dg[:, tap, :], u[:, tb * 512 + tap:tb * 512 + tap + 512], start=(tap == 0), stop=(tap == 4))
                        for tap in range(5)], reads=[dgb, ubb], writes=[pb])
            P.op("act", lambda e, ps=ps, tb=tb: e.activation(out=a[:, tb * 512:(tb + 1) * 512], in_=ps[:], func=AF.Silu, bias=gam[:, bc:bc + 1]),
                 reads=[pb, c.const_b], writes=[ab])
        if cc >= 16:
            P.dma("act", c.bcT[(cc - 16) * 128:(cc - 15) * 128, :], a[:], reads=[ab])
        if cc < 24:
            dst = c.x_tm if cc < 16 else c.B_tm
            col0 = (cc if cc < 16 else cc - 16) * 128
            for t4 in range(L // 512):
                ps, pb = psum_next(c)
                psv = ps[:].bitcast(BF16)
                P.op("pe", [lambda e, k=k, psv=psv, t4=t4: e.transpose(psv[:, k * 128:(k + 1) * 128], a[:, (t4 * 4 + k) * 128:(t4 * 4 + k + 1) * 128], c.ident_b[:])
                            for k in range(4)], reads=[ab, c.const_b], writes=[pb])
                sg, sgb = stg[si[0]], stg_b[si[0]]
                si[0] = (si[0] + 1) % 3
                P.op("act", lambda e, sg=sg, psv=psv: e.copy(out=sg[:].rearrange("p k t -> p (k t)"), in_=psv[:, 0:512]), reads=[pb], writes=[sgb])
                P.dma("act", dst[t4 * 512:(t4 + 1) * 512, col0:col0 + 128].rearrange("(k p) c -> p k c", p=128), sg[:], reads=[sgb])

    for cc in range(32):
        chunk(cc)
    P.barrier()


def ssd_core(c, j):
    P = c.P
    A = c.arena
    A.reset(c.arena_base)
    cst = c.cst
    Minc = (cst[:, 1152:1280], cst[:, 1280:1408])
    Mexc = (cst[:, 1408:1536], cst[:, 1536:1664])
    ones_f = cst[:, 1664:1792]
    NT = L // 128
    NB = 2
    rowc = A.alloc([2, 3, 32], F32)
    dsk = A.alloc([32], F32)
    normw = A.alloc([2048], F32)
    Ub = [A.alloc([128], BF16) for _ in range(2)]
    xtm = [A.alloc([2048], BF16) for _ in range(NB)]
    Btm = [A.alloc([1024], BF16) for _ in range(NB)]
    BT = [A.alloc([8, 128], BF16) for _ in range(NB)]
    CT = [A.alloc([8, 128], BF16) for _ in range(NB)]
    dtT = [A.alloc([128], F32) for _ in range(NB)]
    ztm = [A.alloc([2048], BF16) for _ in range(NB)]
    yfin = [A.alloc([2048], F32) for _ in range(NB)]
    dt = A.alloc([32], F32)
    dta = A.alloc([32], F32)
    ed = A.alloc([96], F32)
    R = A.alloc([32, 128], BF16)
    xh = A.alloc([2048], BF16)
    xw = A.alloc([2048], BF16)
    S32 = A.alloc([8, 256], F32)
    Sbf2 = [A.alloc([8, 256], BF16) for _ in range(2)]
    E8 = [A.alloc([4, 128], F32) for _ in range(8)]
    Sc8 = [A.alloc([4, 128], BF16) for _ in range(8)]
    tmp8 = [A.alloc([256], F32) for _ in range(8)]
    CBm4 = A.alloc([4, 128], F32)
    E8_b = [Buf() for _ in range(8)]
    Sc8_b = [Buf() for _ in range(8)]
    tmp8_b = [Buf() for _ in range(8)]
    CBm4_b = Buf()
    S32_all = Buf()
    Sbf2_b = [Buf() for _ in range(2)]
    y_sb = A.alloc([2048], F32)
    sq = A.alloc([2048], F32)
    sgz = A.alloc([2048], F32)
    ssq = A.alloc([8], F32)
    ytT = A.alloc([16, 512], BF16)
    cb_ = Buf()
    in_b = [Buf() for _ in range(NB)]
    dt_b, dta_b, ed_b, R_b, xh_b, xw_b, y_b, sq_b, sgz_b, ssq_b, yt_b = [Buf() for _ in range(11)]
    E_b = [Buf() for _ in range(2)]
    Sc_b = [Buf() for _ in range(2)]
    CBm_b = [Buf() for _ in range(2)]
    tmpi_b = [Buf() for _ in range(2)]
    S32_b = [Buf() for _ in range(8)]
    Sbf_b = [Buf() for _ in range(8)]
    for dr in range(2):
        P.dma("sp", rowc[:, dr, 0, :], c.ssd_dt_bias[j, dr].partition_broadcast(128), writes=[cb_])
        P.dma("sp", rowc[:, dr, 1, :], c.ssd_a_log[j, dr].partition_broadcast(128), writes=[cb_])
    P.dma("sp", dsk[:], c.ssd_d[j].partition_broadcast(128), writes=[cb_])
    P.dma("sp", normw[:], c.ssd_norm[j].partition_broadcast(128), writes=[cb_])
    for dr in range(2):
        P.op("act", lambda e, dr=dr: e.activation(out=rowc[:, dr, 1, :], in_=rowc[:, dr, 1, :], func=AF.Exp), reads=[cb_], writes=[cb_])
        P.op("dve", lambda e, dr=dr: e.tensor_scalar(out=rowc[:, dr, 1, :], in0=rowc[:, dr, 1, :], scalar1=-1.0, scalar2=None, op0=ALU.mult),
             reads=[cb_], writes=[cb_])
        P.op("dve", lambda e, dr=dr: e.tensor_copy(out=Ub[dr][:], in_=Mexc[dr]), reads=[c.const_b], writes=[cb_])
    rot = [0]

    def psn():
        i = rot[0]
        rot[0] = (i + 1) % 8
        return c.ps[i], c.ps_b[i]

    def run_tile(dr, it, ti):
        t0 = ti * 128
        b = it % NB
        ib = in_b[b]
        P.dma("sp", xtm[b][:], c.x_tm[t0:t0 + 128, :], writes=[ib])
        P.dma("sp", Btm[b][:], c.B_tm[t0:t0 + 128, :], writes=[ib])
        P.dma("sp", BT[b][:], c.bcT[0:1024, t0:t0 + 128].rearrange("(g p) t -> p g t", p=128), writes=[ib])
        P.dma("sp", CT[b][:], c.bcT[1024:2048, t0:t0 + 128].rearrange("(g p) t -> p g t", p=128), writes=[ib])
        P.dma("sp", dtT[b][0:32, :], c.dtT[dr * 32:(dr + 1) * 32, t0:t0 + 128], writes=[ib])
        if dr == 1:
            P.dma("sp", ztm[b][:], c.z_tm[t0:t0 + 128, :], writes=[ib])
            P.dma("sp", yfin[b][:], c.yf[t0:t0 + 128, :], writes=[ib])
        ps, pb = psn()
        P.op("pe", lambda e: e.transpose(ps[:, 0:32], dtT[b][0:32, :], c.ident_f[0:32, 0:32]), reads=[ib, c.const_b], writes=[pb])
        P.op("dve", lambda e: e.tensor_tensor(out=dt[:], in0=ps[:, 0:32], in1=rowc[:, dr, 0, :], op=ALU.add), reads=[pb, cb_], writes=[dt_b])
        P.op("act", lambda e: e.activation(out=dt[:], in_=dt[:], func=AF.Exp), reads=[dt_b], writes=[dt_b])
        P.op("dve", lambda e: e.tensor_scalar_add(out=dt[:], in0=dt[:], scalar1=1.0), reads=[dt_b], writes=[dt_b])
        P.op("act", lambda e: e.activation(out=dt[:], in_=dt[:], func=AF.Ln), reads=[dt_b], writes=[dt_b])
        P.op("dve", lambda e: e.tensor_tensor(out=dta[:], in0=dt[:], in1=rowc[:, dr, 1, :], op=ALU.mult), reads=[dt_b, cb_], writes=[dta_b])
        ps2, pb2 = psn()
        P.op("pe", [lambda e: e.matmul(ps2[:, 0:32], Minc[dr], dta[:], start=True, stop=True),
                    lambda e: e.matmul(ps2[:, 32:64], Mexc[dr], dta[:], start=True, stop=True),
                    lambda e: e.matmul(ps2[:, 64:96], ones_f, dta[:], start=True, stop=True)], reads=[dta_b, c.const_b], writes=[pb2])
        P.op("act", lambda e: e.activation(out=ed[:], in_=ps2[:, 0:96], func=AF.Exp), reads=[pb2], writes=[ed_b])
        P.op("act", [lambda e, h=h: e.activation(out=R[:, h, :], in_=Minc[dr], func=AF.Copy, scale=dta[:, h:h + 1]) for h in range(32)],
             reads=[dta_b, c.const_b], writes=[R_b])
        P.op("dve", lambda e: e.tensor_tensor(out=xh[:].rearrange("p (h q) -> p h q", h=32), in0=xtm[b][:].rearrange("p (h q) -> p h q", h=32),
                                              in1=dt[:].unsqueeze(2).broadcast_to([128, 32, 64]), op=ALU.mult), reads=[ib, dt_b], writes=[xh_b])
        P.op("dve", lambda e: e.tensor_tensor(out=xw[:].rearrange("p (h q) -> p h q", h=32), in0=xh[:].rearrange("p (h q) -> p h q", h=32),
                                              in1=ed[:, 32:64].unsqueeze(2).broadcast_to([128, 32, 64]), op=ALU.mult), reads=[xh_b, ed_b], writes=[xw_b])

        par_r = it % 2
        par_w = 1 - par_r
        st_ps = []
        for g2 in range(4):
            pss, pbs = psn()
            P.op("pe", [lambda e, g=g, pss=pss: e.matmul(pss[:, (g % 2) * 256:(g % 2 + 1) * 256], Btm[b][:, g * 128:(g + 1) * 128], xw[:, g * 256:(g + 1) * 256],
                                                          start=True, stop=True) for g in (2 * g2, 2 * g2 + 1)], reads=[ib, xw_b], writes=[pbs])
            st_ps.append((pss, pbs))
        P.op("dve", lambda e: e.tensor_tensor(out=S32[:].rearrange("p g (k q) -> p (g k) q", k=4), in0=S32[:].rearrange("p g (k q) -> p (g k) q", k=4),
                                              in1=ed[:, 64:96].unsqueeze(2).broadcast_to([128, 32, 64]), op=ALU.mult), reads=[S32_all, ed_b], writes=[S32_all])
        for g2 in range(4):
            pss, pbs = st_ps[g2]
            P.op("dve", lambda e, g2=g2, pss=pss: e.tensor_tensor(out=S32[:, 2 * g2:2 * g2 + 2, :].rearrange("p g q -> p (g q)"),
                                                                  in0=S32[:, 2 * g2:2 * g2 + 2, :].rearrange("p g q -> p (g q)"), in1=pss[:], op=ALU.add),
                 reads=[S32_all, pbs], writes=[S32_all])
        P.op("act", lambda e: e.copy(out=Sbf2[par_w][:].rearrange("p g q -> p (g q)"), in_=S32[:].rearrange("p g q -> p (g q)")),
             reads=[S32_all], writes=[Sbf2_b[par_w]])

        def batch(g0):
            gs = list(range(g0, g0 + 4))
            psc, pbc = psn()
            P.op("pe", [lambda e, g=g: e.matmul(psc[:, (g - g0) * 128:(g - g0 + 1) * 128], BT[b][:, g, :], CT[b][:, g, :], start=True, stop=True) for g in gs],
                 reads=[ib], writes=[pbc])
            dps = []
            for g in gs:
                psd, pbd = psn()
                P.op("pe", lambda e, g=g, psd=psd: e.matmul(psd[:], Ub[dr][:], R[:, 4 * g:4 * g + 4, :].rearrange("p k l -> p (k l)"), start=True, stop=True),
                     reads=[cb_, R_b], writes=[pbd])
                dps.append((psd, pbd))
            P.op("dve", lambda e: e.tensor_tensor(out=CBm4[:], in0=psc[:].rearrange("p (g l) -> p g l", g=4),
                                                  in1=Minc[dr].unsqueeze(1).broadcast_to([128, 4, 128]), op=ALU.mult), reads=[pbc, c.const_b], writes=[CBm4_b])
            for i, g in enumerate(gs):
                psd, pbd = dps[i]
                P.op("act", lambda e, g=g, psd=psd: e.activation(out=E8[g][:].rearrange("p k l -> p (k l)"), in_=psd[:], func=AF.Exp), reads=[pbd], writes=[E8_b[g]])
            for i, g in enumerate(gs):
                P.op("dve", lambda e, g=g, i=i: e.tensor_tensor(out=Sc8[g][:], in0=E8[g][:], in1=CBm4[:, i, :].unsqueeze(1).broadcast_to([128, 4, 128]), op=ALU.mult),
                     reads=[E8_b[g], CBm4_b], writes=[Sc8_b[g]])
            yps = []
            for g in gs:
                psy, pby = psn()
                fns = [lambda e, k=k, g=g, psy=psy: e.matmul(psy[:, k * 64:(k + 1) * 64], Sc8[g][:, k, :], xh[:, (4 * g + k) * 64:(4 * g + k + 1) * 64], start=True, stop=True)
                       for k in range(4)]
                fns.append(lambda e, g=g, psy=psy: e.matmul(psy[:, 256:512], CT[b][:, g, :], Sbf2[par_r][:, g, :], start=True, stop=True))
                P.op("pe", fns, reads=[Sc8_b[g], xh_b, ib, Sbf2_b[par_r]], writes=[pby])
                yps.append((psy, pby))
            for i, g in enumerate(gs):
                psy, pby = yps[i]
                P.op("dve", lambda e, g=g, psy=psy: e.tensor_tensor(out=tmp8[g][:].rearrange("p (k q) -> p k q", k=4), in0=psy[:, 256:512].rearrange("p (k q) -> p k q", k=4),
                                                                    in1=ed[:, 4 * g:4 * g + 4].unsqueeze(2).broadcast_to([128, 4, 64]), op=ALU.mult),
                     reads=[pby, ed_b], writes=[tmp8_b[g]])
                P.op("dve", lambda e, g=g, psy=psy: e.tensor_tensor(out=y_sb[:, g * 256:(g + 1) * 256], in0=tmp8[g][:], in1=psy[:, 0:256], op=ALU.add),
                     reads=[pby, tmp8_b[g]], writes=[y_b])

        batch(0)
        batch(4)
        if dr == 0:
            P.dma("pool", c.yf[t0:t0 + 128, :], y_sb[:], reads=[y_b])
        else:
            P.op("dve", lambda e: e.tensor_tensor(out=y_sb[:], in0=y_sb[:], in1=yfin[b][:], op=ALU.add), reads=[y_b, ib], writes=[y_b])
            P.op("dve", lambda e: e.tensor_tensor(out=sq[:].rearrange("p (h q) -> p h q", h=32), in0=xtm[b][:].rearrange("p (h q) -> p h q", h=32),
                                                  in1=dsk[:].unsqueeze(2).broadcast_to([128, 32, 64]), op=ALU.mult), reads=[ib, cb_], writes=[sq_b])
            P.op("dve", lambda e: e.tensor_tensor(out=y_sb[:], in0=y_sb[:], in1=sq[:], op=ALU.add), reads=[y_b, sq_b], writes=[y_b])
            P.op("act", lambda e: e.activation(out=sgz[:], in_=ztm[b][:], func=AF.Silu), reads=[ib], writes=[sgz_b])
            P.op("dve", lambda e: e.tensor_tensor(out=y_sb[:], in0=y_sb[:], in1=sgz[:], op=ALU.mult), reads=[y_b, sgz_b], writes=[y_b])
            P.op("act", lambda e: e.activation(out=sq[:], in_=y_sb[:], func=AF.Square), reads=[y_b, sq_b], writes=[sq_b])
            P.op("dve", lambda e: e.reduce_sum(out=ssq[:], in_=sq[:].rearrange("p (g v) -> p g v", g=8), axis=AX.X), reads=[sq_b], writes=[ssq_b])
            P.op("dve", lambda e: e.tensor_scalar(out=ssq[:], in0=ssq[:], scalar1=1.0 / 256.0, scalar2=EPS, op0=ALU.mult, op1=ALU.add), reads=[ssq_b], writes=[ssq_b])
            P.op("act", lambda e: e.activation(out=ssq[:], in_=ssq[:], func=AF.Sqrt), reads=[ssq_b], writes=[ssq_b])
            P.op("dve", lambda e: e.reciprocal(out=ssq[:], in_=ssq[:]), reads=[ssq_b], writes=[ssq_b])
            P.op("dve", lambda e: e.tensor_tensor(out=y_sb[:].rearrange("p (g v) -> p g v", g=8), in0=y_sb[:].rearrange("p (g v) -> p g v", g=8),
                                                  in1=ssq[:].unsqueeze(2).broadcast_to([128, 8, 256]), op=ALU.mult), reads=[y_b, ssq_b], writes=[y_b])
            P.op("dve", lambda e: e.tensor_tensor(out=y_sb[:], in0=y_sb[:], in1=normw[:], op=ALU.mult), reads=[y_b, cb_], writes=[y_b])
            slot = ti % 4
            for g4 in range(4):
                pst, pbt = psn()
                P.op("pe", [lambda e, pst=pst, g4=g4, k=k: e.transpose(pst[:, k * 128:(k + 1) * 128], y_sb[:, (g4 * 4 + k) * 128:(g4 * 4 + k + 1) * 128], c.ident_f[:])
                            for k in range(4)], reads=[y_b, c.const_b], writes=[pbt])
                P.op("act", lambda e, pst=pst, g4=g4: e.copy(out=ytT[:, g4 * 4:(g4 + 1) * 4, slot * 128:(slot + 1) * 128],
                                                            in_=pst[:].rearrange("p (k t) -> p k t", k=4)), reads=[pbt], writes=[yt_b])
            if slot == 0:
                tb0 = (ti // 4) * 512
                P.dma("act", c.yT[0:2048, tb0:tb0 + 512].rearrange("(k p) t -> p k t", p=128), ytT[:], reads=[yt_b])

    for dr in range(2):
        P.op("dve", lambda e: e.memset(S32[:].rearrange("p g q -> p (g q)"), 0.0), writes=[S32_all])
        P.op("dve", lambda e: e.memset(Sbf2[0][:].rearrange("p g q -> p (g q)"), 0.0), writes=[Sbf2_b[0]])
        order = list(range(NT)) if dr == 0 else list(range(NT - 1, -1, -1))
        for it, ti in enumerate(order):
            run_tile(dr, it, ti)
        P.barrier()
    P.barrier()


def build_program(cfg):
    nc = bass.Bass("TRN2", target_bir_lowering=False)
    es = contextlib.ExitStack()
    c = Ctx()
    c.nc = nc
    P = c.P = Prog(nc, es)
    c.nlayers = cfg["nlayers"]
    c.last_layer = cfg.get("last_layer", c.nlayers - 1)
    c.skip = cfg.get("skip", ())

    def din(name, shape, dt=F32):
        return nc.dram_tensor(name, list(shape), dt, kind="ExternalInput").ap()

    def dscr(name, shape, dt):
        return nc.dram_tensor(name, list(shape), dt, kind=("ExternalOutput" if cfg.get("debug") else "Internal")).ap()

    c.x_in = din("x", [L, D])
    c.p_in = din("p", [DEPTH, L, PLE])
    c.gam_in = din("gam", [128, cfg["ngam"]])
    c.cst_in = din("cst", [128, 1792])
    c.ple_gate = din("ple_gate", [DEPTH, D, D])
    c.ple_proj = din("ple_proj", [DEPTH, PLE, D])
    c.ffn_w1 = din("ffn_w1", [DEPTH, D, DFF])
    c.ffn_w3 = din("ffn_w3", [DEPTH, D, DFF])
    c.ffn_w2 = din("ffn_w2", [DEPTH, DFF, D])
    c.test_wo = din("test_wo", [D, D])
    c.hy_in_w = din("hy_in_w", [1, D, 3072])
    c.hy_out_w = din("hy_out_w", [1, D, D])
    c.hy_f_w1 = din("hy_f_w1", [1, 33, 64])
    c.hy_f_w2 = din("hy_f_w2", [1, 64, 64])
    c.hy_f_w3 = din("hy_f_w3", [1, 64, 2048])
    c.hy_f_b1 = din("hy_f_b1", [1, 64])
    c.hy_f_b2 = din("hy_f_b2", [1, 64])
    c.hy_sin_freq = din("hy_sin_freq", [1, 64])
    c.hy_z = din("hy_z", [33, L])
    c.hy_decay = din("hy_decay", [D, L])
    c.uT = dscr("uT", [3072, L], BF16)
    c.ssd_in_w = din("ssd_in_w", [2, D, 6208])
    c.ssd_dt_bias = din("ssd_dt_bias", [2, 2, 32])
    c.ssd_a_log = din("ssd_a_log", [2, 2, 32])
    c.ssd_d = din("ssd_d", [2, 32])
    c.ssd_norm = din("ssd_norm", [2, 2048])
    c.ssd_out_w = din("ssd_out_w", [2, 2048, D])
    c.xbcT = dscr("xbcT", [4096, L], BF16)
    c.dtT = dscr("dtT", [64, L], F32)
    c.z_tm = dscr("z_tm", [L, 2048], BF16)
    c.x_tm = dscr("x_tm", [L, 2048], BF16)
    c.B_tm = dscr("B_tm", [L, 1024], BF16)
    c.bcT = dscr("bcT", [2048, L], BF16)
    c.yf = dscr("yf", [L, 2048], F32)
    c.gcol_ssd = 30
    c.gla_in_w = din("gla_in_w", [1, D, 3104])
    c.gla_gk_w = din("gla_gk_w", [1, 2, 16, 512])
    c.gla_gk_b = din("gla_gk_b", [1, 2, 512])
    c.gla_norm = din("gla_norm", [1, 256])
    c.gla_out_w = din("gla_out_w", [1, D, D])
    c.qkT = dscr("qkT", [1024, L], BF16)
    c.glT = dscr("glT", [32, L], BF16)
    c.gtm = dscr("gtm", [L, 2560], BF16)
    c.ofwd = dscr("ofwd", [L, 1024], F32)
    c.dbg = dscr("dbg", [128, 8192], F32)
    c.debug = bool(cfg.get("debug"))
    c.hy_chunks = cfg.get("hy_chunks", 8)
    c.hy_ftab = din("hy_ftab", [4, 16, 128, 2048], BF16)
    c.hy_itab = din("hy_itab", [4, 4, 128, 8192], BF16)
    c.hy_tm = dscr("hy_tm", [3, 2, 2048, 1024], BF16)
    c.hy_x0T = dscr("hy_x0T", [D, L], BF16)
    c.hy_vvT = dscr("hy_vvT", [D, L], BF16)
    c.hy_hF = dscr("hy_hF", [16, 4, 128, 1024], F32)
    c.gcol_hy = 13
    c.out = nc.dram_tensor("out", [L, D], F32, kind="ExternalOutput").ap()
    c.hT = dscr("hT", [D, L], F32)
    c.hnT = dscr("hnT", [D, L], BF16)
    c.yT = dscr("yT", [2048, L], BF16)
    c.hT_b = [Buf() for _ in range(NTT)]
    c.hnT_b = [Buf() for _ in range(NTT)]
    c.yT_b = [Buf() for _ in range(NTT)]
    c.out_b = Buf()

    c.gcol_mix, c.gcol_ffn, c.gcol_ple, c.gcol_fin = 0, 4, 8, 12

    ARENA = 188 * 1024
    c.arena = Arena(nc, es, ARENA)
    A = c.arena
    c.gam = A.alloc([cfg["ngam"]], F32)
    cst = A.alloc([1792], F32)
    c.cst = cst
    c.ident_f = cst[:, 0:128]
    c.ones_mean = A.alloc([128], BF16)
    c.ident_b = A.alloc([128], BF16)
    c.ostage = [A.alloc([512], F32) for _ in range(3)]
    c.ostage_b = [Buf() for _ in range(3)]
    c.ost_i = 0
    c.const_b = Buf()
    c.arena_base = A.off
    c.ps = [es.enter_context(nc.psum_tensor("ps%d" % i, [128, 512], F32)) for i in range(8)]
    c.ps_b = [Buf() for _ in range(8)]
    c.ps_i = 0

    P.dma("sp", c.gam[:], c.gam_in[:, :], writes=[c.const_b])
    P.dma("sp", cst[:], c.cst_in[:, :], writes=[c.const_b])
    P.op("dve", lambda e: e.tensor_copy(out=c.ones_mean[:], in_=cst[:, 128:256]), reads=[c.const_b], writes=[c.const_b])
    P.op("dve", lambda e: e.tensor_copy(out=c.ident_b[:], in_=cst[:, 0:128]), reads=[c.const_b], writes=[c.const_b])

    token_stage(c, -1, None, 0, None, None, first=True)
    kinds = cfg.get("kinds", [0, 1, 2, 0])
    for li in range(cfg["nlayers"]):
        kind = kinds[li]
        j = li // 3
        if kind == 1:
            proj_stage(c, c.hy_in_w[j], 3072, c.uT, bias_col=c.gcol_hy)
            hyena_core(c, j)
            token_stage(c, li, c.yT, 8, c.hy_out_w[j], c.gcol_hy + 16)
        elif kind == 0:
            w = c.ssd_in_w[j]
            proj_stage(c, w[:, 2048:6144], 4096, c.xbcT)
            proj_stage(c, w[:, 6144:6208], 64, c.dtT, odt=F32)
            proj_stage_tm(c, w[:, 0:2048], 2048, c.z_tm)
            ssd_conv_stage(c, j)
            ssd_core(c, j)
            token_stage(c, li, c.yT, 16, c.ssd_out_w[j], None)
        elif kind == 2:
            w = c.gla_in_w[j]
            proj_stage(c, w[:, 0:1024], 1024, c.qkT)
            proj_stage(c, w[:, 3072:3104], 32, c.glT)
            proj_stage_tm(c, w[:, 512:3072], 2560, c.gtm)
            gla_core(c, j)
            token_stage(c, li, c.yT, 8, c.gla_out_w[j], None)
        else:
            A.reset(c.arena_base)
            zt = A.alloc([TT], BF16)
            zb = Buf()
            P.op("pool", lambda e, zt=zt: e.memset(zt[:], 0.0), writes=[zb])
            for tt in range(NTT):
                for r in range(8):
                    P.dma("sp", c.yT[r * 128:(r + 1) * 128, tt * TT:(tt + 1) * TT], zt[:], reads=[zb])
            P.barrier()
            token_stage(c, li, c.yT, 8, c.test_wo, None)
    P.barrier()
    P.emit()
    es.close()
    return nc


def pack_cols(vecs):
    a = np.stack([np.asarray(v, np.float32).reshape(8, 128) for v in vecs], 0)
    return np.ascontiguousarray(a.transpose(2, 0, 1).reshape(128, -1))


def make_consts():
    cst = np.zeros((128, 1792), np.float32)
    cst[:, 0:128] = np.eye(128, dtype=np.float32)
    cst[:, 128:256] = 1.0 / D
    i = np.arange(128)[:, None]
    k = np.arange(128)[None, :]
    same = (i // 64) == (k // 64)
    cst[:, 384:512] = np.where(same & (i <= k), -1.0 / 16.0, 0.0)
    cst[:, 512:640] = np.where(same & (i >= k), -1.0 / 16.0, 0.0)
    cst[:, 640:768] = np.where(same & (i > k), -1.0 / 16.0, 0.0)
    cst[:, 768:896] = np.where(same & (i < k), -1.0 / 16.0, 0.0)
    cst[:, 896:1024] = np.where(same & (i <= k), 1.0, 0.0)
    cst[:, 1024:1152] = np.where(same & (i >= k), 1.0, 0.0)
    cst[:, 1152:1280] = np.where(i <= k, 1.0, 0.0)
    cst[:, 1280:1408] = np.where(i >= k, 1.0, 0.0)
    cst[:, 1408:1536] = np.where(i > k, 1.0, 0.0)
    cst[:, 1536:1664] = np.where(i < k, 1.0, 0.0)
    cst[:, 1664:1792] = 1.0
    return cst


def hyena_consts():
    f32 = np.float32
    t = np.linspace(0.0, 1.0, L, dtype=f32)[:, None]
    w = (2.0 * math.pi * np.arange(L, dtype=f32)[:, None] / L).astype(f32)
    bands = np.linspace(1e-4, 15, 16, dtype=f32)[None]
    z = np.concatenate([t, np.cos(bands * w), -np.sin(bands * w)], axis=-1).astype(f32)
    min_decay = math.log(1e-2) / 1.5
    max_decay = math.log(1e-2) / 0.3
    deltas = np.abs(np.linspace(min_decay, max_decay, D, dtype=f32))
    decay = np.exp(-t * deltas[None]).astype(f32)
    return np.ascontiguousarray(z.T), np.ascontiguousarray(decay.T)


def hyena_dft_tables():
    N = 2 * L
    k = (np.arange(2048, dtype=np.float64) + 0.5)[None, :]
    ftab = np.zeros((4, 16, 128, 2048), np.float32)
    itab = np.zeros((4, 4, 128, 8192), np.float32)
    for par in range(2):
        t = (2.0 * np.arange(2048, dtype=np.float64) + par)[:, None]
        th = 2.0 * np.pi * k * t / N
        for q, M in enumerate((np.cos(th), np.sin(th))):
            ftab[2 * q + par] = M.reshape(16, 128, 16, 128).transpose(2, 1, 0, 3).reshape(16, 128, 2048)
            sgn = 1.0 if q == 0 else -1.0
            MT = (sgn * 2.0 / N) * M.T
            itab[2 * par + q] = MT.reshape(16, 128, 4, 512).transpose(2, 1, 0, 3).reshape(4, 128, 8192)
    return ftab.astype(ml_dtypes.bfloat16), itab.astype(ml_dtypes.bfloat16)


def all_gam(inp):
    vecs = [inp["norm_mix"][l] for l in range(4)] + [inp["norm_ffn"][l] for l in range(4)] + [inp["norm_ple"][l] for l in range(4)] + [inp["final_norm"]]
    vecs += [inp["hy_in_b"][0][g * 1024:(g + 1) * 1024] for g in range(3)]
    vecs += [inp["hy_conv_w"][0][tap][g * 1024:(g + 1) * 1024] for tap in range(3) for g in range(3)]
    vecs += [inp["hy_conv_b"][0][g * 1024:(g + 1) * 1024] for g in range(3)]
    vecs += [inp["hy_skip"][0], inp["hy_out_b"][0]]
    for jj in range(2):
        vecs += [inp["ssd_conv_w"][jj][tap][g * 1024:(g + 1) * 1024] for tap in range(5) for g in range(4)]
        vecs += [inp["ssd_conv_b"][jj][g * 1024:(g + 1) * 1024] for g in range(4)]
    return pack_cols(vecs)


_W_KEYS = ("ssd_in_w", "ssd_dt_bias", "ssd_a_log", "ssd_d", "ssd_norm", "ssd_out_w", "gla_in_w", "gla_gk_w", "gla_gk_b", "gla_norm", "gla_out_w", "ple_gate", "ple_proj", "ffn_w1", "ffn_w3", "ffn_w2", "hy_in_w", "hy_out_w", "hy_f_w1", "hy_f_w2", "hy_f_w3",
           "hy_f_b1", "hy_f_b2", "hy_sin_freq")


def kernel(**inputs):
    inp = {k: np.asarray(v) for k, v in inputs.items()}
    gam = all_gam(inp)
    cfg = {"ngam": gam.shape[1], "nlayers": DEPTH}
    nc = build_program(cfg)
    hz, hd = hyena_consts()
    ftab, itab = hyena_dft_tables()
    cst = make_consts()
    zero_wo = np.zeros((D, D), np.float32)
    in_maps = []
    for b in range(8):
        d = {"x": np.ascontiguousarray(inp["x"][b], dtype=np.float32), "p": np.ascontiguousarray(inp["p"][:, b], dtype=np.float32),
             "gam": gam, "cst": cst, "test_wo": zero_wo, "hy_z": hz, "hy_decay": hd, "hy_ftab": ftab, "hy_itab": itab}
        for k in _W_KEYS:
            d[k] = np.ascontiguousarray(inp[k], dtype=np.float32)
        in_maps.append(d)
    res = run_bass_kernel_spmd(nc, in_maps, core_ids=list(range(8)))
    return np.stack([np.asarray(r["out"], dtype=np.float32) for r in res.results], 0)
```

```python
import contextlib
import math
import numpy as np
import ml_dtypes
import concourse.bass as bass
import concourse.mybir as mybir
from concourse.bass_utils import run_bass_kernel_spmd

F32 = mybir.dt.float32
BF16 = mybir.dt.bfloat16
AF = mybir.ActivationFunctionType
ALU = mybir.AluOpType
AX = mybir.AxisListType

D = 1024
L = 4096
DEPTH = 4
DFF = 2816
NFC = DFF // 128
PLE = 256
EPS = 1e-6
TT = 1024
NTT = L // TT


class Buf:
    __slots__ = ("w", "r", "name")

    def __init__(self, name=""):
        self.w = []
        self.r = []
        self.name = name


class Prog:
    ENG = ("pe", "act", "dve", "pool", "sp")
    ROT = 16000
    NDS = 10

    def __init__(self, nc, es):
        self.nc = nc
        self.es = es
        self.q = {e: [] for e in self.ENG}
        self.cnt = {e: 0 for e in self.ENG}
        self.nsem = 0
        self.cur = {e: self._newsem() for e in self.ENG}
        self.seen = {e: {} for e in self.ENG}
        self.dsem = {e: [] for e in self.ENG}
        self.dnext = {e: 0 for e in self.ENG}
        self.ninst = 0

    def _newsem(self):
        self.nsem += 1
        return self.es.enter_context(self.nc.semaphore("s%d" % self.nsem))

    def _wait(self, e, sem, val):
        k = id(sem)
        if self.seen[e].get(k, 0) < val:
            self.seen[e][k] = val
            self.q[e].append(("w", sem, val))

    def _waits(self, e, reads, writes, is_dma=False):
        need = {}

        def add(t, kind):
            sem, val, eng = t
            if eng == e and (e == "pe" or kind != "raw"):
                return
            if is_dma and kind == "waw" and eng == "dma":
                return
            k = id(sem)
            if k not in need or need[k][1] < val:
                need[k] = (sem, val)

        for b in reads:
            for t in b.w:
                add(t, "raw")
        for b in writes:
            for t in b.w:
                add(t, "waw")
            for t in b.r:
                add(t, "war")
        for sem, val in need.values():
            self._wait(e, sem, val)

    def _mark(self, t, reads, writes):
        for b in reads:
            if t[2] != "dma":
                b.r = [x for x in b.r if x[2] != t[2]]
            b.r.append(t)
        for b in writes:
            if t[2] == "dma" and not b.r and b.w and all(x[2] == "dma" for x in b.w):
                b.w = b.w + [t]
            else:
                b.w = [t]
            b.r = []

    def op(self, e, fn, reads=(), writes=()):
        fns = fn if isinstance(fn, (list, tuple)) else [fn]
        self._waits(e, reads, writes)
        self.cnt[e] += 1
        if self.cnt[e] > self.ROT:
            self.cur[e] = self._newsem()
            self.cnt[e] = 1
        t = (self.cur[e], self.cnt[e], e)
        for f in fns[:-1]:
            self.q[e].append(("i", f, None))
        self.q[e].append(("i", fns[-1], t))
        self.ninst += len(fns)
        self._mark(t, reads, writes)
        return t

    def dma(self, qe, out, in_, reads=(), writes=()):
        lst = self.dsem[qe]
        i = self.dnext[qe]
        self.dnext[qe] = (i + 1) % self.NDS
        if len(lst) <= i:
            lst.append([self._newsem(), 0])
        s = lst[i]
        if s[1] >= self.ROT:
            self._wait(qe, s[0], s[1])
            s[0] = self._newsem()
            s[1] = 0
        self._waits(qe, reads, writes, is_dma=True)
        if s[1] > 0:
            self._wait(qe, s[0], s[1])
        s[1] += 16
        t = (s[0], s[1], "dma")
        self.q[qe].append(("d", out, in_, t))
        self.ninst += 1
        self._mark(t, reads, writes)
        return t

    def barrier(self):
        ticks = [(self.cur[e], self.cnt[e]) for e in self.ENG if self.cnt[e] > 0]
        for e in self.ENG:
            for s in self.dsem[e]:
                if s[1] > 0:
                    ticks.append((s[0], s[1]))
        for e in self.ENG:
            for sem, val in ticks:
                self._wait(e, sem, val)

    def emit(self):
        nc = self.nc

        def run(eng, items):
            for it in items:
                if it[0] == "w":
                    eng.wait_ge(it[1], it[2])
                elif it[0] == "i":
                    ins = it[1](eng)
                    if it[2] is not None:
                        ins.then_inc(it[2][0], 1)
                else:
                    eng.dma_start(out=it[1], in_=it[2]).then_inc(it[3][0], 16)

        with nc.Block() as block:
            @block.tensor
            def _(e):
                run(e, self.q["pe"])

            @block.scalar
            def _(e):
                run(e, self.q["act"])

            @block.vector
            def _(e):
                run(e, self.q["dve"])

            @block.gpsimd
            def _(e):
                run(e, self.q["pool"])

            @block.sync
            def _(e):
                run(e, self.q["sp"])


class Arena:
    def __init__(self, nc, es, nbytes):
        self.t = es.enter_context(nc.sbuf_tensor("arena", [128, nbytes // 2], BF16))
        self.size = nbytes
        self.off = 0

    def reset(self, off=0):
        self.off = off

    def alloc(self, free_shape, dt, parts=128):
        n = int(np.prod(free_shape))
        esz = 4 if dt == F32 else 2
        nb = (n * esz + 63) // 64 * 64
        assert self.off + nb <= self.size, ("arena overflow", self.off, nb, self.size)
        a = self.t[0:parts, self.off // 2:(self.off + n * esz) // 2]
        self.off += nb
        if dt == F32:
            a = a.bitcast(F32)
        if len(free_shape) == 2:
            a = a.rearrange("p (a b) -> p a b", b=free_shape[1])
        elif len(free_shape) == 3:
            a = a.rearrange("p (a b c) -> p a b c", b=free_shape[1], c=free_shape[2])
        return a


class Ctx:
    pass


def psum_next(c):
    i = c.ps_i
    c.ps_i = (i + 1) % len(c.ps)
    return c.ps[i], c.ps_b[i]


def ring_load(c, src_ap, kc, ncols, srcbufs=()):
    P = c.P
    i = c.ring_i
    c.ring_i = (i + 1) % len(c.ring)
    dst = c.ring[i][:, 0:kc * ncols].rearrange("p (a b) -> p a b", b=ncols)
    P.dma("pool", dst, src_ap, reads=srcbufs, writes=[c.ring_b[i]])
    return dst, c.ring_b[i]


def wview(w_ap, r0, nrows, c0, ncols):
    return w_ap[r0:r0 + nrows, c0:c0 + ncols].rearrange("(kc p) n -> p kc n", p=128)


def rmsnorm_tile(c, h_sb, h_bufs, gcol, out_sb, out_bufs, out_f32=False):
    P = c.P
    sq, rstd, ones_mean, gam = c.sq, c.rstd, c.ones_mean, c.gam
    sq_hb = c.sq_hb
    for hf in range(TT // 512):
        P.op("act", lambda e, hf=hf: e.activation(out=sq[:, :, hf * 512:(hf + 1) * 512], in_=h_sb[:, :, hf * 512:(hf + 1) * 512], func=AF.Square),
             reads=h_bufs, writes=[sq_hb[hf]])
    for hf in range(TT // 512):
        ps, pb = psum_next(c)
        fns = []
        for kc in range(8):
            fns.append(lambda e, kc=kc, ps=ps, hf=hf: e.matmul(ps[:], ones_mean[:], sq[:, kc, hf * 512:(hf + 1) * 512],
                                                              start=(kc == 0), stop=(kc == 7)))
        P.op("pe", fns, reads=[sq_hb[hf], c.const_b], writes=[pb])
        rs = rstd[:, hf * 512:(hf + 1) * 512]
        P.op("dve", lambda e, ps=ps, rs=rs: e.tensor_scalar_add(out=rs, in0=ps[:], scalar1=EPS), reads=[pb], writes=[c.rstd_b[hf]])
        P.op("act", lambda e, rs=rs: e.activation(out=rs, in_=rs, func=AF.Sqrt), reads=[c.rstd_b[hf]], writes=[c.rstd_b[hf]])
        P.op("dve", lambda e, rs=rs: e.reciprocal(out=rs, in_=rs), reads=[c.rstd_b[hf]], writes=[c.rstd_b[hf]])
    for kc in range(8):
        P.op("dve", lambda e, kc=kc: e.scalar_tensor_tensor(out=out_sb[:, kc, :], in0=h_sb[:, kc, :],
                                                            scalar=gam[:, gcol * 8 + kc:gcol * 8 + kc + 1], in1=rstd[:],
                                                            op0=ALU.mult, op1=ALU.mult),
             reads=[h_bufs[kc], c.const_b] + c.rstd_b,
             writes=(out_bufs[kc] if isinstance(out_bufs[kc], list) else [out_bufs[kc]]))


def linear_fm(c, w_ap, K, N, x_sb, x_bufs, evac, col_piece=None):
    P = c.P
    KC = K // 128
    if col_piece is None:
        col_piece = min(N, 4096 // KC)
    for c0 in range(0, N, col_piece):
        ncol = min(col_piece, N - c0)
        wsb, wb = ring_load(c, wview(w_ap, 0, K, c0, ncol), KC, ncol)
        for j in range((ncol + 127) // 128):
            dc = (c0 // 128) + j
            wd = min(128, ncol - j * 128)
            for hf in range(TT // 512):
                ps, pb = psum_next(c)
                fns = []
                for kc in range(KC):
                    fns.append(lambda e, kc=kc, ps=ps, hf=hf, j=j, wsb=wsb, wd=wd: e.matmul(
                        ps[0:wd, :], wsb[:, kc, j * 128:j * 128 + wd], x_sb[:, kc, hf * 512:(hf + 1) * 512],
                        start=(kc == 0), stop=(kc == KC - 1)))
                P.op("pe", fns, reads=[wb] + list(x_bufs[:KC]), writes=[pb])
                evac(dc, hf, ps, pb)


def token_stage(c, li, yT, CK, wo_ap, bo_col, first=False):
    P = c.P
    A = c.arena
    A.reset(c.arena_base)
    h_sb = A.alloc([8, TT], F32)
    hn_sb = A.alloc([8, TT], BF16)
    big_flat = A.alloc([16 * TT], BF16)
    big = big_flat.rearrange("p (a b) -> p a b", b=TT)
    big_f32 = big_flat.bitcast(F32).rearrange("p (a b) -> p a b", b=TT)
    c.sq = A.alloc([8, TT], BF16)
    c.rstd = A.alloc([TT], F32)
    tmp = [A.alloc([512], F32) for _ in range(3)]
    ptm = A.alloc([8, PLE], F32)
    pT = A.alloc([2, TT], BF16)
    c.ring = [A.alloc([4096], BF16) for _ in range(5)]
    h_b = [Buf() for _ in range(8)]
    hn_b = [Buf() for _ in range(8)]
    big_b = [Buf() for _ in range(16)]
    c.sq_b = Buf()
    c.sq_hb = [Buf() for _ in range(TT // 512)]
    c.rstd_b = [Buf() for _ in range(TT // 512)]
    tmp_b = [Buf() for _ in range(3)]
    ptm_b = Buf()
    pT_b = Buf()
    c.ring_b = [Buf() for _ in c.ring]
    c.ring_i = 0
    tmp_i = [0]
    xs_b = Buf()

    def next_tmp():
        i = tmp_i[0]
        tmp_i[0] = (i + 1) % 3
        return tmp[i], tmp_b[i]

    last = (li == c.last_layer) and not first
    skip = c.skip

    def load_Y(tt_):
        P.dma("sp", big[:, 0:CK, :], yT[0:CK * 128, tt_ * TT:(tt_ + 1) * TT].rearrange("(c p) t -> p c t", p=128),
              reads=[c.yT_b[tt_]], writes=big_b[:CK])

    def load_p(tt_):
        P.dma("sp", ptm[:], c.p_in[li, tt_ * TT:(tt_ + 1) * TT, :].rearrange("(s p) d -> p s d", p=128), writes=[ptm_b])

    for tt in range(NTT):
        t0 = tt * TT
        if first:
            x_sb = big_f32
            P.dma("sp", x_sb[:, :, :], c.x_in[t0:t0 + TT, :].rearrange("(s p) d -> p s d", p=128), writes=[xs_b])
            for hh in range(2):
                for dc in range(8):
                    ps, pb = psum_next(c)
                    fns = [lambda e, s=s, dc=dc, ps=ps, hh=hh: e.transpose(ps[:, s * 128:(s + 1) * 128], x_sb[:, hh * 4 + s, dc * 128:(dc + 1) * 128], c.ident_f[:])
                           for s in range(4)]
                    P.op("pe", fns, reads=[xs_b, c.const_b], writes=[pb])
                    P.op("act", lambda e, ps=ps, dc=dc, hh=hh: e.copy(out=h_sb[:, dc, hh * 512:(hh + 1) * 512], in_=ps[:]),
                         reads=[pb], writes=[h_b[dc]])
        else:
            P.dma("sp", h_sb[:], c.hT[:, t0:t0 + TT].rearrange("(c p) t -> p c t", p=128), reads=[c.hT_b[tt]], writes=h_b)
            if tt == 0 or last:
                load_Y(tt)
                load_p(tt)

            def ev_out(dc, hf, ps, pb):
                sl = slice(hf * 512, (hf + 1) * 512)
                if bo_col is None:
                    P.op("dve", lambda e: e.tensor_tensor(out=h_sb[:, dc, sl], in0=h_sb[:, dc, sl], in1=ps[:], op=ALU.add),
                         reads=[pb, h_b[dc]], writes=[h_b[dc]])
                else:
                    P.op("dve", lambda e: e.scalar_tensor_tensor(out=h_sb[:, dc, sl], in0=ps[:], scalar=c.gam[:, bo_col * 8 + dc:bo_col * 8 + dc + 1],
                                                                 in1=h_sb[:, dc, sl], op0=ALU.add, op1=ALU.add),
                         reads=[pb, h_b[dc], c.const_b], writes=[h_b[dc]])
            if "outproj" not in skip:
                linear_fm(c, wo_ap, CK * 128, D, big, big_b, ev_out)

            rmsnorm_tile(c, h_sb, h_b, c.gcol_ffn + li, hn_sb, hn_b)
            w1 = c.ffn_w1[li]
            w3 = c.ffn_w3[li]
            w2 = c.ffn_w2[li]
            for fh in range(0 if "ffn" not in skip else 2, 2):
                f_base = fh * 11
                groups = [(0, 4), (4, 8), (8, 11)]
                for (ga, gb) in groups:
                    n = gb - ga
                    w1s, w1b = ring_load(c, wview(w1, 0, D, (f_base + ga) * 128, n * 128), 8, n * 128)
                    w3s, w3b = ring_load(c, wview(w3, 0, D, (f_base + ga) * 128, n * 128), 8, n * 128)
                    for j in range(n):
                        fl = ga + j
                        for hf in range(TT // 512):
                            sl = slice(hf * 512, (hf + 1) * 512)
                            psa, pba = psum_next(c)
                            psb, pbb = psum_next(c)
                            P.op("pe", [lambda e, kc=kc, psa=psa, j=j, sl=sl, w1s=w1s: e.matmul(psa[:], w1s[:, kc, j * 128:(j + 1) * 128], hn_sb[:, kc, sl],
                                                                                               start=(kc == 0), stop=(kc == 7)) for kc in range(8)],
                                 reads=[w1b] + hn_b, writes=[pba])
                            P.op("pe", [lambda e, kc=kc, psb=psb, j=j, sl=sl, w3s=w3s: e.matmul(psb[:], w3s[:, kc, j * 128:(j + 1) * 128], hn_sb[:, kc, sl],
                                                                                               start=(kc == 0), stop=(kc == 7)) for kc in range(8)],
                                 reads=[w3b] + hn_b, writes=[pbb])
                            tm, tmb = next_tmp()
                            P.op("act", lambda e, tm=tm, psa=psa: e.activation(out=tm[:], in_=psa[:], func=AF.Silu), reads=[pba], writes=[tmb])
                            P.op("dve", lambda e, tm=tm, psb=psb, fl=fl, sl=sl: e.tensor_tensor(out=big[:, fl, sl], in0=tm[:], in1=psb[:], op=ALU.mult),
                                 reads=[tmb, pbb], writes=[big_b[fl]])
                w2p = []
                for (ga, gb) in groups:
                    n = gb - ga
                    w2p.append(ring_load(c, wview(w2, (f_base + ga) * 128, n * 128, 0, D), n, D) + (ga, gb))
                for dc in range(8):
                    for hf in range(TT // 512):
                        sl = slice(hf * 512, (hf + 1) * 512)
                        ps, pb = psum_next(c)
                        fns = []
                        for (ws, wb, ga, gb) in w2p:
                            for j in range(gb - ga):
                                fl = ga + j
                                fns.append(lambda e, ws=ws, j=j, fl=fl, ps=ps, sl=sl, dc=dc: e.matmul(
                                    ps[:], ws[:, j, dc * 128:(dc + 1) * 128], big[:, fl, sl], start=(fl == 0), stop=(fl == 10)))
                        P.op("pe", fns, reads=[x[1] for x in w2p] + big_b[:11], writes=[pb])
                        P.op("dve", lambda e, ps=ps, dc=dc, sl=sl: e.tensor_tensor(out=h_sb[:, dc, sl], in0=h_sb[:, dc, sl], in1=ps[:], op=ALU.add),
                             reads=[pb, h_b[dc]], writes=[h_b[dc]])

            if not last and tt + 1 < NTT:
                load_Y(tt + 1)
            rmsnorm_tile(c, h_sb, h_b, c.gcol_ple + li, hn_sb, hn_b)
            for kc in range(2):
                for hf in range(TT // 512):
                    ps, pb = psum_next(c)
                    fns = [lambda e, s=s, kc=kc, hf=hf, ps=ps: e.transpose(ps[:, s * 128:(s + 1) * 128], ptm[:, hf * 4 + s, kc * 128:(kc + 1) * 128], c.ident_f[:])
                           for s in range(4)]
                    P.op("pe", fns, reads=[ptm_b, c.const_b], writes=[pb])
                    P.op("act", lambda e, ps=ps, kc=kc, hf=hf: e.copy(out=pT[:, kc, hf * 512:(hf + 1) * 512], in_=ps[:]),
                         reads=[pb], writes=[pT_b])
            if not last and tt + 1 < NTT:
                load_p(tt + 1)
            gate_t = {}
            wp_sb, wp_b = ring_load(c, wview(c.ple_proj[li], 0, PLE, 0, D), 2, D)

            def ev_gate(dc, hf, ps, pb, wp_sb=wp_sb, wp_b=wp_b):
                tm, tmb = next_tmp()
                P.op("act", lambda e: e.activation(out=tm[:], in_=ps[:], func=AF.Sigmoid), reads=[pb], writes=[tmb])
                gate_t[(dc, hf)] = (tm, tmb)
                ps2, pb2 = psum_next(c)
                sl = slice(hf * 512, (hf + 1) * 512)
                P.op("pe", [lambda e, kc=kc: e.matmul(ps2[:], wp_sb[:, kc, dc * 128:(dc + 1) * 128], pT[:, kc, sl], start=(kc == 0), stop=(kc == 1))
                            for kc in range(2)], reads=[wp_b, pT_b], writes=[pb2])
                P.op("dve", lambda e: e.tensor_tensor(out=tm[:], in0=tm[:], in1=ps2[:], op=ALU.mult), reads=[tmb, pb2], writes=[tmb])
                P.op("dve", lambda e: e.tensor_tensor(out=h_sb[:, dc, sl], in0=h_sb[:, dc, sl], in1=tm[:], op=ALU.add),
                     reads=[tmb, h_b[dc]], writes=[h_b[dc]])
            if "ple" not in skip:
                linear_fm(c, c.ple_gate[li], D, D, hn_sb, hn_b, ev_gate)

        if not last:
            P.dma("act", c.hT[:, t0:t0 + TT].rearrange("(c p) t -> p c t", p=128), h_sb[:], reads=h_b, writes=[c.hT_b[tt]])
            nl = 0 if first else li + 1
            rmsnorm_tile(c, h_sb, h_b, c.gcol_mix + nl, hn_sb, hn_b)
            P.dma("act", c.hnT[:, t0:t0 + TT].rearrange("(c p) t -> p c t", p=128), hn_sb[:], reads=hn_b, writes=[c.hnT_b[tt]])
        else:
            onf = big_f32
            fin_b = [[big_b[2 * k], big_b[2 * k + 1]] for k in range(8)]
            rmsnorm_tile(c, h_sb, h_b, c.gcol_fin, onf, fin_b, out_f32=True)
            for s in range(TT // 128):
                for dg in range(2):
                    ps, pb = psum_next(c)
                    fns = [lambda e, s=s, dg=dg, j=j, ps=ps: e.transpose(ps[:, j * 128:(j + 1) * 128], onf[:, dg * 4 + j, s * 128:(s + 1) * 128], c.ident_f[:])
                           for j in range(4)]
                    P.op("pe", fns, reads=big_b + [c.const_b], writes=[pb])
                    ob, obb = c.ostage[c.ost_i], c.ostage_b[c.ost_i]
                    c.ost_i = (c.ost_i + 1) % len(c.ostage)
                    P.op("act", lambda e, ps=ps, ob=ob: e.copy(out=ob[:], in_=ps[:]), reads=[pb], writes=[obb])
                    P.dma("act", c.out[t0 + s * 128:t0 + (s + 1) * 128, dg * 512:(dg + 1) * 512], ob[:], reads=[obb], writes=[c.out_b])
    P.barrier()


def proj_stage(c, w_ap, N, out_dram, bias_col=None, odt=BF16):
    P = c.P
    A = c.arena
    A.reset(c.arena_base)
    hn_sb = A.alloc([8, TT], BF16)
    hn_b = [Buf() for _ in range(8)]
    stg = [A.alloc([TT], odt) for _ in range(3)]
    stg_b = [Buf() for _ in range(3)]
    c.ring = [A.alloc([4096], BF16) for _ in range(5)]
    c.ring_b = [Buf() for _ in c.ring]
    c.ring_i = 0
    gam = c.gam
    for tt in range(NTT):
        t0 = tt * TT
        P.dma("sp", hn_sb[:], c.hnT[:, t0:t0 + TT].rearrange("(c p) t -> p c t", p=128), writes=hn_b)

        def evac(dc, hf, ps, pb, t0=t0):
            wd = min(128, N - dc * 128)
            sg, sgb = stg[dc % 3], stg_b[dc % 3]
            sl = slice(hf * 512, (hf + 1) * 512)
            if bias_col is None:
                P.op("act", lambda e: e.copy(out=sg[0:wd, sl], in_=ps[0:wd, :]), reads=[pb], writes=[sgb])
            else:
                col = bias_col * 8 + dc
                P.op("act", lambda e: e.activation(out=sg[0:wd, sl], in_=ps[0:wd, :], func=AF.Identity, bias=gam[0:wd, col:col + 1]),
                     reads=[pb, c.const_b], writes=[sgb])
            if hf == TT // 512 - 1:
                P.dma("act", out_dram[dc * 128:dc * 128 + wd, t0:t0 + TT], sg[0:wd, :], reads=[sgb])
        linear_fm(c, w_ap, D, N, hn_sb, hn_b, evac)
    P.barrier()


def proj_stage_tm(c, w_ap, N, out_dram):
    P = c.P
    A = c.arena
    A.reset(c.arena_base)
    hn_sb = A.alloc([8, TT], BF16)
    hn_b = [Buf() for _ in range(8)]
    stg = [A.alloc([512], BF16) for _ in range(4)]
    stg_b = [Buf() for _ in range(4)]
    c.ring = [A.alloc([4096], BF16) for _ in range(5)]
    c.ring_b = [Buf() for _ in c.ring]
    c.ring_i = 0
    si = 0
    for tt in range(NTT):
        t0 = tt * TT
        P.dma("sp", hn_sb[:], c.hnT[:, t0:t0 + TT].rearrange("(c p) t -> p c t", p=128), writes=hn_b)
        for n0 in range(0, N, 512):
            wsb, wb = ring_load(c, wview(w_ap, 0, D, n0, 512), 8, 512)
            for sub in range(TT // 128):
                ps, pb = psum_next(c)
                P.op("pe", [lambda e, kc=kc, ps=ps, sub=sub, wsb=wsb: e.matmul(ps[:], hn_sb[:, kc, sub * 128:(sub + 1) * 128], wsb[:, kc, :],
                                                                              start=(kc == 0), stop=(kc == 7)) for kc in range(8)],
                     reads=[wb] + hn_b, writes=[pb])
                sg, sgb = stg[si], stg_b[si]
                si = (si + 1) % 4
                P.op("act", lambda e, sg=sg, ps=ps: e.copy(out=sg[:], in_=ps[:]), reads=[pb], writes=[sgb])
                P.dma("act", out_dram[t0 + sub * 128:t0 + (sub + 1) * 128, n0:n0 + 512], sg[:], reads=[sgb])
    P.barrier()


def hyena_core(c, j):
    P = c.P
    A = c.arena
    A.reset(c.arena_base)
    gam = c.gam
    G = c.gcol_hy
    PI = math.pi
    hid2 = A.alloc([L], F32)
    mark = A.off
    zT = A.alloc([L], F32)
    hid1 = A.alloc([L], F32)
    w1 = A.alloc([64], F32)
    w2 = A.alloc([64], F32)
    sc = A.alloc([8], F32)
    mtmp = A.alloc([512], F32)
    mt_b = Buf()
    MAGIC = 12582912.0
    mb = Buf()
    P.dma("sp", zT[0:33, :], c.hy_z[:, :], writes=[mb])
    P.dma("sp", w1[0:33, :], c.hy_f_w1[j], writes=[mb])
    P.dma("sp", w2[0:64, :], c.hy_f_w2[j], writes=[mb])
    P.dma("sp", sc[0:64, 0:1], c.hy_sin_freq[j].rearrange("(p o) -> p o", o=1), writes=[mb])
    P.dma("sp", sc[0:64, 1:2], c.hy_f_b1[j].rearrange("(p o) -> p o", o=1), writes=[mb])
    P.dma("sp", sc[0:64, 2:3], c.hy_f_b2[j].rearrange("(p o) -> p o", o=1), writes=[mb])
    P.barrier()
    P.op("dve", lambda e: e.tensor_tensor(out=sc[0:64, 3:4], in0=sc[0:64, 1:2], in1=sc[0:64, 0:1], op=ALU.mult), reads=[mb], writes=[mb])
    P.op("dve", lambda e: e.tensor_tensor(out=sc[0:64, 4:5], in0=sc[0:64, 2:3], in1=sc[0:64, 0:1], op=ALU.mult), reads=[mb], writes=[mb])
    P.op("dve", lambda e: e.memset(sc[0:64, 5:6], -PI), reads=[mb], writes=[mb])
    hb1 = Buf()
    hb2 = Buf()
    for (wsb, kk, src, dst, fbcol, sb_, db_) in ((w1, 33, zT, hid1, 3, mb, hb1), (w2, 64, hid1, hid2, 4, hb1, hb2)):
        for tb in range(8):
            sl = slice(tb * 512, (tb + 1) * 512)
            ps, pb = psum_next(c)
            P.op("pe", lambda e, ps=ps, wsb=wsb, kk=kk, src=src, sl=sl: e.matmul(ps[0:64, :], wsb[0:kk, 0:64], src[0:kk, sl], start=True, stop=True),
                 reads=[mb, sb_], writes=[pb])
            P.op("dve", lambda e, ps=ps, dst=dst, sl=sl, fbcol=fbcol: e.tensor_scalar(out=dst[0:64, sl], in0=ps[0:64, :], scalar1=sc[0:64, 0:1],
                                                                                    scalar2=sc[0:64, fbcol:fbcol + 1], op0=ALU.mult, op1=ALU.add),
                 reads=[pb, mb], writes=[db_])
            P.op("dve", lambda e, dst=dst, sl=sl: e.tensor_scalar(out=mtmp[0:64, :], in0=dst[0:64, sl], scalar1=1.0 / (2.0 * PI), scalar2=MAGIC,
                                                                 op0=ALU.mult, op1=ALU.add), reads=[db_], writes=[mt_b])
            P.op("dve", lambda e: e.tensor_scalar(out=mtmp[0:64, :], in0=mtmp[0:64, :], scalar1=-MAGIC, scalar2=-2.0 * PI,
                                                  op0=ALU.add, op1=ALU.mult), reads=[mt_b], writes=[mt_b])
            P.op("dve", lambda e, dst=dst, sl=sl: e.tensor_tensor(out=dst[0:64, sl], in0=dst[0:64, sl], in1=mtmp[0:64, :], op=ALU.add),
                 reads=[mt_b, db_], writes=[db_])
            P.op("act", lambda e, dst=dst, sl=sl: e.activation(out=dst[0:64, sl], in_=dst[0:64, sl], func=AF.Sin),
                 reads=[db_], writes=[db_])
    P.barrier()
    A.reset(mark)
    w3 = A.alloc([2048], F32)
    hf = A.alloc([L], F32)
    hb = A.alloc([L], F32)
    hs = A.alloc([L], F32)
    dstg = [A.alloc([512], F32) for _ in range(2)]
    ub = [A.alloc([L + 2], BF16) for _ in range(3)]
    x0c = A.alloc([L], F32)
    x1c = A.alloc([L], F32)
    vv = A.alloc([L], F32)
    tstg = [A.alloc([4, 128], BF16) for _ in range(3)]
    w3_b, hf_b, hb_b, hs_b, x0_b, x1_b, vv_b = [Buf() for _ in range(7)]
    dstg_b = [Buf() for _ in range(2)]
    ub_b = [Buf() for _ in range(3)]
    tstg_b = [Buf() for _ in range(3)]
    tsi = [0]
    P.dma("sp", w3[0:64, :], c.hy_f_w3[j], writes=[w3_b])
    for k in range(3):
        P.op("dve", lambda e, k=k: e.memset(ub[k][:, 0:1], 0.0), writes=[ub_b[k]])
        P.op("dve", lambda e, k=k: e.memset(ub[k][:, L + 1:L + 2], 0.0), writes=[ub_b[k]])

    def tm_store(src, src_b, kind, cc):
        sv = src[:].rearrange("p (t two) -> p t two", two=2)
        for par in range(2):
            for t4 in range(4):
                ps, pb = psum_next(c)
                P.op("pe", [lambda e, k=k, ps=ps, t4=t4, par=par: e.transpose(ps[:, k * 128:(k + 1) * 128], sv[:, (t4 * 4 + k) * 128:(t4 * 4 + k + 1) * 128, par], c.ident_f[:])
                            for k in range(4)], reads=[src_b, c.const_b], writes=[pb])
                sg, sgb = tstg[tsi[0]], tstg_b[tsi[0]]
                tsi[0] = (tsi[0] + 1) % 3
                P.op("act", lambda e, sg=sg, ps=ps: e.copy(out=sg[:].rearrange("p k t -> p (k t)"), in_=ps[:]), reads=[pb], writes=[sgb])
                P.dma("act", c.hy_tm[kind, par, t4 * 512:(t4 + 1) * 512, cc * 128:(cc + 1) * 128].rearrange("(k p) c -> p k c", p=128), sg[:], reads=[sgb])

    def prep(cc):
        P.dma("sp", x1c[:], c.hy_decay[cc * 128:(cc + 1) * 128, :], writes=[x1_b])
        for (dirn, hsb, hbuf) in ((0, hf, hf_b), (1, hb, hb_b)):
            for tb in range(8):
                sl = slice(tb * 512, (tb + 1) * 512)
                ps, pb = psum_next(c)
                c0 = dirn * 1024 + cc * 128
                P.op("pe", lambda e, ps=ps, c0=c0, sl=sl: e.matmul(ps[:], w3[0:64, c0:c0 + 128], hid2[0:64, sl], start=True, stop=True),
                     reads=[w3_b, hb2], writes=[pb])
                P.op("dve", lambda e, ps=ps, hsb=hsb, sl=sl: e.tensor_tensor(out=hsb[:, sl], in0=ps[:], in1=x1c[:, sl], op=ALU.mult),
                     reads=[pb, x1_b], writes=[hbuf])
        P.op("dve", lambda e: e.memset(hb[:, 0:1], 0.0), reads=[hb_b], writes=[hb_b])
        P.op("dve", lambda e: e.tensor_tensor(out=hs[:], in0=hf[:], in1=hb[:], op=ALU.add), reads=[hf_b, hb_b], writes=[hs_b])
        P.op("dve", lambda e: e.tensor_tensor(out=hb[:], in0=hf[:], in1=hb[:], op=ALU.subtract), reads=[hf_b, hb_b], writes=[hb_b])
        tm_store(hs, hs_b, 0, cc)
        tm_store(hb, hb_b, 1, cc)
        for k in range(3):
            r0 = k * 1024 + cc * 128
            P.dma("sp", ub[k][:, 1:L + 1], c.uT[r0:r0 + 128, :], writes=[ub_b[k]])
        outs = (x0c, x1c, vv)
        outs_b = (x0_b, x1_b, vv_b)
        for k in range(3):
            o, ob = outs[k], outs_b[k]
            wc = [(G + 3 + tap * 3 + k) * 8 + cc for tap in range(3)]
            bc = (G + 12 + k) * 8 + cc
            P.op("dve", lambda e, o=o, k=k, wc=wc, bc=bc: e.tensor_scalar(out=o[:], in0=ub[k][:, 0:L], scalar1=gam[:, wc[0]:wc[0] + 1],
                                                                          scalar2=gam[:, bc:bc + 1], op0=ALU.mult, op1=ALU.add),
                 reads=[ub_b[k], c.const_b], writes=[ob])
            for tap in (1, 2):
                P.op("dve", lambda e, o=o, k=k, wc=wc, tap=tap: e.scalar_tensor_tensor(out=o[:], in0=ub[k][:, tap:tap + L], scalar=gam[:, wc[tap]:wc[tap] + 1],
                                                                                      in1=o[:], op0=ALU.mult, op1=ALU.add),
                     reads=[ub_b[k], c.const_b, ob], writes=[ob])
        P.op("dve", lambda e: e.tensor_tensor(out=vv[:], in0=vv[:], in1=x1c[:], op=ALU.mult), reads=[vv_b, x1_b], writes=[vv_b])
        P.dma("pool", c.hy_x0T[cc * 128:(cc + 1) * 128, :], x0c[:], reads=[x0_b])
        P.dma("pool", c.hy_vvT[cc * 128:(cc + 1) * 128, :], vv[:], reads=[vv_b])
        tm_store(vv, vv_b, 2, cc)

    for cc in range(8):
        prep(cc)
    P.barrier()

    A.reset(c.arena_base)
    tme = [A.alloc([16, 512], BF16) for _ in range(4)]
    tab = [[A.alloc([16, 128], BF16) for _ in range(4)] for _ in range(2)]
    tsb = [A.alloc([512], F32) for _ in range(4)]
    hst = [A.alloc([4, 512], F32) for _ in range(2)]
    tme_b = [Buf() for _ in range(4)]
    tab_b = [Buf() for _ in range(2)]
    tsb_b = [Buf() for _ in range(4)]
    hst_b = [Buf() for _ in range(2)]

    def load_tabs(kc):
        b = kc % 2
        for ti in range(4):
            P.dma("sp", tab[b][ti][:], c.hy_ftab[ti, kc].rearrange("p (a k) -> p a k", k=128), writes=[tab_b[b]])
        return tab[b], tab_b[b]

    def fwd_groups(kc, srcs, src_bufs):
        tb_, tbb = load_tabs(kc)
        outs = []
        for gi in range(4):
            ps, pb = psum_next(c)
            P.op("pe", [lambda e, a=a, ps=ps, gi=gi: e.matmul(ps[:], tb_[gi][:, a, :], srcs[gi][:, a, :], start=(a == 0), stop=(a == 15)) for a in range(16)],
                 reads=[tbb, src_bufs[gi]], writes=[pb])
            outs.append((ps, pb))
        return outs

    def combine(outs, dst4, dst_b):
        (pce, bce), (pco, bco), (pse, bse), (pso, bso) = outs
        i1, i2 = (0, 1) if combine.flip == 0 else (2, 3)
        combine.flip ^= 1
        t1, t1b, t2, t2b = tsb[i1], tsb_b[i1], tsb[i2], tsb_b[i2]
        P.op("act", lambda e: e.copy(out=t1[:], in_=pco[:]), reads=[bco], writes=[t1b])
        P.op("act", lambda e: e.copy(out=t2[:], in_=pso[:]), reads=[bso], writes=[t2b])
        P.op("dve", lambda e: e.tensor_tensor(out=dst4[0], in0=pce[:], in1=t1[:], op=ALU.add), reads=[bce, t1b], writes=[dst_b])
        P.op("dve", lambda e: e.tensor_tensor(out=dst4[2], in0=pce[:], in1=t1[:], op=ALU.subtract), reads=[bce, t1b], writes=[dst_b])
        P.op("dve", lambda e: e.scalar_tensor_tensor(out=dst4[1], in0=pse[:], scalar=-1.0, in1=t2[:], op0=ALU.mult, op1=ALU.subtract),
             reads=[bse, t2b], writes=[dst_b])
        P.op("dve", lambda e: e.tensor_tensor(out=dst4[3], in0=pse[:], in1=t2[:], op=ALU.subtract), reads=[bse, t2b], writes=[dst_b])
    combine.flip = 0

    def passF(half):
        cs = slice(half * 512, (half + 1) * 512)
        for i, (kind, par) in enumerate(((0, 0), (0, 1), (1, 0), (1, 1))):
            P.dma("sp", tme[i][:], c.hy_tm[kind, par, :, cs].rearrange("(a p) c -> p a c", p=128), writes=[tme_b[i]])
        for kc in range(16):
            outs = fwd_groups(kc, tme, tme_b)
            hb_i = kc % 2
            st, stb = hst[hb_i], hst_b[hb_i]
            combine(outs, [st[:, q, :] for q in range(4)], stb)
            P.dma("pool", c.hy_hF[kc, :, :, cs].rearrange("q p c -> p q c"), st[:], reads=[stb])

    for half in range(2):
        passF(half)
    P.barrier()

    A.reset(c.arena_base)
    Z = A.alloc([16, 4, 512], BF16)
    Z_b = [Buf() for _ in range(16)]
    mark2 = A.off
    ueo = [A.alloc([16, 512], BF16) for _ in range(2)]
    tab = [[A.alloc([16, 128], BF16) for _ in range(4)] for _ in range(2)]
    tsb = [A.alloc([512], F32) for _ in range(4)]
    hsl = [A.alloc([4, 512], F32) for _ in range(2)]
    U4 = A.alloc([4, 512], F32)
    Y4 = A.alloc([4, 512], F32)
    mm = [A.alloc([512], F32) for _ in range(2)]
    ueo_b = [Buf() for _ in range(2)]
    tab_b = [Buf() for _ in range(2)]
    tsb_b = [Buf() for _ in range(4)]
    hsl_b = [Buf() for _ in range(2)]
    U_b, Y_b = Buf(), Buf()
    mm_b = [Buf() for _ in range(2)]
    fwd_end = A.off
    A.reset(mark2)
    itab = [[A.alloc([16, 512], BF16) for _ in range(2)] for _ in range(2)]
    ycomb = [A.alloc([512, 2], F32) for _ in range(4)]
    vsl = [A.alloc([1024], BF16) for _ in range(2)]
    xsl = [A.alloc([1024], BF16) for _ in range(2)]
    ost = [A.alloc([1024], BF16) for _ in range(2)]
    ytmp = [A.alloc([1024], F32) for _ in range(2)]
    itab_b = [Buf() for _ in range(2)]
    ycomb_b = [Buf() for _ in range(4)]
    vsl_b = [Buf() for _ in range(2)]
    xsl_b = [Buf() for _ in range(2)]
    ost_b = [Buf() for _ in range(2)]
    ytmp_b = [Buf() for _ in range(2)]
    skg = (G + 15) * 8

    def cmul(kc, hq):
        U = [U4[:, q, :] for q in range(4)]
        Y = [Y4[:, q, :] for q in range(4)]
        H = [hsl[hq][:, q, :] for q in range(4)]
        hb_ = hsl_b[hq]
        for o in (0, 2):
            re, im = o, o + 1
            P.op("dve", lambda e, re=re: e.tensor_tensor(out=mm[0][:], in0=U[re], in1=H[re], op=ALU.mult), reads=[U_b, hb_], writes=[mm_b[0]])
            P.op("dve", lambda e, im=im: e.tensor_tensor(out=mm[1][:], in0=U[im], in1=H[im], op=ALU.mult), reads=[U_b, hb_], writes=[mm_b[1]])
            P.op("dve", lambda e, re=re: e.tensor_tensor(out=Y[re], in0=mm[0][:], in1=mm[1][:], op=ALU.subtract), reads=mm_b, writes=[Y_b])
            P.op("dve", lambda e, re=re, im=im: e.tensor_tensor(out=mm[0][:], in0=U[re], in1=H[im], op=ALU.mult), reads=[U_b, hb_, Y_b], writes=[mm_b[0]])
            P.op("dve", lambda e, re=re, im=im: e.tensor_tensor(out=mm[1][:], in0=U[im], in1=H[re], op=ALU.mult), reads=[U_b, hb_, Y_b], writes=[mm_b[1]])
            P.op("dve", lambda e, im=im: e.tensor_tensor(out=Y[im], in0=mm[0][:], in1=mm[1][:], op=ALU.add), reads=mm_b, writes=[Y_b])
        P.op("dve", lambda e: e.tensor_tensor(out=Z[:, kc, 0, :], in0=Y[0], in1=Y[2], op=ALU.add), reads=[Y_b], writes=[Z_b[kc]])
        P.op("dve", lambda e: e.tensor_tensor(out=Z[:, kc, 1, :], in0=Y[1], in1=Y[3], op=ALU.subtract), reads=[Y_b], writes=[Z_b[kc]])
        P.op("dve", lambda e: e.tensor_tensor(out=Z[:, kc, 2, :], in0=Y[0], in1=Y[2], op=ALU.subtract), reads=[Y_b], writes=[Z_b[kc]])
        P.op("dve", lambda e: e.tensor_tensor(out=Z[:, kc, 3, :], in0=Y[1], in1=Y[3], op=ALU.add), reads=[Y_b], writes=[Z_b[kc]])

    def passU(half):
        cs = slice(half * 512, (half + 1) * 512)
        for par in range(2):
            P.dma("sp", ueo[par][:], c.hy_tm[2, par, :, cs].rearrange("(a p) c -> p a c", p=128), writes=[ueo_b[par]])
        srcs = [ueo[0], ueo[1], ueo[0], ueo[1]]
        sbufs = [ueo_b[0], ueo_b[1], ueo_b[0], ueo_b[1]]
        for kc in range(16):
            hq = kc % 2
            P.dma("sp", hsl[hq][:], c.hy_hF[kc, :, :, cs].rearrange("q p c -> p q c"), writes=[hsl_b[hq]])
            outs = fwd_groups(kc, srcs, sbufs)
            combine(outs, [U4[:, q, :] for q in range(4)], U_b)
            cmul(kc, hq)
        P.barrier()
        ii = [0]
        for tb in range(4):
            for par in range(2):
                ib_ = ii[0] % 2
                ii[0] += 1
                for q in range(2):
                    P.dma("sp", itab[ib_][q][:], c.hy_itab[2 * par + q, tb].rearrange("p (a t) -> p a t", t=512), writes=[itab_b[ib_]])
                for cq in range(4):
                    ps, pb = psum_next(c)
                    fns = []
                    for kc in range(16):
                        for q in range(2):
                            fns.append(lambda e, kc=kc, q=q, ps=ps, cq=cq, par=par, ib_=ib_: e.matmul(
                                ps[:], Z[:, kc, 2 * par + q, cq * 128:(cq + 1) * 128], itab[ib_][q][:, kc, :],
                                start=(kc == 0 and q == 0), stop=(kc == 15 and q == 1)))
                    P.op("pe", fns, reads=Z_b + [itab_b[ib_]], writes=[pb])
                    P.op("act", lambda e, ps=ps, cq=cq, par=par: e.copy(out=ycomb[cq][:, :, par], in_=ps[:]), reads=[pb], writes=[ycomb_b[cq]])
            for cq in range(4):
                r0 = half * 512 + cq * 128
                sb_ = cq % 2
                P.dma("sp", vsl[sb_][:], c.hy_vvT[r0:r0 + 128, tb * 1024:(tb + 1) * 1024], writes=[vsl_b[sb_]])
                P.dma("sp", xsl[sb_][:], c.hy_x0T[r0:r0 + 128, tb * 1024:(tb + 1) * 1024], writes=[xsl_b[sb_]])
                skc = skg + half * 4 + cq
                P.op("dve", lambda e, sb_=sb_, cq=cq, skc=skc: e.scalar_tensor_tensor(out=ytmp[sb_][:], in0=vsl[sb_][:], scalar=gam[:, skc:skc + 1],
                                                                                    in1=ycomb[cq][:].rearrange("p t two -> p (t two)"), op0=ALU.mult, op1=ALU.add),
                     reads=[vsl_b[sb_], ycomb_b[cq], c.const_b], writes=[ytmp_b[sb_]])
                P.op("dve", lambda e, sb_=sb_: e.tensor_tensor(out=ost[sb_][:], in0=ytmp[sb_][:], in1=xsl[sb_][:], op=ALU.mult),
                     reads=[ytmp_b[sb_], xsl_b[sb_]], writes=[ost_b[sb_]])
                P.dma("pool", c.yT[r0:r0 + 128, tb * 1024:(tb + 1) * 1024], ost[sb_][:], reads=[ost_b[sb_]])
        P.barrier()

    for half in range(2):
        passU(half)
    P.barrier()


def gla_core(c, j):
    P = c.P
    A = c.arena
    A.reset(c.arena_base)
    cst = c.cst
    Mtri = (cst[:, 384:512], cst[:, 512:640])
    Mrev = (cst[:, 640:768], cst[:, 768:896])
    mask = (cst[:, 896:1024], cst[:, 1024:1152])
    NT = L // 128
    SCALE = 128.0 ** -0.5
    gkw = [A.alloc([512], BF16) for _ in range(2)]
    gkb = [A.alloc([512], BF16) for _ in range(2)]
    one_row = A.alloc([128], BF16)
    normw = A.alloc([256], F32)
    S32 = A.alloc([4, 256], F32)
    Sbf = [A.alloc([4, 256], BF16) for _ in range(2)]
    NB = 2
    qT = [A.alloc([4, 128], BF16) for _ in range(NB)]
    kT = [A.alloc([4, 128], BF16) for _ in range(NB)]
    ktm = [A.alloc([512], BF16) for _ in range(NB)]
    vtm = [A.alloc([1024], BF16) for _ in range(NB)]
    gtm_sb = [A.alloc([1024], BF16) for _ in range(NB)]
    gl_sb = [A.alloc([128], BF16) for _ in range(NB)]
    ofin = [A.alloc([1024], F32) for _ in range(NB)]
    sp2 = [A.alloc([512], F32) for _ in range(2)]
    eb2 = [A.alloc([512], F32) for _ in range(2)]
    enb2 = [A.alloc([512], F32) for _ in range(2)]
    erev2 = [A.alloc([512], F32) for _ in range(2)]
    qt2 = [A.alloc([4, 128], BF16) for _ in range(2)]
    kt2 = [A.alloc([4, 128], BF16) for _ in range(2)]
    khat2 = [A.alloc([512], BF16) for _ in range(2)]
    att2 = [A.alloc([4, 128], BF16) for _ in range(2)]
    qtz2 = [A.alloc([4, 2, 128], BF16) for _ in range(2)]
    tb2 = [[Buf() for _ in range(9)] for _ in range(2)]
    o_sb = A.alloc([1024], F32)
    sq = A.alloc([1024], F32)
    sg = A.alloc([1024], F32)
    ssq = A.alloc([4], F32)
    ytT = A.alloc([8, 512], BF16)
    cb_ = Buf()
    in_b = [Buf() for _ in range(NB)]
    o_b, sq_b, sg_b, ssq_b, yt_b = [Buf() for _ in range(5)]
    S32_b = [Buf() for _ in range(4)]
    Sbf_b = [[Buf() for _ in range(4)] for _ in range(2)]
    for dr in range(2):
        P.dma("pool", gkw[dr][0:16, :], c.gla_gk_w[j, dr], writes=[cb_])
        P.dma("pool", gkb[dr][0:1, :], c.gla_gk_b[j, dr].rearrange("(o n) -> o n", o=1), writes=[cb_])
    P.dma("sp", normw[:], c.gla_norm[j].partition_broadcast(128), writes=[cb_])
    P.op("dve", lambda e: e.memset(one_row[0:1, :], 1.0), writes=[cb_])
    for q_ in range(2):
        P.op("dve", lambda e, q_=q_: e.memset(qtz2[q_][:].rearrange("p h c t -> p (h c t)"), 0.0), writes=[tb2[q_][8]])
    ps_o = [(c.ps[0], c.ps_b[0]), (c.ps[1], c.ps_b[1])]
    rot = [2]

    def psn():
        i = rot[0]
        rot[0] = 2 + (i - 1) % 6
        return c.ps[i], c.ps_b[i]

    def run_tile(dr, it, ti):
        t0 = ti * 128
        p2 = it % 2
        sp, eb, enb, erev, qt, kt, khat, att, qtz = sp2[p2], eb2[p2], enb2[p2], erev2[p2], qt2[p2], kt2[p2], khat2[p2], att2[p2], qtz2[p2]
        sp_b, eb_b, enb_b, erev_b, qt_b, kt_b, khat_b, att_b, qtz_b = tb2[p2]
        b = it % NB
        ib = in_b[b]
        P.dma("sp", qT[b][:], c.qkT[0:512, t0:t0 + 128].rearrange("(h p) t -> p h t", p=128), writes=[ib])
        P.dma("sp", kT[b][:], c.qkT[512:1024, t0:t0 + 128].rearrange("(h p) t -> p h t", p=128), writes=[ib])
        P.dma("sp", ktm[b][:], c.gtm[t0:t0 + 128, 0:512], writes=[ib])
        P.dma("sp", vtm[b][:], c.gtm[t0:t0 + 128, 512:1536], writes=[ib])
        P.dma("sp", gl_sb[b][0:16, :], c.glT[dr * 16:(dr + 1) * 16, t0:t0 + 128], writes=[ib])
        if dr == 1:
            P.dma("sp", gtm_sb[b][:], c.gtm[t0:t0 + 128, 1536:2560], writes=[ib])
            P.dma("sp", ofin[b][:], c.ofwd[t0:t0 + 128, :], writes=[ib])
        ps, pb = psn()
        P.op("pe", [lambda e, ps=ps, b=b: e.matmul(ps[:], gl_sb[b][0:16, :], gkw[dr][0:16, :], start=True, stop=False),
                    lambda e, ps=ps: e.matmul(ps[:], one_row[0:1, :], gkb[dr][0:1, :], start=False, stop=True)],
             reads=[ib, cb_], writes=[pb])
        P.op("act", lambda e, ps=ps: e.activation(out=sp[:], in_=ps[:], func=AF.Exp, scale=-1.0), reads=[pb], writes=[sp_b])
        P.op("dve", lambda e: e.tensor_scalar_add(out=sp[:], in0=sp[:], scalar1=1.0), reads=[sp_b], writes=[sp_b])
        P.op("act", lambda e: e.activation(out=sp[:], in_=sp[:], func=AF.Ln), reads=[sp_b], writes=[sp_b])
        psb_, pbb_ = psn()
        P.op("pe", [lambda e, h=h, psb_=psb_: e.matmul(psb_[:, h * 128:(h + 1) * 128], sp[:, h * 128:(h + 1) * 128], Mtri[dr], start=True, stop=True)
                    for h in range(4)], reads=[sp_b, c.const_b], writes=[pbb_])
        psr, pbr = psn()
        P.op("pe", lambda e, psr=psr: e.matmul(psr[:], Mrev[dr], sp[:], start=True, stop=True), reads=[sp_b, c.const_b], writes=[pbr])
        P.op("act", lambda e, psb_=psb_: e.activation(out=eb[:], in_=psb_[:], func=AF.Exp), reads=[pbb_], writes=[eb_b])
        P.op("act", lambda e, psb_=psb_: e.activation(out=enb[:], in_=psb_[:], func=AF.Exp, scale=-1.0), reads=[pbb_], writes=[enb_b])
        P.op("act", lambda e, psr=psr: e.activation(out=erev[:], in_=psr[:], func=AF.Exp), reads=[pbr], writes=[erev_b])
        P.op("dve", lambda e, b=b: e.scalar_tensor_tensor(out=qt[:].rearrange("p h t -> p (h t)"), in0=qT[b][:].rearrange("p h t -> p (h t)"),
                                                          scalar=SCALE, in1=eb[:], op0=ALU.mult, op1=ALU.mult),
             reads=[ib, eb_b], writes=[qt_b])
        for cc_ in range(2):
            P.op("act", lambda e, cc_=cc_: e.copy(out=qtz[:, :, cc_, cc_ * 64:(cc_ + 1) * 64], in_=qt[:, :, cc_ * 64:(cc_ + 1) * 64]),
                 reads=[qt_b], writes=[qtz_b])
        P.op("dve", lambda e, b=b: e.tensor_tensor(out=kt[:].rearrange("p h t -> p (h t)"), in0=kT[b][:].rearrange("p h t -> p (h t)"),
                                                   in1=enb[:], op=ALU.mult), reads=[ib, enb_b], writes=[kt_b])
        P.op("dve", lambda e, b=b: e.tensor_tensor(out=khat[:], in0=ktm[b][:], in1=erev[:], op=ALU.mult), reads=[ib, erev_b], writes=[khat_b])
        psa, pba = psn()
        P.op("pe", [lambda e, h=h, psa=psa: e.matmul(psa[:, h * 128:(h + 1) * 128], kt[:, h, :], qt[:, h, :], start=True, stop=True) for h in range(4)],
             reads=[kt_b, qt_b], writes=[pba])
        P.op("dve", lambda e, psa=psa: e.tensor_tensor(out=att[:], in0=psa[:].rearrange("p (h t) -> p h t", h=4),
                                                       in1=mask[dr].unsqueeze(1).broadcast_to([128, 4, 128]), op=ALU.mult),
             reads=[pba, c.const_b], writes=[att_b])
        if c.debug and dr == 0 and ti == 0:
            P.dma("pool", c.dbg[:, 0:512], sp[:], reads=[sp_b])
            P.dma("pool", c.dbg[:, 512:1024], eb[:], reads=[eb_b])
            P.dma("pool", c.dbg[:, 1024:1536], erev[:], reads=[erev_b])
            P.dma("pool", c.dbg[:, 1536:2048], att[:].rearrange("p h t -> p (h t)"), reads=[att_b])
            P.dma("pool", c.dbg[:, 2048:2560], qt[:].rearrange("p h t -> p (h t)"), reads=[qt_b])
            P.dma("pool", c.dbg[:, 2560:3072], kt[:].rearrange("p h t -> p (h t)"), reads=[kt_b])
            P.dma("pool", c.dbg[:, 3072:3584], khat[:], reads=[khat_b])
            P.dma("pool", c.dbg[:, 3584:4608], qtz[:].rearrange("p h c t -> p (h c t)"), reads=[qtz_b])
        def chain():
            chunks = (0, 1) if dr == 0 else (1, 0)
            fns = []
            for h in range(4):
                po = ps_o[h // 2][0]
                oc = slice((h % 2) * 256, (h % 2 + 1) * 256)
                r0 = chunks[0] * 64
                fns.append(lambda e, po=po, oc=oc, h=h, b=b: e.matmul(po[:, oc], att[:, h, :], vtm[b][:, h * 256:(h + 1) * 256], start=(h % 2 == 0), stop=False,
                                                                      skip_group_check=True))
                fns.append(lambda e, po=po, oc=oc, h=h, c0=chunks[0]: e.matmul(po[:, oc], qtz[:, h, c0, :], Sbf[0][:, h, :], start=False, stop=False,
                                                                               skip_group_check=True))
            P.op("pe", fns, reads=[att_b, ib, qtz_b] + Sbf_b[0], writes=[ps_o[0][1], ps_o[1][1]])
            for ci, ch in enumerate(chunks):
                r0 = ch * 64
                for h in range(4):
                    if dr == 0:
                        lc = h * 128 + r0 + 63
                    else:
                        lc = h * 128 + r0
                    pss, pbs = psn()
                    P.op("pe", lambda e, pss=pss, h=h, r0=r0, b=b: e.matmul(pss[:, 0:256], khat[r0:r0 + 64, h * 128:(h + 1) * 128],
                                                                          vtm[b][r0:r0 + 64, h * 256:(h + 1) * 256], start=True, stop=True),
                         reads=[khat_b, ib], writes=[pbs])
                    P.op("dve", lambda e, pss=pss, h=h, lc=lc: e.scalar_tensor_tensor(out=S32[:, h, :], in0=S32[:, h, :], scalar=eb[:, lc:lc + 1],
                                                                                      in1=pss[:, 0:256], op0=ALU.mult, op1=ALU.add),
                         reads=[pbs, eb_b, S32_b[h]], writes=[S32_b[h]])
                    P.op("act", lambda e, h=h, ci=ci: e.copy(out=Sbf[1 - ci][:, h, :], in_=S32[:, h, :]), reads=[S32_b[h]], writes=[Sbf_b[1 - ci][h]])
                if ci == 0:
                    r1 = chunks[1] * 64
                    fns = []
                    for h in range(4):
                        po = ps_o[h // 2][0]
                        oc = slice((h % 2) * 256, (h % 2 + 1) * 256)
                        fns.append(lambda e, po=po, oc=oc, h=h, c1=chunks[1]: e.matmul(po[:, oc], qtz[:, h, c1, :], Sbf[1][:, h, :], start=False, stop=True, skip_group_check=True))
                    P.op("pe", fns, reads=[qtz_b] + Sbf_b[1], writes=[ps_o[0][1], ps_o[1][1]])
            if dr == 0:
                for k in range(2):
                    P.op("act", lambda e, k=k: e.copy(out=o_sb[:, k * 512:(k + 1) * 512], in_=ps_o[k][0][:]), reads=[ps_o[k][1]], writes=[o_b])
                P.dma("act", c.ofwd[t0:t0 + 128, :], o_sb[:], reads=[o_b])
            else:
                for k in range(2):
                    P.op("dve", lambda e, k=k, b=b: e.tensor_tensor(out=o_sb[:, k * 512:(k + 1) * 512], in0=ps_o[k][0][:], in1=ofin[b][:, k * 512:(k + 1) * 512], op=ALU.add),
                         reads=[ps_o[k][1], ib], writes=[o_b])
                P.op("dve", lambda e: e.tensor_tensor(out=sq[:], in0=o_sb[:], in1=o_sb[:], op=ALU.mult), reads=[o_b], writes=[sq_b])
                P.op("dve", lambda e: e.reduce_sum(out=ssq[:], in_=sq[:].rearrange("p (h v) -> p h v", h=4), axis=AX.X), reads=[sq_b], writes=[ssq_b])
                P.op("dve", lambda e: e.tensor_scalar(out=ssq[:], in0=ssq[:], scalar1=1.0 / 256.0, scalar2=EPS, op0=ALU.mult, op1=ALU.add), reads=[ssq_b], writes=[ssq_b])
                P.op("act", lambda e: e.activation(out=ssq[:], in_=ssq[:], func=AF.Sqrt), reads=[ssq_b], writes=[ssq_b])
                P.op("dve", lambda e: e.reciprocal(out=ssq[:], in_=ssq[:]), reads=[ssq_b], writes=[ssq_b])
                P.op("act", lambda e, b=b: e.activation(out=sg[:], in_=gtm_sb[b][:], func=AF.Silu), reads=[ib], writes=[sg_b])
                P.op("dve", lambda e: e.tensor_tensor(out=sq[:].rearrange("p (h v) -> p h v", h=4), in0=o_sb[:].rearrange("p (h v) -> p h v", h=4),
                                                      in1=ssq[:].unsqueeze(2).broadcast_to([128, 4, 256]), op=ALU.mult), reads=[o_b, ssq_b, sq_b], writes=[sq_b])
                P.op("dve", lambda e: e.tensor_tensor(out=sq[:].rearrange("p (h v) -> p h v", h=4), in0=sq[:].rearrange("p (h v) -> p h v", h=4),
                                                      in1=normw[:].unsqueeze(1).broadcast_to([128, 4, 256]), op=ALU.mult), reads=[sq_b, cb_], writes=[sq_b])
                P.op("dve", lambda e: e.tensor_tensor(out=sq[:], in0=sq[:], in1=sg[:], op=ALU.mult), reads=[sq_b, sg_b], writes=[sq_b])
                slot = ti % 4
                for g in range(2):
                    pst, pbt = psn()
                    P.op("pe", [lambda e, pst=pst, g=g, k=k: e.transpose(pst[:, k * 128:(k + 1) * 128], sq[:, (g * 4 + k) * 128:(g * 4 + k + 1) * 128], c.ident_f[:])
                                for k in range(4)], reads=[sq_b, c.const_b], writes=[pbt])
                    P.op("act", lambda e, pst=pst, g=g, slot=slot: e.copy(out=ytT[:, g * 4:(g + 1) * 4, slot * 128:(slot + 1) * 128],
                                                                         in_=pst[:].rearrange("p (k t) -> p k t", k=4)), reads=[pbt], writes=[yt_b])
                if slot == 0:
                    tb0 = (ti // 4) * 512
                    P.dma("act", c.yT[0:1024, tb0:tb0 + 512].rearrange("(k p) t -> p k t", p=128), ytT[:], reads=[yt_b])
        return chain

    for dr in range(2):
        for h in range(4):
            P.op("dve", lambda e, h=h: e.memset(S32[:, h, :], 0.0), writes=[S32_b[h]])
            P.op("dve", lambda e, h=h: e.memset(Sbf[0][:, h, :], 0.0), writes=[Sbf_b[0][h]])
        order = list(range(NT)) if dr == 0 else list(range(NT - 1, -1, -1))
        pending = None
        for it, ti in enumerate(order):
            ch = run_tile(dr, it, ti)
            if pending is not None:
                pending()
            pending = ch
        pending()
        P.barrier()
    P.barrier()


def ssd_conv_stage(c, j):
    P = c.P
    A = c.arena
    A.reset(c.arena_base)
    gam = c.gam
    G = c.gcol_ssd + j * 24
    ub = [A.alloc([L + 4], BF16) for _ in range(2)]
    acc = [A.alloc([L], BF16) for _ in range(2)]
    stg = [A.alloc([4, 128], BF16) for _ in range(3)]
    diag = [A.alloc([5, 128], BF16) for _ in range(2)]
    diag_b = [Buf() for _ in range(2)]
    ub_b = [Buf() for _ in range(2)]
    acc_b = [Buf() for _ in range(2)]
    stg_b = [Buf() for _ in range(3)]
    si = [0]

    def chunk(cc):
        b = cc % 2
        u, ubb, a, ab = ub[b], ub_b[b], acc[b], acc_b[b]
        P.op("pool", lambda e: e.memset(u[:, 0:2], 0.0), writes=[ubb])
        P.op("pool", lambda e: e.memset(u[:, L + 2:L + 4], 0.0), writes=[ubb])
        P.dma("sp", u[:, 2:L + 2], c.xbcT[cc * 128:(cc + 1) * 128, :], writes=[ubb])
        grp, kc = cc // 8, cc % 8
        wc = [(G + tap * 4 + grp) * 8 + kc for tap in range(5)]
        bc = (G + 20 + grp) * 8 + kc
        dg, dgb = diag[b], diag_b[b]
        for tap in range(5):
            P.op("dve", lambda e, tap=tap: e.tensor_scalar(out=dg[:, tap, :], in0=c.ident_f[:], scalar1=gam[:, wc[tap]:wc[tap] + 1], scalar2=None, op0=ALU.mult),
                 reads=[c.const_b], writes=[dgb])
        for tb in range(L // 512):
            ps, pb = psum_next(c)
            P.op("pe", [lambda e, tap=tap, ps=ps, tb=tb: e.matmul(ps[:], dg[:, tap, :], u[:, tb * 512 + tap:tb * 512 + tap + 512], start=(tap == 0), stop=(tap == 4))
                        for tap in range(5)], reads=[dgb, ubb], writes=[pb])
            P.op("act", lambda e, ps=ps, tb=tb: e.activation(out=a[:, tb * 512:(tb + 1) * 512], in_=ps[:], func=AF.Silu, bias=gam[:, bc:bc + 1]),
                 reads=[pb, c.const_b], writes=[ab])
        if cc >= 16:
            P.dma("act", c.bcT[(cc - 16) * 128:(cc - 15) * 128, :], a[:], reads=[ab])
        if cc < 24:
            dst = c.x_tm if cc < 16 else c.B_tm
            col0 = (cc if cc < 16 else cc - 16) * 128
            for t4 in range(L // 512):
                ps, pb = psum_next(c)
                psv = ps[:].bitcast(BF16)
                P.op("pe", [lambda e, k=k, psv=psv, t4=t4: e.transpose(psv[:, k * 128:(k + 1) * 128], a[:, (t4 * 4 + k) * 128:(t4 * 4 + k + 1) * 128], c.ident_b[:])
                            for k in range(4)], reads=[ab, c.const_b], writes=[pb])
                sg, sgb = stg[si[0]], stg_b[si[0]]
                si[0] = (si[0] + 1) % 3
                P.op("act", lambda e, sg=sg, psv=psv: e.copy(out=sg[:].rearrange("p k t -> p (k t)"), in_=psv[:, 0:512]), reads=[pb], writes=[sgb])
                P.dma("act", dst[t4 * 512:(t4 + 1) * 512, col0:col0 + 128].rearrange("(k p) c -> p k c", p=128), sg[:], reads=[sgb])

    for cc in range(32):
        chunk(cc)
    P.barrier()


def ssd_core(c, j):
    P = c.P
    A = c.arena
    A.reset(c.arena_base)
    cst = c.cst
    Minc = (cst[:, 1152:1280], cst[:, 1280:1408])
    Mexc = (cst[:, 1408:1536], cst[:, 1536:1664])
    ones_f = cst[:, 1664:1792]
    NT = L // 128
    NB = 2
    rowc = A.alloc([2, 3, 32], F32)
    dsk = A.alloc([32], F32)
    normw = A.alloc([2048], F32)
    Ub = [A.alloc([128], BF16) for _ in range(2)]
    xtm = [A.alloc([2048], BF16) for _ in range(NB)]
    Btm = [A.alloc([1024], BF16) for _ in range(NB)]
    BT = [A.alloc([8, 128], BF16) for _ in range(NB)]
    CT = [A.alloc([8, 128], BF16) for _ in range(NB)]
    dtT = [A.alloc([128], F32) for _ in range(NB)]
    ztm = [A.alloc([2048], BF16) for _ in range(NB)]
    yfin = [A.alloc([2048], F32) for _ in range(NB)]
    dt = A.alloc([32], F32)
    dta = A.alloc([32], F32)
    ed = A.alloc([96], F32)
    R = A.alloc([32, 128], BF16)
    xh = A.alloc([2048], BF16)
    xw = A.alloc([2048], BF16)
    S32 = A.alloc([8, 256], F32)
    Sbf2 = [A.alloc([8, 256], BF16) for _ in range(2)]
    E8 = [A.alloc([4, 128], BF16) for _ in range(8)]
    Sc8 = [A.alloc([4, 128], BF16) for _ in range(8)]
    tmp8 = [A.alloc([256], F32) for _ in range(8)]
    CBm4 = A.alloc([4, 128], BF16)
    E8_b = [Buf() for _ in range(8)]
    Sc8_b = [Buf() for _ in range(8)]
    tmp8_b = [Buf() for _ in range(8)]
    CBm4_b = Buf()
    S32_all = Buf()
    Sbf2_b = [Buf() for _ in range(2)]
    y_sb = A.alloc([2048], F32)
    sq = A.alloc([2048], F32)
    sgz = A.alloc([2048], F32)
    ssq = A.alloc([8], F32)
    ytT = A.alloc([16, 512], BF16)
    cb_ = Buf()
    in_b = [Buf() for _ in range(NB)]
    dt_b, dta_b, ed_b, R_b, xh_b, xw_b, y_b, sq_b, sgz_b, ssq_b, yt_b = [Buf() for _ in range(11)]
    E_b = [Buf() for _ in range(2)]
    Sc_b = [Buf() for _ in range(2)]
    CBm_b = [Buf() for _ in range(2)]
    tmpi_b = [Buf() for _ in range(2)]
    S32_b = [Buf() for _ in range(8)]
    Sbf_b = [Buf() for _ in range(8)]
    for dr in range(2):
        P.dma("sp", rowc[:, dr, 0, :], c.ssd_dt_bias[j, dr].partition_broadcast(128), writes=[cb_])
        P.dma("sp", rowc[:, dr, 1, :], c.ssd_a_log[j, dr].partition_broadcast(128), writes=[cb_])
    P.dma("sp", dsk[:], c.ssd_d[j].partition_broadcast(128), writes=[cb_])
    P.dma("sp", normw[:], c.ssd_norm[j].partition_broadcast(128), writes=[cb_])
    for dr in range(2):
        P.op("act", lambda e, dr=dr: e.activation(out=rowc[:, dr, 1, :], in_=rowc[:, dr, 1, :], func=AF.Exp), reads=[cb_], writes=[cb_])
        P.op("dve", lambda e, dr=dr: e.tensor_scalar(out=rowc[:, dr, 1, :], in0=rowc[:, dr, 1, :], scalar1=-1.0, scalar2=None, op0=ALU.mult),
             reads=[cb_], writes=[cb_])
        P.op("dve", lambda e, dr=dr: e.tensor_copy(out=Ub[dr][:], in_=Mexc[dr]), reads=[c.const_b], writes=[cb_])
    rot = [0]

    def psn():
        i = rot[0]
        rot[0] = (i + 1) % 8
        return c.ps[i], c.ps_b[i]

    def run_tile(dr, it, ti):
        t0 = ti * 128
        b = it % NB
        ib = in_b[b]
        P.dma("sp", xtm[b][:], c.x_tm[t0:t0 + 128, :], writes=[ib])
        P.dma("sp", Btm[b][:], c.B_tm[t0:t0 + 128, :], writes=[ib])
        P.dma("sp", BT[b][:], c.bcT[0:1024, t0:t0 + 128].rearrange("(g p) t -> p g t", p=128), writes=[ib])
        P.dma("sp", CT[b][:], c.bcT[1024:2048, t0:t0 + 128].rearrange("(g p) t -> p g t", p=128), writes=[ib])
        P.dma("sp", dtT[b][0:32, :], c.dtT[dr * 32:(dr + 1) * 32, t0:t0 + 128], writes=[ib])
        if dr == 1:
            P.dma("sp", ztm[b][:], c.z_tm[t0:t0 + 128, :], writes=[ib])
            P.dma("sp", yfin[b][:], c.yf[t0:t0 + 128, :], writes=[ib])
        ps, pb = psn()
        P.op("pe", lambda e: e.transpose(ps[:, 0:32], dtT[b][0:32, :], c.ident_f[0:32, 0:32]), reads=[ib, c.const_b], writes=[pb])
        P.op("dve", lambda e: e.tensor_tensor(out=dt[:], in0=ps[:, 0:32], in1=rowc[:, dr, 0, :], op=ALU.add), reads=[pb, cb_], writes=[dt_b])
        P.op("act", lambda e: e.activation(out=dt[:], in_=dt[:], func=AF.Exp), reads=[dt_b], writes=[dt_b])
        P.op("dve", lambda e: e.tensor_scalar_add(out=dt[:], in0=dt[:], scalar1=1.0), reads=[dt_b], writes=[dt_b])
        P.op("act", lambda e: e.activation(out=dt[:], in_=dt[:], func=AF.Ln), reads=[dt_b], writes=[dt_b])
        P.op("dve", lambda e: e.tensor_tensor(out=dta[:], in0=dt[:], in1=rowc[:, dr, 1, :], op=ALU.mult), reads=[dt_b, cb_], writes=[dta_b])
        ps2, pb2 = psn()
        P.op("pe", [lambda e: e.matmul(ps2[:, 0:32], Minc[dr], dta[:], start=True, stop=True),
                    lambda e: e.matmul(ps2[:, 32:64], Mexc[dr], dta[:], start=True, stop=True),
                    lambda e: e.matmul(ps2[:, 64:96], ones_f, dta[:], start=True, stop=True)], reads=[dta_b, c.const_b], writes=[pb2])
        P.op("act", lambda e: e.activation(out=ed[:], in_=ps2[:, 0:96], func=AF.Exp), reads=[pb2], writes=[ed_b])
        P.op("act", [lambda e, h=h: e.activation(out=R[:, h, :], in_=Minc[dr], func=AF.Copy, scale=dta[:, h:h + 1]) for h in range(32)],
             reads=[dta_b, c.const_b], writes=[R_b])
        P.op("dve", lambda e: e.tensor_tensor(out=xh[:].rearrange("p (h q) -> p h q", h=32), in0=xtm[b][:].rearrange("p (h q) -> p h q", h=32),
                                              in1=dt[:].unsqueeze(2).broadcast_to([128, 32, 64]), op=ALU.mult), reads=[ib, dt_b], writes=[xh_b])
        P.op("dve", lambda e: e.tensor_tensor(out=xw[:].rearrange("p (h q) -> p h q", h=32), in0=xh[:].rearrange("p (h q) -> p h q", h=32),
                                              in1=ed[:, 32:64].unsqueeze(2).broadcast_to([128, 32, 64]), op=ALU.mult), reads=[xh_b, ed_b], writes=[xw_b])

        par_r = it % 2
        par_w = 1 - par_r
        st_ps = []
        for g2 in range(4):
            pss, pbs = psn()
            P.op("pe", [lambda e, g=g, pss=pss: e.matmul(pss[:, (g % 2) * 256:(g % 2 + 1) * 256], Btm[b][:, g * 128:(g + 1) * 128], xw[:, g * 256:(g + 1) * 256],
                                                          start=True, stop=True) for g in (2 * g2, 2 * g2 + 1)], reads=[ib, xw_b], writes=[pbs])
            st_ps.append((pss, pbs))
        P.op("dve", lambda e: e.tensor_tensor(out=S32[:].rearrange("p g (k q) -> p (g k) q", k=4), in0=S32[:].rearrange("p g (k q) -> p (g k) q", k=4),
                                              in1=ed[:, 64:96].unsqueeze(2).broadcast_to([128, 32, 64]), op=ALU.mult), reads=[S32_all, ed_b], writes=[S32_all])
        for g2 in range(4):
            pss, pbs = st_ps[g2]
            P.op("dve", lambda e, g2=g2, pss=pss: e.tensor_tensor(out=S32[:, 2 * g2:2 * g2 + 2, :].rearrange("p g q -> p (g q)"),
                                                                  in0=S32[:, 2 * g2:2 * g2 + 2, :].rearrange("p g q -> p (g q)"), in1=pss[:], op=ALU.add),
                 reads=[S32_all, pbs], writes=[S32_all])
        P.op("act", lambda e: e.copy(out=Sbf2[par_w][:].rearrange("p g q -> p (g q)"), in_=S32[:].rearrange("p g q -> p (g q)")),
             reads=[S32_all], writes=[Sbf2_b[par_w]])

        def batch(g0):
            gs = list(range(g0, g0 + 4))
            psc, pbc = psn()
            P.op("pe", [lambda e, g=g: e.matmul(psc[:, (g - g0) * 128:(g - g0 + 1) * 128], BT[b][:, g, :], CT[b][:, g, :], start=True, stop=True) for g in gs],
                 reads=[ib], writes=[pbc])
            dps = []
            for g in gs:
                psd, pbd = psn()
                P.op("pe", lambda e, g=g, psd=psd: e.matmul(psd[:], Ub[dr][:], R[:, 4 * g:4 * g + 4, :].rearrange("p k l -> p (k l)"), start=True, stop=True),
                     reads=[cb_, R_b], writes=[pbd])
                dps.append((psd, pbd))
            P.op("dve", lambda e: e.tensor_tensor(out=CBm4[:], in0=psc[:].rearrange("p (g l) -> p g l", g=4),
                                                  in1=Minc[dr].unsqueeze(1).broadcast_to([128, 4, 128]), op=ALU.mult), reads=[pbc, c.const_b], writes=[CBm4_b])
            for i, g in enumerate(gs):
                psd, pbd = dps[i]
                P.op("act", lambda e, g=g, psd=psd: e.activation(out=E8[g][:].rearrange("p k l -> p (k l)"), in_=psd[:], func=AF.Exp), reads=[pbd], writes=[E8_b[g]])
            for i, g in enumerate(gs):
                P.op("dve", lambda e, g=g, i=i: e.tensor_tensor(out=Sc8[g][:], in0=E8[g][:], in1=CBm4[:, i, :].unsqueeze(1).broadcast_to([128, 4, 128]), op=ALU.mult),
                     reads=[E8_b[g], CBm4_b], writes=[Sc8_b[g]])
            yps = []
            for g in gs:
                psy, pby = psn()
                fns = [lambda e, k=k, g=g, psy=psy: e.matmul(psy[:, k * 64:(k + 1) * 64], Sc8[g][:, k, :], xh[:, (4 * g + k) * 64:(4 * g + k + 1) * 64], start=True, stop=True)
                       for k in range(4)]
                fns.append(lambda e, g=g, psy=psy: e.matmul(psy[:, 256:512], CT[b][:, g, :], Sbf2[par_r][:, g, :], start=True, stop=True))
                P.op("pe", fns, reads=[Sc8_b[g], xh_b, ib, Sbf2_b[par_r]], writes=[pby])
                yps.append((psy, pby))
            for i, g in enumerate(gs):
                psy, pby = yps[i]
                P.op("dve", lambda e, g=g, psy=psy: e.tensor_tensor(out=tmp8[g][:].rearrange("p (k q) -> p k q", k=4), in0=psy[:, 256:512].rearrange("p (k q) -> p k q", k=4),
                                                                    in1=ed[:, 4 * g:4 * g + 4].unsqueeze(2).broadcast_to([128, 4, 64]), op=ALU.mult),
                     reads=[pby, ed_b], writes=[tmp8_b[g]])
                P.op("dve", lambda e, g=g, psy=psy: e.tensor_tensor(out=y_sb[:, g * 256:(g + 1) * 256], in0=tmp8[g][:], in1=psy[:, 0:256], op=ALU.add),
                     reads=[pby, tmp8_b[g]], writes=[y_b])

        batch(0)
        batch(4)
        if dr == 0:
            P.dma("pool", c.yf[t0:t0 + 128, :], y_sb[:], reads=[y_b])
        else:
            P.op("dve", lambda e: e.tensor_tensor(out=y_sb[:], in0=y_sb[:], in1=yfin[b][:], op=ALU.add), reads=[y_b, ib], writes=[y_b])
            P.op("dve", lambda e: e.tensor_tensor(out=sq[:].rearrange("p (h q) -> p h q", h=32), in0=xtm[b][:].rearrange("p (h q) -> p h q", h=32),
                                                  in1=dsk[:].unsqueeze(2).broadcast_to([128, 32, 64]), op=ALU.mult), reads=[ib, cb_], writes=[sq_b])
            P.op("dve", lambda e: e.tensor_tensor(out=y_sb[:], in0=y_sb[:], in1=sq[:], op=ALU.add), reads=[y_b, sq_b], writes=[y_b])
            P.op("act", lambda e: e.activation(out=sgz[:], in_=ztm[b][:], func=AF.Silu), reads=[ib], writes=[sgz_b])
            P.op("dve", lambda e: e.tensor_tensor(out=y_sb[:], in0=y_sb[:], in1=sgz[:], op=ALU.mult), reads=[y_b, sgz_b], writes=[y_b])
            P.op("act", lambda e: e.activation(out=sq[:], in_=y_sb[:], func=AF.Square), reads=[y_b, sq_b], writes=[sq_b])
            P.op("dve", lambda e: e.reduce_sum(out=ssq[:], in_=sq[:].rearrange("p (g v) -> p g v", g=8), axis=AX.X), reads=[sq_b], writes=[ssq_b])
            P.op("dve", lambda e: e.tensor_scalar(out=ssq[:], in0=ssq[:], scalar1=1.0 / 256.0, scalar2=EPS, op0=ALU.mult, op1=ALU.add), reads=[ssq_b], writes=[ssq_b])
            P.op("act", lambda e: e.activation(out=ssq[:], in_=ssq[:], func=AF.Sqrt), reads=[ssq_b], writes=[ssq_b])
            P.op("dve", lambda e: e.reciprocal(out=ssq[:], in_=ssq[:]), reads=[ssq_b], writes=[ssq_b])
            P.op("dve", lambda e: e.tensor_tensor(out=y_sb[:].rearrange("p (g v) -> p g v", g=8), in0=y_sb[:].rearrange("p (g v) -> p g v", g=8),
                                                  in1=ssq[:].unsqueeze(2).broadcast_to([128, 8, 256]), op=ALU.mult), reads=[y_b, ssq_b], writes=[y_b])
            P.op("dve", lambda e: e.tensor_tensor(out=y_sb[:], in0=y_sb[:], in1=normw[:], op=ALU.mult), reads=[y_b, cb_], writes=[y_b])
            slot = ti % 4
            for g4 in range(4):
                pst, pbt = psn()
                P.op("pe", [lambda e, pst=pst, g4=g4, k=k: e.transpose(pst[:, k * 128:(k + 1) * 128], y_sb[:, (g4 * 4 + k) * 128:(g4 * 4 + k + 1) * 128], c.ident_f[:])
                            for k in range(4)], reads=[y_b, c.const_b], writes=[pbt])
                P.op("act", lambda e, pst=pst, g4=g4: e.copy(out=ytT[:, g4 * 4:(g4 + 1) * 4, slot * 128:(slot + 1) * 128],
                                                            in_=pst[:].rearrange("p (k t) -> p k t", k=4)), reads=[pbt], writes=[yt_b])
            if slot == 0:
                tb0 = (ti // 4) * 512
                P.dma("act", c.yT[0:2048, tb0:tb0 + 512].rearrange("(k p) t -> p k t", p=128), ytT[:], reads=[yt_b])

    for dr in range(2):
        P.op("dve", lambda e: e.memset(S32[:].rearrange("p g q -> p (g q)"), 0.0), writes=[S32_all])
        P.op("dve", lambda e: e.memset(Sbf2[0][:].rearrange("p g q -> p (g q)"), 0.0), writes=[Sbf2_b[0]])
        order = list(range(NT)) if dr == 0 else list(range(NT - 1, -1, -1))
        for it, ti in enumerate(order):
            run_tile(dr, it, ti)
        P.barrier()
    P.barrier()


def build_program(cfg):
    nc = bass.Bass("TRN2", target_bir_lowering=False)
    es = contextlib.ExitStack()
    c = Ctx()
    c.nc = nc
    P = c.P = Prog(nc, es)
    c.nlayers = cfg["nlayers"]
    c.last_layer = cfg.get("last_layer", c.nlayers - 1)
    c.skip = cfg.get("skip", ())

    def din(name, shape, dt=F32):
        return nc.dram_tensor(name, list(shape), dt, kind="ExternalInput").ap()

    def dscr(name, shape, dt):
        return nc.dram_tensor(name, list(shape), dt, kind=("ExternalOutput" if cfg.get("debug") else "Internal")).ap()

    c.x_in = din("x", [L, D])
    c.p_in = din("p", [DEPTH, L, PLE])
    c.gam_in = din("gam", [128, cfg["ngam"]])
    c.cst_in = din("cst", [128, 1792])
    c.ple_gate = din("ple_gate", [DEPTH, D, D])
    c.ple_proj = din("ple_proj", [DEPTH, PLE, D])
    c.ffn_w1 = din("ffn_w1", [DEPTH, D, DFF])
    c.ffn_w3 = din("ffn_w3", [DEPTH, D, DFF])
    c.ffn_w2 = din("ffn_w2", [DEPTH, DFF, D])
    c.test_wo = din("test_wo", [D, D])
    c.hy_in_w = din("hy_in_w", [1, D, 3072])
    c.hy_out_w = din("hy_out_w", [1, D, D])
    c.hy_f_w1 = din("hy_f_w1", [1, 33, 64])
    c.hy_f_w2 = din("hy_f_w2", [1, 64, 64])
    c.hy_f_w3 = din("hy_f_w3", [1, 64, 2048])
    c.hy_f_b1 = din("hy_f_b1", [1, 64])
    c.hy_f_b2 = din("hy_f_b2", [1, 64])
    c.hy_sin_freq = din("hy_sin_freq", [1, 64])
    c.hy_z = din("hy_z", [33, L])
    c.hy_decay = din("hy_decay", [D, L])
    c.uT = dscr("uT", [3072, L], BF16)
    c.ssd_in_w = din("ssd_in_w", [2, D, 6208])
    c.ssd_dt_bias = din("ssd_dt_bias", [2, 2, 32])
    c.ssd_a_log = din("ssd_a_log", [2, 2, 32])
    c.ssd_d = din("ssd_d", [2, 32])
    c.ssd_norm = din("ssd_norm", [2, 2048])
    c.ssd_out_w = din("ssd_out_w", [2, 2048, D])
    c.xbcT = dscr("xbcT", [4096, L], BF16)
    c.dtT = dscr("dtT", [64, L], F32)
    c.z_tm = dscr("z_tm", [L, 2048], BF16)
    c.x_tm = dscr("x_tm", [L, 2048], BF16)
    c.B_tm = dscr("B_tm", [L, 1024], BF16)
    c.bcT = dscr("bcT", [2048, L], BF16)
    c.yf = dscr("yf", [L, 2048], F32)
    c.gcol_ssd = 30
    c.gla_in_w = din("gla_in_w", [1, D, 3104])
    c.gla_gk_w = din("gla_gk_w", [1, 2, 16, 512])
    c.gla_gk_b = din("gla_gk_b", [1, 2, 512])
    c.gla_norm = din("gla_norm", [1, 256])
    c.gla_out_w = din("gla_out_w", [1, D, D])
    c.qkT = dscr("qkT", [1024, L], BF16)
    c.glT = dscr("glT", [32, L], BF16)
    c.gtm = dscr("gtm", [L, 2560], BF16)
    c.ofwd = dscr("ofwd", [L, 1024], F32)
    c.dbg = dscr("dbg", [128, 8192], F32)
    c.debug = bool(cfg.get("debug"))
    c.hy_chunks = cfg.get("hy_chunks", 8)
    c.hy_ftab = din("hy_ftab", [4, 16, 128, 2048], BF16)
    c.hy_itab = din("hy_itab", [4, 4, 128, 8192], BF16)
    c.hy_tm = dscr("hy_tm", [3, 2, 2048, 1024], BF16)
    c.hy_x0T = dscr("hy_x0T", [D, L], BF16)
    c.hy_vvT = dscr("hy_vvT", [D, L], BF16)
    c.hy_hF = dscr("hy_hF", [16, 4, 128, 1024], F32)
    c.gcol_hy = 13
    c.out = nc.dram_tensor("out", [L, D], F32, kind="ExternalOutput").ap()
    c.hT = dscr("hT", [D, L], F32)
    c.hnT = dscr("hnT", [D, L], BF16)
    c.yT = dscr("yT", [2048, L], BF16)
    c.hT_b = [Buf() for _ in range(NTT)]
    c.hnT_b = [Buf() for _ in range(NTT)]
    c.yT_b = [Buf() for _ in range(NTT)]
    c.out_b = Buf()

    c.gcol_mix, c.gcol_ffn, c.gcol_ple, c.gcol_fin = 0, 4, 8, 12

    ARENA = 188 * 1024
    c.arena = Arena(nc, es, ARENA)
    A = c.arena
    c.gam = A.alloc([cfg["ngam"]], F32)
    cst = A.alloc([1792], F32)
    c.cst = cst
    c.ident_f = cst[:, 0:128]
    c.ones_mean = A.alloc([128], BF16)
    c.ident_b = A.alloc([128], BF16)
    c.ostage = [A.alloc([512], F32) for _ in range(3)]
    c.ostage_b = [Buf() for _ in range(3)]
    c.ost_i = 0
    c.const_b = Buf()
    c.arena_base = A.off
    c.ps = [es.enter_context(nc.psum_tensor("ps%d" % i, [128, 512], F32)) for i in range(8)]
    c.ps_b = [Buf() for _ in range(8)]
    c.ps_i = 0

    P.dma("sp", c.gam[:], c.gam_in[:, :], writes=[c.const_b])
    P.dma("sp", cst[:], c.cst_in[:, :], writes=[c.const_b])
    P.op("dve", lambda e: e.tensor_copy(out=c.ones_mean[:], in_=cst[:, 128:256]), reads=[c.const_b], writes=[c.const_b])
    P.op("dve", lambda e: e.tensor_copy(out=c.ident_b[:], in_=cst[:, 0:128]), reads=[c.const_b], writes=[c.const_b])

    token_stage(c, -1, None, 0, None, None, first=True)
    kinds = cfg.get("kinds", [0, 1, 2, 0])
    for li in range(cfg["nlayers"]):
        kind = kinds[li]
        j = li // 3
        if kind == 1:
            proj_stage(c, c.hy_in_w[j], 3072, c.uT, bias_col=c.gcol_hy)
            hyena_core(c, j)
            token_stage(c, li, c.yT, 8, c.hy_out_w[j], c.gcol_hy + 16)
        elif kind == 0:
            w = c.ssd_in_w[j]
            proj_stage(c, w[:, 2048:6144], 4096, c.xbcT)
            proj_stage(c, w[:, 6144:6208], 64, c.dtT, odt=F32)
            proj_stage_tm(c, w[:, 0:2048], 2048, c.z_tm)
            ssd_conv_stage(c, j)
            ssd_core(c, j)
            token_stage(c, li, c.yT, 16, c.ssd_out_w[j], None)
        elif kind == 2:
            w = c.gla_in_w[j]
            proj_stage(c, w[:, 0:1024], 1024, c.qkT)
            proj_stage(c, w[:, 3072:3104], 32, c.glT)
            proj_stage_tm(c, w[:, 512:3072], 2560, c.gtm)
            gla_core(c, j)
            token_stage(c, li, c.yT, 8, c.gla_out_w[j], None)
        else:
            A.reset(c.arena_base)
            zt = A.alloc([TT], BF16)
            zb = Buf()
            P.op("pool", lambda e, zt=zt: e.memset(zt[:], 0.0), writes=[zb])
            for tt in range(NTT):
                for r in range(8):
                    P.dma("sp", c.yT[r * 128:(r + 1) * 128, tt * TT:(tt + 1) * TT], zt[:], reads=[zb])
            P.barrier()
            token_stage(c, li, c.yT, 8, c.test_wo, None)
    P.barrier()
    P.emit()
    es.close()
    return nc


def pack_cols(vecs):
    a = np.stack([np.asarray(v, np.float32).reshape(8, 128) for v in vecs], 0)
    return np.ascontiguousarray(a.transpose(2, 0, 1).reshape(128, -1))


def make_consts():
    cst = np.zeros((128, 1792), np.float32)
    cst[:, 0:128] = np.eye(128, dtype=np.float32)
    cst[:, 128:256] = 1.0 / D
    i = np.arange(128)[:, None]
    k = np.arange(128)[None, :]
    same = (i // 64) == (k // 64)
    cst[:, 384:512] = np.where(same & (i <= k), -1.0 / 16.0, 0.0)
    cst[:, 512:640] = np.where(same & (i >= k), -1.0 / 16.0, 0.0)
    cst[:, 640:768] = np.where(same & (i > k), -1.0 / 16.0, 0.0)
    cst[:, 768:896] = np.where(same & (i < k), -1.0 / 16.0, 0.0)
    cst[:, 896:1024] = np.where(same & (i <= k), 1.0, 0.0)
    cst[:, 1024:1152] = np.where(same & (i >= k), 1.0, 0.0)
    cst[:, 1152:1280] = np.where(i <= k, 1.0, 0.0)
    cst[:, 1280:1408] = np.where(i >= k, 1.0, 0.0)
    cst[:, 1408:1536] = np.where(i > k, 1.0, 0.0)
    cst[:, 1536:1664] = np.where(i < k, 1.0, 0.0)
    cst[:, 1664:1792] = 1.0
    return cst


def hyena_consts():
    f32 = np.float32
    t = np.linspace(0.0, 1.0, L, dtype=f32)[:, None]
    w = (2.0 * math.pi * np.arange(L, dtype=f32)[:, None] / L).astype(f32)
    bands = np.linspace(1e-4, 15, 16, dtype=f32)[None]
    z = np.concatenate([t, np.cos(bands * w), -np.sin(bands * w)], axis=-1).astype(f32)
    min_decay = math.log(1e-2) / 1.5
    max_decay = math.log(1e-2) / 0.3
    deltas = np.abs(np.linspace(min_decay, max_decay, D, dtype=f32))
    decay = np.exp(-t * deltas[None]).astype(f32)
    return np.ascontiguousarray(z.T), np.ascontiguousarray(decay.T)


def hyena_dft_tables():
    N = 2 * L
    k = (np.arange(2048, dtype=np.float64) + 0.5)[None, :]
    ftab = np.zeros((4, 16, 128, 2048), np.float32)
    itab = np.zeros((4, 4, 128, 8192), np.float32)
    for par in range(2):
        t = (2.0 * np.arange(2048, dtype=np.float64) + par)[:, None]
        th = 2.0 * np.pi * k * t / N
        for q, M in enumerate((np.cos(th), np.sin(th))):
            ftab[2 * q + par] = M.reshape(16, 128, 16, 128).transpose(2, 1, 0, 3).reshape(16, 128, 2048)
            sgn = 1.0 if q == 0 else -1.0
            MT = (sgn * 2.0 / N) * M.T
            itab[2 * par + q] = MT.reshape(16, 128, 4, 512).transpose(2, 1, 0, 3).reshape(4, 128, 8192)
    return ftab.astype(ml_dtypes.bfloat16), itab.astype(ml_dtypes.bfloat16)


def all_gam(inp):
    vecs = [inp["norm_mix"][l] for l in range(4)] + [inp["norm_ffn"][l] for l in range(4)] + [inp["norm_ple"][l] for l in range(4)] + [inp["final_norm"]]
    vecs += [inp["hy_in_b"][0][g * 1024:(g + 1) * 1024] for g in range(3)]
    vecs += [inp["hy_conv_w"][0][tap][g * 1024:(g + 1) * 1024] for tap in range(3) for g in range(3)]
    vecs += [inp["hy_conv_b"][0][g * 1024:(g + 1) * 1024] for g in range(3)]
    vecs += [inp["hy_skip"][0], inp["hy_out_b"][0]]
    for jj in range(2):
        vecs += [inp["ssd_conv_w"][jj][tap][g * 1024:(g + 1) * 1024] for tap in range(5) for g in range(4)]
        vecs += [inp["ssd_conv_b"][jj][g * 1024:(g + 1) * 1024] for g in range(4)]
    return pack_cols(vecs)


_W_KEYS = ("ssd_in_w", "ssd_dt_bias", "ssd_a_log", "ssd_d", "ssd_norm", "ssd_out_w", "gla_in_w", "gla_gk_w", "gla_gk_b", "gla_norm", "gla_out_w", "ple_gate", "ple_proj", "ffn_w1", "ffn_w3", "ffn_w2", "hy_in_w", "hy_out_w", "hy_f_w1", "hy_f_w2", "hy_f_w3",
           "hy_f_b1", "hy_f_b2", "hy_sin_freq")


def kernel(**inputs):
    inp = {k: np.asarray(v) for k, v in inputs.items()}
    gam = all_gam(inp)
    cfg = {"ngam": gam.shape[1], "nlayers": DEPTH}
    nc = build_program(cfg)
    hz, hd = hyena_consts()
    ftab, itab = hyena_dft_tables()
    cst = make_consts()
    zero_wo = np.zeros((D, D), np.float32)
    in_maps = []
    for b in range(8):
        d = {"x": np.ascontiguousarray(inp["x"][b], dtype=np.float32), "p": np.ascontiguousarray(inp["p"][:, b], dtype=np.float32),
             "gam": gam, "cst": cst, "test_wo": zero_wo, "hy_z": hz, "hy_decay": hd, "hy_ftab": ftab, "hy_itab": itab}
        for k in _W_KEYS:
            d[k] = np.ascontiguousarray(inp[k], dtype=np.float32)
        in_maps.append(d)
    res = run_bass_kernel_spmd(nc, in_maps, core_ids=list(range(8)))
    return np.stack([np.asarray(r["out"], dtype=np.float32) for r in res.results], 0)
```
